# Optimizing a Trainium2 kernel written in Bass

```python
import math
import jax, jax.numpy as jnp
from jax import lax
import numpy as np

D_MODEL = 1024
BATCH = 8
SEQ = 2048
DEPTH = 4
DEC_BATCH = 128
DEC_SEQ = 8
PAST_LEN = 16384
PAGE_SIZE = 128

N_MIXERS = 3
N_A = (DEPTH + 2) // N_MIXERS
N_B = (DEPTH + 1) // N_MIXERS
N_C = DEPTH // N_MIXERS
CONV_A_WIDTH = 31
SSM_GROUP = 16
SSM_GROUPS = D_MODEL // SSM_GROUP
SSM_STATE = 64
SCONV_WIDTH = 3
N_MEM = 256
MEM_HEADS = 4
MEM_HEAD_DIM = D_MODEL // MEM_HEADS
D_FF = -(-8 * D_MODEL // (3 * 256)) * 256
RMS_EPS = 1e-6
LN_EPS = 1e-5

kernel_name = 'hybrid_conv_s5_shortconv_memxattn_decode_step'


def rmsnorm(x, g):
    xf = x.astype(jnp.float32)
    y = xf * lax.rsqrt(jnp.mean(xf * xf, axis=-1, keepdims=True) + RMS_EPS) * g.astype(jnp.float32)
    return y.astype(x.dtype)


def causal_dwconv(x, buf, w):
    k = w.shape[0]
    xc = jnp.concatenate([buf.astype(x.dtype), x], axis=1)
    y = lax.conv_general_dilated(xc, w.astype(x.dtype)[:, None, :], window_strides=(1,), padding='VALID',
                                 dimension_numbers=('NWC', 'WIO', 'NWC'), feature_group_count=x.shape[-1])
    return y, xc[:, xc.shape[1] - (k - 1):]


def conformer_conv(h, buf, w_pw1, b_pw1, w_dw, b_dw, ln_g, ln_b, w_pw2, b_pw2):
    z = h @ w_pw1 + b_pw1
    g = z[..., :D_MODEL] * jax.nn.sigmoid(z[..., D_MODEL:])
    c, new_buf = causal_dwconv(g, buf, w_dw)
    cf = (c + b_dw).astype(jnp.float32)
    mu = jnp.mean(cf, axis=-1, keepdims=True)
    var = jnp.mean(jnp.square(cf - mu), axis=-1, keepdims=True)
    cn = ((cf - mu) * lax.rsqrt(var + LN_EPS) * ln_g.astype(jnp.float32) + ln_b.astype(jnp.float32)).astype(h.dtype)
    return jax.nn.silu(cn) @ w_pw2 + b_pw2, new_buf


def _cplx_combine(e1, e2):
    a1r, a1i, b1r, b1i = e1
    a2r, a2i, b2r, b2i = e2
    return (a1r * a2r - a1i * a2i,
            a1r * a2i + a1i * a2r,
            a2r * b1r - a2i * b1i + b2r,
            a2r * b1i + a2i * b1r + b2i)


def s5_mixer(h, s0_re, s0_im, lam_re, lam_im, log_dt, b_re, b_im, c_re, c_im, d_skip, w_glu, b_glu):
    f32 = jnp.float32
    bsz, t, _ = h.shape
    u = h.astype(f32)
    ug = u.reshape(bsz, t, SSM_GROUPS, SSM_GROUP)
    lr, li = lam_re.astype(f32), lam_im.astype(f32)
    dt = jnp.exp(log_dt.astype(f32))[:, None]
    mag = jnp.exp(lr * dt)
    abr, abi = mag * jnp.cos(li * dt), mag * jnp.sin(li * dt)
    den = lr * lr + li * li
    fr = ((abr - 1.0) * lr + abi * li) / den
    fi = (abi * lr - (abr - 1.0) * li) / den
    br, bi = b_re.astype(f32), b_im.astype(f32)
    bbr = fr[..., None] * br - fi[..., None] * bi
    bbi = fr[..., None] * bi + fi[..., None] * br
    xr = jnp.einsum('btgh,gph->tbgp', ug, bbr)
    xi = jnp.einsum('btgh,gph->tbgp', ug, bbi)
    s0r, s0i = s0_re.astype(f32), s0_im.astype(f32)
    xr = xr.at[0].add(abr * s0r - abi * s0i)
    xi = xi.at[0].add(abr * s0i + abi * s0r)
    ar = jnp.broadcast_to(abr, (t, 1, SSM_GROUPS, SSM_STATE))
    ai = jnp.broadcast_to(abi, (t, 1, SSM_GROUPS, SSM_STATE))
    _, _, sr, si = lax.associative_scan(_cplx_combine, (ar, ai, xr, xi), axis=0)
    y = (jnp.einsum('tbgp,ghp->btgh', sr, c_re.astype(f32))
         - jnp.einsum('tbgp,ghp->btgh', si, c_im.astype(f32)))
    y = y.reshape(bsz, t, D_MODEL) + d_skip.astype(f32) * u
    y = jax.nn.gelu(y).astype(h.dtype)
    z = y @ w_glu + b_glu
    out = z[..., :D_MODEL] * jax.nn.sigmoid(z[..., D_MODEL:])
    return out, sr[-1].astype(s0_re.dtype), si[-1].astype(s0_im.dtype)


def short_gated_conv(h, buf, w_in, w_conv, w_out):
    z = h @ w_in
    gb, gc, v = z[..., :D_MODEL], z[..., D_MODEL:2 * D_MODEL], z[..., 2 * D_MODEL:]
    c, new_buf = causal_dwconv(gc * v, buf, w_conv)
    return (gb * c) @ w_out, new_buf


def memory_attention(h, mk, mv, w_q, w_o):
    bsz, t, _ = h.shape
    q = (h @ w_q).reshape(bsz, t, MEM_HEADS, MEM_HEAD_DIM)
    s = jnp.einsum('bthd,bmhd->bhtm', q, mk).astype(jnp.float32) * (MEM_HEAD_DIM ** -0.5)
    p = jax.nn.softmax(s, axis=-1).astype(mv.dtype)
    o = jnp.einsum('bhtm,bmhd->bthd', p, mv).reshape(bsz, t, D_MODEL)
    return o @ w_o


def swiglu(h, w_gu, w_down):
    z = h @ w_gu
    return (jax.nn.silu(z[..., :D_FF]) * z[..., D_FF:]) @ w_down


def trunk(x, mem_k, mem_v, st_conv, st_re, st_im, st_sconv, w):
    conv_out, re_out, im_out, sconv_out = [], [], [], []
    for i in range(DEPTH):
        kind, j = i % N_MIXERS, i // N_MIXERS
        g = w['norm_g'][i]
        h = rmsnorm(x, g[0])
        if kind == 0:
            m, nb = conformer_conv(h, st_conv[j], w['a_w_pw1'][j], w['a_b_pw1'][j], w['a_w_dw'][j], w['a_b_dw'][j],
                                   w['a_ln_g'][j], w['a_ln_b'][j], w['a_w_pw2'][j], w['a_b_pw2'][j])
            conv_out.append(nb)
        elif kind == 1:
            m, sr, si = s5_mixer(h, st_re[j], st_im[j], w['b_lam_re'][j], w['b_lam_im'][j], w['b_log_dt'][j],
                                 w['b_B_re'][j], w['b_B_im'][j], w['b_C_re'][j], w['b_C_im'][j], w['b_D'][j],
                                 w['b_w_glu'][j], w['b_b_glu'][j])
            re_out.append(sr)
            im_out.append(si)
        else:
            m, nb = short_gated_conv(h, st_sconv[j], w['c_w_in'][j], w['c_w_conv'][j], w['c_w_out'][j])
            sconv_out.append(nb)
        x = x + rmsnorm(m, g[1])
        h = rmsnorm(x, g[2])
        x = x + rmsnorm(memory_attention(h, mem_k[i], mem_v[i], w['x_w_q'][i], w['x_w_o'][i]), g[3])
        h = rmsnorm(x, g[4])
        x = x + rmsnorm(swiglu(h, w['f_w_gu'][i], w['f_w_down'][i]), g[5])
    return x, jnp.stack(conv_out), jnp.stack(re_out), jnp.stack(im_out), jnp.stack(sconv_out)


def setup_inputs(seed: int = 0) -> dict:
    key = jax.random.key(seed)
    ks = iter(jax.random.split(key, 48))
    f32 = jnp.float32

    def nrm(shape, scale):
        return jax.random.normal(next(ks), shape, f32) * scale

    D = D_MODEL
    n_idx = jnp.arange(SSM_STATE, dtype=f32)
    return {
        'x_prompt': nrm((BATCH, SEQ, D), 1.0),
        'x_sample': nrm((DEC_BATCH, DEC_SEQ, D), 1.0),
        'state_conv_a': nrm((N_A, DEC_BATCH, CONV_A_WIDTH - 1, D), 1.0),
        'state_ssm_re': nrm((N_B, DEC_BATCH, SSM_GROUPS, SSM_STATE), 0.5),
        'state_ssm_im': nrm((N_B, DEC_BATCH, SSM_GROUPS, SSM_STATE), 0.5),
        'state_sconv': nrm((N_C, DEC_BATCH, SCONV_WIDTH - 1, D), 1.0),
        'cache_mem_k': nrm((DEPTH, DEC_BATCH, N_MEM, MEM_HEADS, MEM_HEAD_DIM), 1.0),
        'cache_mem_v': nrm((DEPTH, DEC_BATCH, N_MEM, MEM_HEADS, MEM_HEAD_DIM), 1.0),
        'mem_prompt': nrm((BATCH, N_MEM, D), 1.0),
        'norm_g': 1.0 + nrm((DEPTH, 6, D), 0.02),
        'a_w_pw1': nrm((N_A, D, 2 * D), D ** -0.5),
        'a_b_pw1': nrm((N_A, 2 * D), 0.01),
        'a_w_dw': nrm((N_A, CONV_A_WIDTH, D), CONV_A_WIDTH ** -0.5),
        'a_b_dw': nrm((N_A, D), 0.01),
        'a_ln_g': 1.0 + nrm((N_A, D), 0.02),
        'a_ln_b': nrm((N_A, D), 0.01),
        'a_w_pw2': nrm((N_A, D, D), D ** -0.5),
        'a_b_pw2': nrm((N_A, D), 0.01),
        'b_lam_re': -0.5 + nrm((N_B, SSM_GROUPS, SSM_STATE), 0.01),
        'b_lam_im': math.pi * n_idx + nrm((N_B, SSM_GROUPS, SSM_STATE), 0.01),
        'b_log_dt': jax.random.uniform(next(ks), (N_B, SSM_GROUPS), f32, math.log(1e-3), math.log(1e-1)),
        'b_B_re': nrm((N_B, SSM_GROUPS, SSM_STATE, SSM_GROUP), (2 * SSM_GROUP) ** -0.5),
        'b_B_im': nrm((N_B, SSM_GROUPS, SSM_STATE, SSM_GROUP), (2 * SSM_GROUP) ** -0.5),
        'b_C_re': nrm((N_B, SSM_GROUPS, SSM_GROUP, SSM_STATE), (2 * SSM_STATE) ** -0.5),
        'b_C_im': nrm((N_B, SSM_GROUPS, SSM_GROUP, SSM_STATE), (2 * SSM_STATE) ** -0.5),
        'b_D': nrm((N_B, D), 1.0),
        'b_w_glu': nrm((N_B, D, 2 * D), D ** -0.5),
        'b_b_glu': nrm((N_B, 2 * D), 0.01),
        'c_w_in': nrm((N_C, D, 3 * D), D ** -0.5),
        'c_w_conv': nrm((N_C, SCONV_WIDTH, D), SCONV_WIDTH ** -0.5),
        'c_w_out': nrm((N_C, D, D), D ** -0.5),
        'x_w_q': nrm((DEPTH, D, D), D ** -0.5),
        'x_w_k': nrm((DEPTH, D, D), D ** -0.5),
        'x_w_v': nrm((DEPTH, D, D), D ** -0.5),
        'x_w_o': nrm((DEPTH, D, D), D ** -0.5),
        'f_w_gu': nrm((DEPTH, D, 2 * D_FF), D ** -0.5),
        'f_w_down': nrm((DEPTH, D_FF, D), D_FF ** -0.5),
    }


def reference(x_prompt, x_sample, state_conv_a, state_ssm_re, state_ssm_im, state_sconv, cache_mem_k, cache_mem_v,
              mem_prompt, norm_g, a_w_pw1, a_b_pw1, a_w_dw, a_b_dw, a_ln_g, a_ln_b, a_w_pw2, a_b_pw2,
              b_lam_re, b_lam_im, b_log_dt, b_B_re, b_B_im, b_C_re, b_C_im, b_D, b_w_glu, b_b_glu,
              c_w_in, c_w_conv, c_w_out, x_w_q, x_w_k, x_w_v, x_w_o, f_w_gu, f_w_down):
    w = {'norm_g': norm_g,
         'a_w_pw1': a_w_pw1, 'a_b_pw1': a_b_pw1, 'a_w_dw': a_w_dw, 'a_b_dw': a_b_dw,
         'a_ln_g': a_ln_g, 'a_ln_b': a_ln_b, 'a_w_pw2': a_w_pw2, 'a_b_pw2': a_b_pw2,
         'b_lam_re': b_lam_re, 'b_lam_im': b_lam_im, 'b_log_dt': b_log_dt, 'b_B_re': b_B_re, 'b_B_im': b_B_im,
         'b_C_re': b_C_re, 'b_C_im': b_C_im, 'b_D': b_D, 'b_w_glu': b_w_glu, 'b_b_glu': b_b_glu,
         'c_w_in': c_w_in, 'c_w_conv': c_w_conv, 'c_w_out': c_w_out,
         'x_w_q': x_w_q, 'x_w_o': x_w_o, 'f_w_gu': f_w_gu, 'f_w_down': f_w_down}

    bp = x_prompt.shape[0]
    mem_k_p = jnp.einsum('bmd,ldf->lbmf', mem_prompt, x_w_k).reshape(DEPTH, bp, N_MEM, MEM_HEADS, MEM_HEAD_DIM)
    mem_v_p = jnp.einsum('bmd,ldf->lbmf', mem_prompt, x_w_v).reshape(DEPTH, bp, N_MEM, MEM_HEADS, MEM_HEAD_DIM)
    z_conv = jnp.zeros((N_A, bp, CONV_A_WIDTH - 1, D_MODEL), x_prompt.dtype)
    z_re = jnp.zeros((N_B, bp, SSM_GROUPS, SSM_STATE), state_ssm_re.dtype)
    z_im = jnp.zeros((N_B, bp, SSM_GROUPS, SSM_STATE), state_ssm_im.dtype)
    z_sconv = jnp.zeros((N_C, bp, SCONV_WIDTH - 1, D_MODEL), x_prompt.dtype)
    y_prompt, conv_p, re_p, im_p, sconv_p = trunk(x_prompt, mem_k_p, mem_v_p, z_conv, z_re, z_im, z_sconv, w)

    y_sample, conv_s, re_s, im_s, sconv_s = trunk(x_sample, cache_mem_k, cache_mem_v, state_conv_a,
                                                  state_ssm_re, state_ssm_im, state_sconv, w)

    return (y_prompt, y_sample, mem_k_p, mem_v_p, conv_p, conv_s, re_p, im_p, re_s, im_s, sconv_p, sconv_s)
```

```python
import math
from contextlib import ExitStack

import numpy as np
import concourse.bass as bass
import concourse.mybir as mybir
from concourse.bass_utils import run_bass_kernel_spmd

F32 = mybir.dt.float32
BF16 = mybir.dt.bfloat16
I32 = mybir.dt.int32
ALU = mybir.AluOpType
AF = mybir.ActivationFunctionType

D = 1024
NCH = 8
DEPTH = 4
DFF = 2816
NMEM = 256
GW = 1152
RMS_EPS = 1e-6
LN_EPS = 1e-5
TWO_PI = 2.0 * math.pi

ENGS = ("pe", "act", "dve", "pool", "sp")


class Res:
    __slots__ = ("name", "w", "r")

    def __init__(self, name):
        self.name = name
        self.w = None
        self.r = {}


class DSem:
    def __init__(self, sem):
        self.sem = sem
        self.count = 0


class _Rec:
    def __init__(self):
        self.call = None

    def __getattr__(self, name):
        def f(*a, **k):
            self.call = (name, a, k)
            return self
        return f


class Sched:
    def __init__(self, nc, es):
        self.nc = nc
        self.es = es
        self.ops = {e: [] for e in ENGS}
        self.esem = {e: es.enter_context(nc.semaphore("sem_" + e)) for e in ("pe", "act", "dve", "pool")}
        self.dsems = []
        self.strict = False
        self.clock = 0
        self.pending = []
        self.pend_clock = 0
        self.deferring = False

    def dsem(self, name):
        d = DSem(self.es.enter_context(self.nc.semaphore(name)))
        self.dsems.append(d)
        return d

    def op(self, eng, fn, reads=(), writes=(), dma=None, par=False):
        deps = []
        for r in reads:
            if r.w is not None:
                deps.append(("raw", r.w))
        for w in writes:
            if w.w is not None:
                if not (par and dma is not None and w.w[0] == "d" and w.w[1] is dma):
                    deps.append(("waw", w.w))
            for t in w.r.values():
                deps.append(("war", t))
        self.clock += 1
        rc = _Rec()
        fn(rc)
        rec = dict(call=rc.call, deps=None, inc=False, dma=dma, clock=self.clock, eng=eng)
        if dma is None:
            tok = ("e", eng, rec)
        else:
            dma.count += 16
            tok = ("d", dma, dma.count)
        keep = []
        for kind, t in deps:
            if t[0] == "e" and t[1] == eng:
                if eng == "pe" or (kind != "raw" and not (self.strict or eng in ("pool", "act"))):
                    continue
            keep.append(t)
            if t[0] == "e":
                t[2]["inc"] = True
        rec["deps"] = keep
        if eng == "pe" and self.deferring:
            self.pending.append(rec)
        else:
            if eng == "pe" and self.pending and any(t[0] == "e" and t[2]["clock"] >= self.pend_clock for t in keep):
                self.pe_flush()
            self.ops[eng].append(rec)
        for r in reads:
            key = eng if dma is None else ("d", id(dma))
            r.r[key] = tok
        for w in writes:
            w.w = tok
            w.r = {}
        return tok

    def pe_defer_begin(self):
        if not self.pending:
            self.pend_clock = self.clock + 1
        self.deferring = True

    def pe_defer_end(self):
        self.deferring = False

    def pe_flush(self):
        self.ops["pe"].extend(self.pending)
        self.pending = []

    def emit(self, block):
        self.pe_flush()
        for e in ("pe", "act", "dve", "pool"):
            cnt = 0
            for rec in self.ops[e]:
                if rec["inc"] and rec["dma"] is None:
                    cnt += 1
                    rec["idx"] = cnt
        final = [(d.sem, d.count) for d in self.dsems if d.count > 0]

        def run(e, eng, tail=False):
            seen = {}
            for rec in self.ops[e]:
                need = {}
                for t in rec["deps"]:
                    if t[0] == "e":
                        sem = self.esem[t[1]]
                        val = t[2]["idx"]
                    else:
                        sem = t[1].sem
                        val = t[2]
                    k = id(sem)
                    if k not in need or need[k][1] < val:
                        need[k] = (sem, val)
                for k, (sem, val) in need.items():
                    if seen.get(k, 0) < val:
                        eng.wait_ge(sem, val)
                        seen[k] = val
                name, a, k = rec["call"]
                ins = getattr(eng, name)(*a, **k)
                if rec["dma"] is not None:
                    ins.then_inc(rec["dma"].sem, 16)
                elif rec["inc"]:
                    ins.then_inc(self.esem[e], 1)
            if tail:
                for sem, val in final:
                    eng.wait_ge(sem, val)

        @block.tensor
        def _(eng):
            run("pe", eng)

        @block.scalar
        def _(eng):
            run("act", eng)

        @block.vector
        def _(eng):
            run("dve", eng)

        @block.gpsimd
        def _(eng):
            run("pool", eng)

        @block.sync
        def _(eng):
            run("sp", eng, tail=True)


def build_program(cfg):
    nc = bass.Bass("TRN2", target_bir_lowering=False)
    es = ExitStack()
    S = Sched(nc, es)
    S.strict = bool(cfg.get("strict"))
    dbg = cfg.get("dbg")
    layers = cfg.get("layers", list(range(DEPTH)))
    groups_sel = cfg.get("groups", [0, 1])

    def din(name, shape):
        return nc.dram_tensor(name, list(shape), F32, kind="ExternalInput").ap()

    def dout(name, shape):
        return nc.dram_tensor(name, list(shape), F32, kind="ExternalOutput").ap()

    xp = din("xp", [2048, D])
    xs = din("xs", [128, D])
    st_conv = din("st_conv", [2, 16, 30, D])
    st_re = din("st_re", [16, 4096])
    st_im = din("st_im", [16, 4096])
    st_sc = din("st_sc", [16, 2, D])
    ck = din("ck", [4, 16, NMEM, D])
    cv = din("cv", [4, 16, NMEM, D])
    memp = din("memp", [NMEM, D])
    vecs = din("vecs", [128, D])
    a_w_pw1 = din("a_w_pw1", [2, D, 2 * D])
    a_w_pw2 = din("a_w_pw2", [2, D, D])
    b_lam_re = din("b_lam_re", [64, 64])
    b_lam_im = din("b_lam_im", [64, 64])
    b_log_dt = din("b_log_dt", [64])
    b_B_re = din("b_B_re", [64, 64, 16])
    b_B_im = din("b_B_im", [64, 64, 16])
    b_C_re = din("b_C_re", [64, 16, 64])
    b_C_im = din("b_C_im", [64, 16, 64])
    b_w_glu = din("b_w_glu", [D, 2 * D])
    c_w_in = din("c_w_in", [D, 3 * D])
    c_w_out = din("c_w_out", [D, D])
    x_w_q = din("x_w_q", [4, D, D])
    x_w_k = din("x_w_k", [4, D, D])
    x_w_v = din("x_w_v", [4, D, D])
    x_w_o = din("x_w_o", [4, D, D])
    f_w_gu = din("f_w_gu", [4, D, 2 * DFF])
    f_w_down = din("f_w_down", [4, DFF, D])

    yp = dout("yp", [2048, D])
    ys = dout("ys", [128, D])
    nk = dout("nk", [4, NMEM, D])
    nv = dout("nv", [4, NMEM, D])
    ncap = dout("ncap", [2, 30, D])
    ncas = dout("ncas", [2, 16, 30, D])
    nrep = dout("nrep", [1, 4096])
    nimp = dout("nimp", [1, 4096])
    nres = dout("nres", [16, 4096])
    nims = dout("nims", [16, 4096])
    nscp = dout("nscp", [2, D])
    nscs = dout("nscs", [16, 2, D])
    if dbg:
        dbg_out = dout("dbg_out", [128, NCH * GW])

    def sb(name, shape, dt):
        return es.enter_context(nc.sbuf_tensor(name, list(shape), dt))

    X = sb("X", [128, NCH, GW], F32)
    H = sb("H", [128, NCH, GW], BF16)
    MW = 2 + GW
    M = sb("M", [128, NCH, MW], F32)
    MF = M[:].rearrange("p c w -> p (c w)")
    BIGN = 13312
    BIG = sb("BIG", [128, BIGN], BF16)
    NSLOT = 3
    WSL = 4096
    WR = [sb(f"WR{i}", [128, WSL], BF16) for i in range(NSLOT)]
    KSB = [sb(f"KS{i}", [128, 2, D], BF16) for i in range(2)]
    KT = sb("KT", [128, NCH, NMEM], BF16)
    NV = cfg.get("nv", 2)
    VV = [sb(f"VV{i}", [128, 2, D], BF16) for i in range(NV)]
    NT = 6
    TT = [sb(f"TT{i}", [128, 512], F32) for i in range(NT)]
    PT = [sb(f"PT{i}", [128, 2, 512], BF16) for i in range(2)]
    CV = sb("CV", [128, NCH, 128], F32)
    IDF = sb("IDF", [128, 128], F32)
    IDB = sb("IDB", [128, 128], BF16)
    ONB = sb("ONB", [128, 128], BF16)
    MEMT = sb("MEMT", [128, NCH, NMEM], BF16)
    GSF = sb("GSF", [128, NCH, 160], F32)
    GPF = sb("GPF", [128, NCH, 32], F32)
    CARRY_G = [sb(f"CARG{i}", [128, NCH, 30], BF16) for i in range(2)]
    CARRY_P = sb("CARP", [128, NCH, 2], F32)
    CARRY_S = sb("CARS", [128, 32, 2], F32)
    LB = sb("LB", [128, 8, 2, 128], BF16)
    LC = sb("LC", [128, 32, 2, 32], BF16)
    S5P = sb("S5P", [128, 12, 32], F32)
    MASK = sb("MASK", [128, 128], F32)
    DUMMY = sb("DUMMY", [128, 4], F32)

    PS = [es.enter_context(nc.psum_tensor(f"ps{i}", [128, 512], F32)) for i in range(8)]
    PSR = [Res(f"ps{i}") for i in range(8)]
    ps_free = list(range(8))
    ps_rr = [0]

    ST_BANKS = [6, 7]
    stb_rr = [0]

    def psum_stat():
        i = ST_BANKS[stb_rr[0] % 2]
        stb_rr[0] += 1
        return PS[i], PSR[i]

    def psum():
        i = ps_free[ps_rr[0] % len(ps_free)]
        ps_rr[0] += 1
        return PS[i], PSR[i]

    def psum_hold():
        i = ps_free[ps_rr[0] % len(ps_free)]
        ps_free.remove(i)
        return i

    def psum_release(i):
        ps_free.append(i)
        ps_free.sort()

    class R:
        pass

    rX = {}
    rH = {}
    rM = {}

    def res(dct, key, name):
        if key not in dct:
            dct[key] = Res(f"{name}{key}")
        return dct[key]

    rMisc = {}

    def rm(name):
        return res(rMisc, name, "")

    wr_res = [Res(f"wr{i}") for i in range(NSLOT)]
    wr_sem = [S.dsem(f"wrs{i}") for i in range(NSLOT)]
    wr_rr = [0]

    def wslot():
        i = wr_rr[0] % NSLOT
        wr_rr[0] += 1
        return WR[i], wr_res[i], wr_sem[i]

    class Tile:
        def __init__(self, kind, col0, n, tok0, gi, ti):
            self.kind, self.col0, self.n, self.tok0, self.gi, self.ti = kind, col0, n, tok0, gi, ti
            self.cols = slice(col0, col0 + n)
            self.mcols = slice(2 + col0, 2 + col0 + n)

    GROUPS = [
        [Tile("P", 0, 512, 0, 0, 0), Tile("P", 512, 512, 512, 0, 1), Tile("S", 1024, 128, 0, 0, 2)],
        [Tile("P", 0, 512, 1024, 1, 0), Tile("P", 512, 512, 1536, 1, 1)],
    ]

    def rXt(t):
        return res(rX, t.ti, "X")

    def rHt(t):
        return res(rH, t.ti, "H")

    def rMt(t):
        return res(rM, t.ti, "M")

    def cvs(row, c):
        return CV[:, c, row:row + 1]

    rCV = rm("CV")
    rCONST = rm("CONST")

    def row_norm(l, i):
        return l * 6 + i
    ROW_BPW1 = 24
    ROW_WDW = 28
    ROW_BDW = 90
    ROW_LNG = 92
    ROW_LNB = 94
    ROW_BPW2 = 96
    ROW_BD = 98
    ROW_BGLU = 99
    ROW_WCONV = 101

    def setup_consts():
        S.op("pool", lambda e: e.memset(IDF[:], 0.0), writes=[rCONST])
        S.op("pool", lambda e: e.affine_select(out=IDF[:], in_=IDF[:], pattern=[[-1, 128]], compare_op=ALU.not_equal,
                                               fill=1.0, base=0, channel_multiplier=1), reads=[rCONST], writes=[rCONST])
        S.op("dve", lambda e: e.tensor_copy(out=IDB[:], in_=IDF[:]), reads=[rCONST], writes=[rm("IDB")])
        S.op("dve", lambda e: e.memset(ONB[:], 1.0), writes=[rm("ONB")])
        stg = M[:, 0:1, 0:D]
        sem = S.dsem("vecs")
        S.op("sp", lambda e: e.dma_start(out=M[:, 0, 0:D], in_=vecs), writes=[rm("Mstage")], dma=sem)
        for c in range(NCH):
            ps, pr = psum()
            S.op("pe", lambda e, c=c, ps=ps: e.transpose(out=ps[:, 0:128], in_=M[:, 0, c * 128:(c + 1) * 128], identity=IDF[:]),
                 reads=[rm("Mstage"), rCONST], writes=[pr])
            S.op("act", lambda e, c=c, ps=ps: e.copy(out=CV[:, c, :], in_=ps[:, 0:128]), reads=[pr], writes=[rCV])

    def load_x(tiles):
        for t in tiles:
            src = xp[t.tok0:t.tok0 + t.n, :] if t.kind == "P" else xs
            nb = t.n // 128
            stg = M[:, 0:4, 0:D]
            sem = S.dsem(f"ldx{t.gi}_{t.ti}")
            S.op("sp", lambda e, src=src, nb=nb: e.dma_start(out=M[:, 0:nb, 0:D], in_=src.rearrange("(b p) d -> p b d", p=128)),
                 writes=[rm("Mstage")], dma=sem)
            for c in range(NCH):
                ps, pr = psum()
                for b in range(nb):
                    S.op("pe", lambda e, c=c, b=b, ps=ps: e.transpose(out=ps[:, b * 128:(b + 1) * 128],
                                                                    in_=M[:, b, c * 128:(c + 1) * 128], identity=IDF[:]),
                         reads=[rm("Mstage"), rCONST], writes=[pr])
                S.op("act" if c % 2 else "dve",
                     (lambda e, c=c, ps=ps, t=t: e.copy(out=X[:, c, t.cols], in_=ps[:, 0:t.n])) if c % 2 else
                     (lambda e, c=c, ps=ps, t=t: e.tensor_copy(out=X[:, c, t.cols], in_=ps[:, 0:t.n])),
                     reads=[pr], writes=[rXt(t)])

    st_sems = [S.dsem("st0"), S.dsem("st1")]
    st_res = [Res("ost0"), Res("ost1")]
    st_rr = [0]

    def store_y(tiles):
        for t in tiles:
            dst = yp[t.tok0:t.tok0 + t.n, :] if t.kind == "P" else ys
            for b in range(t.n // 128):
                i = st_rr[0] % 2
                st_rr[0] += 1
                ost = M[:, i, 0:D]
                for hh in range(2):
                    ps, pr = psum()
                    for cc in range(4):
                        c = hh * 4 + cc
                        S.op("pe", lambda e, c=c, cc=cc, b=b, ps=ps, t=t: e.transpose(
                            out=ps[:, cc * 128:(cc + 1) * 128], in_=X[:, c, t.col0 + b * 128:t.col0 + (b + 1) * 128], identity=IDF[:]),
                            reads=[rXt(t), rCONST], writes=[pr])
                    S.op("act" if hh else "dve",
                         (lambda e, hh=hh, ps=ps, i=i: e.copy(out=M[:, i, hh * 512:(hh + 1) * 512], in_=ps[:, :])) if hh else
                         (lambda e, hh=hh, ps=ps, i=i: e.tensor_copy(out=M[:, i, hh * 512:(hh + 1) * 512], in_=ps[:, :])),
                         reads=[pr], writes=[st_res[i]])
                S.op("sp", lambda e, i=i, dst=dst, b=b: e.dma_start(out=dst[b * 128:(b + 1) * 128, :], in_=M[:, i, 0:D]),
                     reads=[st_res[i]], dma=st_sems[i])

    tt_res = [Res(f"tt{i}") for i in range(NT)]
    tt_sem = [S.dsem(f"tts{i}") for i in range(NT)]
    tt_rr = [0]

    def tmp():
        i = tt_rr[0] % NT
        tt_rr[0] += 1
        return TT[i], tt_res[i], tt_sem[i]

    def rstd_from(src_ap_fn, src_res, n, eps):
        slot, sres, _ = wslot()
        SQv = slot[:, 0:NCH * 512].rearrange("p (c w) -> p c w", c=NCH)
        S.op("act", lambda e: e.activation(out=SQv[:, :, 0:n], in_=src_ap_fn(), func=AF.Square), reads=[src_res], writes=[sres])
        ps, pr = psum()
        for c in range(NCH):
            S.op("pe", lambda e, c=c, ps=ps: e.matmul(ps[:, 0:n], lhsT=ONB[:], rhs=SQv[:, c, 0:n], start=(c == 0), stop=(c == NCH - 1)),
                 reads=[sres, rm("ONB")], writes=[pr])
        rb, rr_, _ = tmp()
        S.op("act", lambda e, ps=ps: e.activation(out=rb[:, 0:n], in_=ps[:, 0:n], func=AF.Sqrt, scale=1.0 / D, bias=EPSB[eps][:, 0:1]),
             reads=[pr, rCONST2], writes=[rr_])
        S.op("dve", lambda e: e.reciprocal(out=rb[:, 0:n], in_=rb[:, 0:n]), reads=[rr_], writes=[rr_])
        return rb, rr_

    EPSB = {RMS_EPS: sb("EPS1", [128, 1], F32), LN_EPS: sb("EPS2", [128, 1], F32)}
    HPI = sb("HPI", [128, 1], F32)
    rCONST2 = rm("CONST2")

    def setup_eps():
        S.op("dve", lambda e: e.memset(EPSB[RMS_EPS][:], RMS_EPS), writes=[rCONST2])
        S.op("dve", lambda e: e.memset(EPSB[LN_EPS][:], LN_EPS), writes=[rCONST2])
        S.op("dve", lambda e: e.memset(HPI[:], math.pi / 2), writes=[rCONST2])

    def pre_norm(t, grow):
        rb, rr_ = rstd_from(lambda: X[:, :, t.cols], rXt(t), t.n, RMS_EPS)
        for c in range(NCH):
            S.op("dve", lambda e, c=c: e.scalar_tensor_tensor(out=H[:, c, t.cols], in0=X[:, c, t.cols], scalar=cvs(grow, c),
                                                             in1=rb[:, 0:t.n], op0=ALU.mult, op1=ALU.mult),
                 reads=[rXt(t), rr_, rCV], writes=[rHt(t)])

    def norm_boundary(tiles, grow_post, grow_next):
        st = {}
        for t in tiles:
            S.op("act", lambda e, t=t: e.activation(out=H[:, :, t.cols], in_=M[:, :, t.mcols], func=AF.Square), reads=[rMt(t)], writes=[rHt(t)])
        for t in tiles:
            ps, pr = psum()
            for c in range(NCH):
                S.op("pe", lambda e, c=c, ps=ps, t=t: e.matmul(ps[:, 0:t.n], lhsT=ONB[:], rhs=H[:, c, t.cols], start=(c == 0), stop=(c == NCH - 1)),
                     reads=[rHt(t), rm("ONB")], writes=[pr])
            st[t.ti] = (ps, pr)
        rsb = {}
        for t in tiles:
            ps, pr = st[t.ti]
            rb, rr_, _ = tmp()
            rsb[t.ti] = (rb, rr_)
            S.op("act", lambda e, ps=ps, rb=rb, t=t: e.activation(out=rb[:, 0:t.n], in_=ps[:, 0:t.n], func=AF.Sqrt, scale=1.0 / D, bias=EPSB[RMS_EPS][:, 0:1]),
                 reads=[pr, rCONST2], writes=[rr_])

        def post_dve(t):
            rb, rr_ = rsb[t.ti]
            S.op("dve", lambda e: e.reciprocal(out=rb[:, 0:t.n], in_=rb[:, 0:t.n]), reads=[rr_], writes=[rr_])
            for c in range(NCH):
                S.op("dve", lambda e, c=c: e.scalar_tensor_tensor(out=M[:, c, t.mcols], in0=M[:, c, t.mcols], scalar=cvs(grow_post, c),
                                                                 in1=rb[:, 0:t.n], op0=ALU.mult, op1=ALU.mult),
                     reads=[rMt(t), rr_, rCV], writes=[rMt(t)])
            S.op("dve", lambda e: e.tensor_tensor(out=X[:, :, t.cols], in0=X[:, :, t.cols], in1=M[:, :, t.mcols], op=ALU.add),
                 reads=[rMt(t), rXt(t)], writes=[rXt(t)])
            if grow_next is None:
                return
            S.op("act", lambda e: e.activation(out=H[:, :, t.cols], in_=X[:, :, t.cols], func=AF.Square), reads=[rXt(t)], writes=[rHt(t)])
            ps, pr = psum()
            for c in range(NCH):
                S.op("pe", lambda e, c=c, ps=ps: e.matmul(ps[:, 0:t.n], lhsT=ONB[:], rhs=H[:, c, t.cols], start=(c == 0), stop=(c == NCH - 1)),
                     reads=[rHt(t), rm("ONB")], writes=[pr])
            rb2, rr2, _ = tmp()
            rsb[("n", t.ti)] = (rb2, rr2)
            S.op("act", lambda e: e.activation(out=rb2[:, 0:t.n], in_=ps[:, 0:t.n], func=AF.Sqrt, scale=1.0 / D, bias=EPSB[RMS_EPS][:, 0:1]),
                 reads=[pr, rCONST2], writes=[rr2])

        def pre_dve(t):
            rb2, rr2 = rsb[("n", t.ti)]
            S.op("dve", lambda e: e.reciprocal(out=rb2[:, 0:t.n], in_=rb2[:, 0:t.n]), reads=[rr2], writes=[rr2])
            for c in range(NCH):
                S.op("dve", lambda e, c=c: e.scalar_tensor_tensor(out=H[:, c, t.cols], in0=X[:, c, t.cols], scalar=cvs(grow_next, c),
                                                                 in1=rb2[:, 0:t.n], op0=ALU.mult, op1=ALU.mult),
                     reads=[rXt(t), rr2, rCV], writes=[rHt(t)])

        prev = None
        for t in tiles:
            post_dve(t)
            if prev is not None and grow_next is not None:
                pre_dve(prev)
            prev = t
        if grow_next is not None:
            pre_dve(prev)

    deferred = []

    def flush_deferred(t=None):
        while deferred and (t is None or deferred[0][0] is t):
            tt_, fn_ = deferred.pop(0)
            fn_(tt_)

    def proj(tiles, wparts, KC, rhs_fn, rhs_res_fn, jobs, epi, nblk, tile_outer=False, after_tile=None, defer_last=False):
        def load(blk):
            parts = wparts(blk)
            wtot = sum(p[2] for p in parts)
            slot, sres, ssem = wslot()
            sv = slot[:, 0:KC * wtot].rearrange("p (k w) -> p k w", k=KC)
            for pi, (src, off, w) in enumerate(parts):
                S.op("pool", lambda e, src=src, off=off, w=w, sv=sv: e.dma_start(
                    out=sv[:, :, off:off + w], in_=src.rearrange("(k p) w -> p k w", p=128)),
                    writes=[sres], dma=ssem, par=(pi > 0))
            return sv, sres

        def run(blk, sv, sres, t):
            for ji, job in enumerate(jobs(blk)):
                banks = [psum() for _ in job]
                for (ps, pr), off in zip(banks, job):
                    for k in range(KC):
                        S.op("pe", lambda e, k=k, off=off, ps=ps, t=t, sv=sv: e.matmul(
                            ps[:, 0:t.n], lhsT=sv[:, k, off:off + 128], rhs=rhs_fn(k, t), start=(k == 0), stop=(k == KC - 1)),
                            reads=[sres] + rhs_res_fn(t), writes=[pr])
                epi(blk, ji, t, [b[0] for b in banks], [b[1] for b in banks])

        if tile_outer:
            assert nblk <= NSLOT - 1
            loaded = [load(blk) for blk in range(nblk)]
            prev = None
            for t in tiles:
                flush_deferred(t)
                for blk in range(nblk):
                    run(blk, loaded[blk][0], loaded[blk][1], t)
                if prev is not None and after_tile is not None:
                    after_tile(prev)
                prev = t
            if after_tile is not None:
                if defer_last:
                    deferred.append((prev, after_tile))
                else:
                    after_tile(prev)
        else:
            for blk in range(nblk):
                sv, sres = load(blk)
                for t in tiles:
                    flush_deferred(t)
                    run(blk, sv, sres, t)
            if after_tile is not None:
                for t in tiles:
                    after_tile(t)

    S5N = 10
    s5_res = [Res(f"s5scr{i}") for i in range(S5N)]

    def all_M_res():
        extra = [rm(x) for x in ("SRB", "SIB", "MSKM", "INI", "NSN")]
        return [rm("Mstage")] + [res(rM, k, "M") for k in range(3)] + s5_res + [rm("Mhist")] + extra

    def join_M():
        S.op("dve", lambda e: e.memset(DUMMY[:, 0:1], 0.0), writes=all_M_res())

    def join_BIG():
        rl = [res(rMisc, ("BIG", k), "BIG") for k in range(3)] + [res(rMisc, ("Q", k, h), "Q") for k in range(3) for h in range(4)]
        rl += [res(rMisc, ("G", k), "G") for k in range(3)] + [rm("GPhist"), rm("GShist"), rm("AS0")]
        rl += [rm("GSF"), rm("PSH"), rm("SOUT")]
        S.op("dve", lambda e: e.memset(DUMMY[:, 2:3], 0.0), writes=rl)

    out_sems = {}

    def osem(name):
        if name not in out_sems:
            out_sems[name] = S.dsem("o_" + name)
        return out_sems[name]

    def act_copy(out, in_, reads, writes, scale=None):
        if scale is None:
            S.op("act", lambda e: e.copy(out=out, in_=in_), reads=reads, writes=writes)
        else:
            S.op("act", lambda e: e.mul(out=out, in_=in_, mul=scale), reads=reads, writes=writes)

    def dve_copy(out, in_, reads, writes):
        S.op("dve", lambda e: e.tensor_copy(out=out, in_=in_), reads=reads, writes=writes)

    cp_rr = [0]

    def any_copy(out, in_, reads, writes):
        act_copy(out, in_, reads, writes)

    def transpose_out(src_fn, nrows, nchunks_src_res, dst_ap, name):
        for hh in range(2):
            ps, pr = psum()
            for cc in range(4):
                c = hh * 4 + cc
                S.op("pe", lambda e, c=c, cc=cc, ps=ps: e.transpose(out=ps[0:nrows, cc * 128:(cc + 1) * 128], in_=src_fn(c),
                                                                  identity=IDF[:]),
                     reads=nchunks_src_res + [rCONST], writes=[pr])
            tb, tr, tsem = tmp()
            any_copy(tb[0:nrows, :], ps[0:nrows, :], [pr], [tr])
            S.op("sp", lambda e, tb=tb, hh=hh: e.dma_start(out=dst_ap[:, hh * 512:(hh + 1) * 512], in_=tb[0:nrows, :]),
                 reads=[tr], dma=tsem)

    def rBt(t):
        return res(rMisc, ("BIG", t.ti), "BIG")

    def ffn(l, gi, tiles, pre_done, nxt, dl=False):
        if not pre_done:
            for t in tiles:
                pre_norm(t, row_norm(l, 4))
        join_BIG()
        AT = BIG[:, 0:11 * GW].rearrange("p (f w) -> p f w", f=11)
        for half in range(2):
            def wparts(blk, half=half):
                f0 = half * 11 + blk * 2
                nf = min(2, 11 - blk * 2)
                w = nf * 128
                return [(f_w_gu[l][:, f0 * 128:f0 * 128 + w], 0, w), (f_w_gu[l][:, DFF + f0 * 128:DFF + f0 * 128 + w], w, w)]

            def jobs(blk):
                nf = min(2, 11 - blk * 2)
                return [[i * 128, nf * 128 + i * 128] for i in range(nf)]

            def epi(blk, ji, t, pss, prs):
                fi = blk * 2 + ji
                tb, tr, _ = tmp()
                S.op("act", lambda e: e.activation(out=tb[:, 0:t.n], in_=pss[0][:, 0:t.n], func=AF.Silu), reads=[prs[0]], writes=[tr])
                S.op("dve", lambda e: e.tensor_tensor(out=AT[:, fi, t.cols], in0=tb[:, 0:t.n], in1=pss[1][:, 0:t.n], op=ALU.mult),
                     reads=[tr, prs[1]], writes=[rBt(t)])
            proj(tiles, wparts, 8, lambda k, t: H[:, k, t.cols], lambda t: [rHt(t)], jobs, epi, 6)

            def wparts2(blk, half=half):
                return [(f_w_down[l][half * 1408:(half + 1) * 1408, blk * 256:(blk + 1) * 256], 0, 256)]

            def epi2(blk, ji, t, pss, prs, half=half):
                c = blk * 2 + ji
                if half == 0:
                    act_copy(M[:, c, t.mcols], pss[0][:, 0:t.n], [prs[0]], [rMt(t)])
                else:
                    S.op("dve", lambda e: e.tensor_tensor(out=M[:, c, t.mcols], in0=M[:, c, t.mcols], in1=pss[0][:, 0:t.n], op=ALU.add),
                         reads=[prs[0], rMt(t)], writes=[rMt(t)])
            proj(tiles, wparts2, 11, lambda k, t: AT[:, k, t.cols], lambda t: [rBt(t)], lambda blk: [[0], [128]], epi2, 4)
        norm_boundary(tiles, row_norm(l, 5), nxt)

    rMEMT = rm("MEMT")
    rNK = [[Res(f"nk{l}_{i}") for i in range(4)] for l in range(DEPTH)]
    rNV = [[Res(f"nv{l}_{i}") for i in range(4)] for l in range(DEPTH)]
    rKSB, rKT = [Res("ks0"), Res("ks1")], rm("KT")
    rVV = [Res(f"vv{i}") for i in range(NV)]
    ks_sems = [S.dsem("ks0"), S.dsem("ks1")]
    ks_rr = [0]
    vv_sem = [S.dsem(f"vvs{i}") for i in range(NV)]
    vv_rr = [0]
    rPT = [Res("pt0"), Res("pt1")]
    pt_rr = [0]

    def setup_memT():
        sem = S.dsem("memp")
        S.op("sp", lambda e: e.dma_start(out=M[:, 0:2, 0:D], in_=memp.rearrange("(b p) d -> p b d", p=128)),
             writes=[rm("Mstage")], dma=sem)
        for c in range(NCH):
            ps, pr = psum()
            for b in range(2):
                S.op("pe", lambda e, c=c, b=b, ps=ps: e.transpose(out=ps[:, b * 128:(b + 1) * 128], in_=M[:, b, c * 128:(c + 1) * 128],
                                                                identity=IDF[:]), reads=[rm("Mstage"), rCONST], writes=[pr])
            any_copy(MEMT[:, c, :], ps[:, 0:NMEM], [pr], [rMEMT])

    def kv_project(l):
        for (W, dst, rdst) in ((x_w_k, nk, rNK), (x_w_v, nv, rNV)):
            for blk in range(2):
                slot, sres, ssem = wslot()
                sv = slot[:, 0:8 * 512].rearrange("p (k w) -> p k w", k=8)
                S.op("pool", lambda e, W=W, blk=blk, sv=sv: e.dma_start(
                    out=sv, in_=W[l][:, blk * 512:(blk + 1) * 512].rearrange("(k p) w -> p k w", p=128)), writes=[sres], dma=ssem)
                for mc in range(2):
                    ps, pr = psum()
                    for k in range(NCH):
                        S.op("pe", lambda e, k=k, mc=mc, ps=ps, sv=sv: e.matmul(ps[:, :], lhsT=MEMT[:, k, mc * 128:(mc + 1) * 128],
                                                                             rhs=sv[:, k, :], start=(k == 0), stop=(k == NCH - 1)),
                             reads=[sres, rMEMT], writes=[pr])
                    tb, tr, tsem = tmp()
                    any_copy(tb[:, :], ps[:, :], [pr], [tr])
                    S.op("sp", lambda e, dst=dst, mc=mc, blk=blk, tb=tb: e.dma_start(
                        out=dst[l][mc * 128:(mc + 1) * 128, blk * 512:(blk + 1) * 512], in_=tb[:, :]),
                        reads=[tr], writes=[rdst[l][mc * 2 + blk]], dma=tsem)

    def load_kv(ksrc, vsrc, src_res):
        vi = vv_rr[0] % NV
        vv_rr[0] += 1
        ki = ks_rr[0] % 2
        ks_rr[0] += 1
        KS, rKS, ks_sem = KSB[ki], rKSB[ki], ks_sems[ki]
        S.op("pool", lambda e: e.dma_start(out=KS[:], in_=ksrc.rearrange("(b p) d -> p b d", p=128)), reads=src_res, writes=[rKS], dma=ks_sem)
        S.op("pool", lambda e: e.dma_start(out=VV[vi][:], in_=vsrc.rearrange("(b p) d -> p b d", p=128)), reads=src_res, writes=[rVV[vi]],
             dma=vv_sem[vi])
        for half in range(2):
            ps, pr = psum()
            psb = ps[:].bitcast(BF16).rearrange("p (f m) -> p f m", f=4)
            for fcc in range(4):
                for mc in range(2):
                    S.op("pe", lambda e, fcc=fcc, mc=mc, psb=psb, half=half: e.transpose(
                        out=psb[:, fcc, mc * 128:(mc + 1) * 128], in_=KS[:, mc, (half * 4 + fcc) * 128:(half * 4 + fcc + 1) * 128],
                        identity=IDB[:]), reads=[rKS, rm("IDB")], writes=[pr])
            any_copy(KT[:, half * 4:(half + 1) * 4, :], psb, [pr], [rKT])
        return vi

    def rQ(t, h):
        return res(rMisc, ("Q", t.ti, h), "Q")

    def attention(l, gi, tiles, pre_done, nxt, dl=False):
        if gi == 0:
            kv_project(l)
        if not pre_done:
            for t in tiles:
                pre_norm(t, row_norm(l, 2))
        join_BIG()
        QT = BIG[:, 0:8 * GW].rearrange("p (c w) -> p c w", c=8)

        def epi(blk, ji, t, pss, prs):
            c = blk * 4 + ji
            act_copy(QT[:, c, t.cols], pss[0][:, 0:t.n], [prs[0]], [rQ(t, c // 2)], scale=0.0625)
        proj(tiles, lambda blk: [(x_w_q[l][:, blk * 512:(blk + 1) * 512], 0, 512)], 8, lambda k, t: H[:, k, t.cols],
             lambda t: [rHt(t)], lambda blk: [[0], [128], [256], [384]], epi, 2)

        ptiles = [t for t in tiles if t.kind == "P"]
        stiles = [t for t in tiles if t.kind == "S"]
        if ptiles:
            vi = load_kv(nk[l], nv[l], rNK[l] + rNV[l])
            for t in ptiles:
                n = t.n
                for h in range(4):
                    pi = pt_rr[0] % 2
                    pt_rr[0] += 1
                    sc = [psum(), psum()]
                    for mc in range(2):
                        for dc in range(2):
                            S.op("pe", lambda e, mc=mc, dc=dc, ps=sc[mc][0]: e.matmul(
                                ps[:, 0:n], lhsT=KT[:, 2 * h + dc, mc * 128:(mc + 1) * 128], rhs=QT[:, 2 * h + dc, t.cols],
                                start=(dc == 0), stop=(dc == 1)), reads=[rKT, rQ(t, h)], writes=[sc[mc][1]])
                        S.op("act", lambda e, mc=mc, ps=sc[mc][0], pi=pi: e.activation(out=PT[pi][:, mc, 0:n], in_=ps[:, 0:n], func=AF.Exp),
                             reads=[sc[mc][1]], writes=[rPT[pi]])
                    pss, prs = psum()
                    for mc in range(2):
                        S.op("pe", lambda e, mc=mc, pi=pi: e.matmul(pss[:, 0:n], lhsT=ONB[:], rhs=PT[pi][:, mc, 0:n],
                                                                  start=(mc == 0), stop=(mc == 1)), reads=[rPT[pi], rm("ONB")], writes=[prs])
                    tb, tr, _ = tmp()
                    S.op("dve", lambda e, tb=tb: e.reciprocal(out=tb[:, 0:n], in_=pss[:, 0:n]), reads=[prs], writes=[tr])
                    for dc in range(2):
                        po, pro = psum()
                        for mc in range(2):
                            S.op("pe", lambda e, mc=mc, dc=dc, po=po, pi=pi: e.matmul(
                                po[:, 0:n], lhsT=VV[vi][:, mc, (2 * h + dc) * 128:(2 * h + dc + 1) * 128], rhs=PT[pi][:, mc, 0:n],
                                start=(mc == 0), stop=(mc == 1)), reads=[rVV[vi], rPT[pi]], writes=[pro])
                        S.op("dve", lambda e, dc=dc, po=po, tb=tb: e.tensor_tensor(out=QT[:, 2 * h + dc, t.cols], in0=po[:, 0:n],
                                                                               in1=tb[:, 0:n], op=ALU.mult),
                             reads=[pro, tr], writes=[rQ(t, h)])
        for t in stiles:
            hb = [psum_hold() for _ in range(3)]
            rhold = [PSR[i] for i in hb]
            for b in range(16):
                vi = load_kv(ck[l][b], cv[l][b], [])
                pi = pt_rr[0] % 2
                pt_rr[0] += 1
                pts = PT[pi][:, 0, 0:64]
                ps, pr = psum()
                for h in range(4):
                    for mc in range(2):
                        for dc in range(2):
                            S.op("pe", lambda e, h=h, mc=mc, dc=dc, ps=ps, b=b: e.matmul(
                                ps[:, (mc * 4 + h) * 8:(mc * 4 + h + 1) * 8], lhsT=KT[:, 2 * h + dc, mc * 128:(mc + 1) * 128],
                                rhs=QT[:, 2 * h + dc, t.col0 + b * 8:t.col0 + (b + 1) * 8], start=(dc == 0), stop=(dc == 1)),
                                reads=[rKT, rQ(t, h)], writes=[pr])
                S.op("act", lambda e, ps=ps, pts=pts: e.activation(out=pts, in_=ps[:, 0:64], func=AF.Exp), reads=[pr], writes=[rPT[pi]])
                for mc in range(2):
                    S.op("pe", lambda e, mc=mc, b=b, pts=pts: e.matmul(PS[hb[2]][:, b * 32:(b + 1) * 32], lhsT=ONB[:],
                                                                      rhs=pts[:, mc * 32:(mc + 1) * 32], start=(mc == 0), stop=(mc == 1)),
                         reads=[rPT[pi], rm("ONB")], writes=[rhold[2]])
                for h in range(4):
                    for dc in range(2):
                        col = ((h % 2) * 2 + dc) * 128 + b * 8
                        for mc in range(2):
                            S.op("pe", lambda e, h=h, dc=dc, mc=mc, col=col, vi=vi, pts=pts: e.matmul(
                                PS[hb[h // 2]][:, col:col + 8], lhsT=VV[vi][:, mc, (2 * h + dc) * 128:(2 * h + dc + 1) * 128],
                                rhs=pts[:, (mc * 4 + h) * 8:(mc * 4 + h + 1) * 8], start=(mc == 0), stop=(mc == 1)),
                                reads=[rVV[vi], rPT[pi]], writes=[rhold[h // 2]])
            tb, tr, _ = tmp()
            S.op("dve", lambda e, tb=tb: e.reciprocal(out=tb[:, :], in_=PS[hb[2]][:, :]), reads=[rhold[2]], writes=[tr])
            rsv = tb[:, :].rearrange("p (b h t) -> p b h t", b=16, h=4)
            for h in range(4):
                for dc in range(2):
                    col = ((h % 2) * 2 + dc) * 128
                    S.op("dve", lambda e, h=h, dc=dc, col=col, rsv=rsv: e.tensor_tensor(
                        out=QT[:, 2 * h + dc, t.cols].rearrange("p (b t) -> p b t", b=16),
                        in0=PS[hb[h // 2]][:, col:col + 128].rearrange("p (b t) -> p b t", b=16), in1=rsv[:, :, h, :], op=ALU.mult),
                        reads=[rhold[h // 2], tr], writes=[rQ(t, h)])
            for i in hb:
                psum_release(i)

        def epi2(blk, ji, t, pss, prs):
            c = blk * 4 + ji
            act_copy(M[:, c, t.mcols], pss[0][:, 0:t.n], [prs[0]], [rMt(t)])
        proj(tiles, lambda blk: [(x_w_o[l][:, blk * 512:(blk + 1) * 512], 0, 512)], 8, lambda k, t: QT[:, k, t.cols],
             lambda t: [rQ(t, h) for h in range(4)], lambda blk: [[0], [128], [256], [384]], epi2, 2)
        norm_boundary(tiles, row_norm(l, 3), nxt)

    GPW = 30 + 1024
    GP = BIG[:, 0:8 * GPW].rearrange("p (c w) -> p c w", c=8)
    GS = BIG[:, 8 * GPW:8 * GPW + 8 * 16 * 38].rearrange("p (c b w) -> p c b w", c=8, b=16)
    rGPh, rGSh = rm("GPhist"), rm("GShist")
    rGSF, rGPF = rm("GSF"), rm("GPF")
    rCARG = [Res("carg0"), Res("carg1")]

    def rG(t):
        return res(rMisc, ("G", t.ti), "G")

    def mixer0(l, gi, tiles, pre_done, nxt, dl=False):
        flush_deferred()
        j = l // 3
        if not pre_done:
            for t in tiles:
                pre_norm(t, row_norm(l, 0))
        join_BIG()
        join_M()
        has_s = any(t.kind == "S" for t in tiles)
        if gi == 0:
            S.op("pool", lambda e: e.memset(GP[:, :, 0:30], 0.0), writes=[rGPh])
        else:
            S.op("pool", lambda e: e.tensor_copy(out=GP[:, :, 0:30], in_=CARRY_G[j][:]), reads=[rCARG[j]], writes=[rGPh])
        if has_s:
            sem = S.dsem(f"stc{j}")
            MS = MF[:, 0:4 * D].rearrange("p (b d) -> p b d", b=4)
            src = st_conv[j].rearrange("b j d -> (b j) d")
            S.op("sp", lambda e: e.dma_start(out=MS[:, 0:3, :], in_=src[0:384, :].rearrange("(b p) d -> p b d", p=128)),
                 writes=[rm("Mstage")], dma=sem)
            S.op("sp", lambda e: e.dma_start(out=MS[0:96, 3, :], in_=src[384:480, :]), writes=[rm("Mstage")], dma=sem, par=True)
            for c in range(NCH):
                ps, pr = psum()
                for rb in range(4):
                    nr = 128 if rb < 3 else 96
                    S.op("pe", lambda e, c=c, rb=rb, nr=nr, ps=ps: e.transpose(out=ps[:, rb * 128:rb * 128 + nr],
                                                                             in_=MS[0:nr, rb, c * 128:(c + 1) * 128], identity=IDF[0:nr, 0:nr]),
                         reads=[rm("Mstage"), rCONST], writes=[pr])
                any_copy(GS[:, c, :, 0:30], ps[:, 0:480].rearrange("p (b w) -> p b w", b=16), [pr], [rGSh])
            osm = osem(f"ncas_cp{j}")
            S.op("sp", lambda e: e.dma_start(out=ncas[j][:, 0:22, :], in_=st_conv[j][:, 8:30, :]), dma=osm)
            join_M()

        def wparts(blk):
            return [(a_w_pw1[j][:, blk * 256:(blk + 1) * 256], 0, 256), (a_w_pw1[j][:, D + blk * 256:D + (blk + 1) * 256], 256, 256)]

        def epi(blk, ji, t, pss, prs):
            c = blk * 2 + ji
            n = t.n
            tb, tr, _ = tmp()
            S.op("act", lambda e: e.activation(out=tb[:, 0:n], in_=pss[1][:, 0:n], func=AF.Sigmoid, bias=cvs(ROW_BPW1 + 2 * j + 1, c)),
                 reads=[prs[1], rCV], writes=[tr])
            if t.kind == "P":
                S.op("dve", lambda e: e.scalar_tensor_tensor(out=GP[:, c, 30 + t.col0:30 + t.col0 + n], in0=pss[0][:, 0:n],
                                                           scalar=cvs(ROW_BPW1 + 2 * j, c), in1=tb[:, 0:n], op0=ALU.add, op1=ALU.mult),
                     reads=[prs[0], tr, rCV], writes=[rG(t)])
                if t.tok0 + n == 2048:
                    S.op("dve", lambda e: e.scalar_tensor_tensor(out=GPF[:, c, 0:30], in0=pss[0][:, n - 30:n], scalar=cvs(ROW_BPW1 + 2 * j, c),
                                                               in1=tb[:, n - 30:n], op0=ALU.add, op1=ALU.mult),
                         reads=[prs[0], tr, rCV], writes=[rGPF])
            else:
                S.op("dve", lambda e: e.scalar_tensor_tensor(out=GSF[:, c, 0:128], in0=pss[0][:, 0:n], scalar=cvs(ROW_BPW1 + 2 * j, c),
                                                           in1=tb[:, 0:n], op0=ALU.add, op1=ALU.mult),
                     reads=[prs[0], tr, rCV], writes=[rGSF])
                S.op("dve", lambda e: e.tensor_copy(out=GS[:, c, :, 30:38], in_=GSF[:, c, 0:128].rearrange("p (b t) -> p b t", b=16)),
                     reads=[rGSF], writes=[rG(t)])
        proj(tiles, wparts, 8, lambda k, t: H[:, k, t.cols], lambda t: [rHt(t)], lambda blk: [[0, 256], [128, 384]], epi, 4)

        ptiles = [t for t in tiles if t.kind == "P"]
        if gi == 0:
            last = ptiles[-1]
            S.op("pool", lambda e: e.tensor_copy(out=CARRY_G[j][:], in_=GP[:, :, 1024:1054]), reads=[rG(last)], writes=[rCARG[j]])
        if has_s:
            for hh in range(2):
                ps, pr = psum()
                for cc in range(4):
                    c = hh * 4 + cc
                    S.op("pe", lambda e, c=c, cc=cc, ps=ps: e.transpose(out=ps[:, cc * 128:(cc + 1) * 128], in_=GSF[:, c, 0:128], identity=IDF[:]),
                         reads=[rGSF, rCONST], writes=[pr])
                tb, tr, tsem = tmp()
                any_copy(tb[:, :], ps[:, :], [pr], [tr])
                for b in range(16):
                    S.op("sp", lambda e, tb=tb, hh=hh, b=b: e.dma_start(out=ncas[j][b, 22:30, hh * 512:(hh + 1) * 512], in_=tb[b * 8:(b + 1) * 8, :]),
                         reads=[tr], dma=tsem, par=(b > 0))
        if gi == 1:
            for hh in range(2):
                ps, pr = psum()
                for cc in range(4):
                    c = hh * 4 + cc
                    S.op("pe", lambda e, c=c, cc=cc, ps=ps: e.transpose(out=ps[0:30, cc * 128:(cc + 1) * 128], in_=GPF[:, c, 0:30], identity=IDF[:]),
                         reads=[rGPF, rCONST], writes=[pr])
                tb, tr, tsem = tmp()
                any_copy(tb[0:30, :], ps[0:30, :], [pr], [tr])
                S.op("sp", lambda e, tb=tb, hh=hh: e.dma_start(out=ncap[j][:, hh * 512:(hh + 1) * 512], in_=tb[0:30, :]), reads=[tr], dma=tsem)

        hist_res = {"P": rGPh, "S": rGSh}
        for c in range(NCH):
            slot, sres, ssem = wslot()
            DG = slot[:, 0:31 * 128].rearrange("p (k m) -> p k m", k=31)
            r0 = ROW_WDW + 31 * j
            S.op("dve", lambda e, c=c, DG=DG: e.tensor_tensor(out=DG, in0=IDB[:].unsqueeze(1).to_broadcast([128, 31, 128]),
                                                            in1=CV[:, c, r0:r0 + 31].unsqueeze(2).to_broadcast([128, 31, 128]), op=ALU.mult),
                 reads=[rm("IDB"), rCV], writes=[sres])
            for ti, t in enumerate(tiles):
                ps, pr = psum()
                rd = [sres, rG(t), hist_res[t.kind]] + ([rG(tiles[ti - 1])] if (t.kind == "P" and ti > 0) else [])
                for k in range(31):
                    if t.kind == "P":
                        rhs = (lambda k=k, c=c, t=t: GP[:, c, t.col0 + k:t.col0 + k + t.n])
                    else:
                        rhs = (lambda k=k, c=c: GS[:, c, :, k:k + 8])
                    S.op("pe", lambda e, k=k, ps=ps, rhs=rhs, DG=DG, t=t: e.matmul(ps[:, 0:t.n], lhsT=DG[:, k, :], rhs=rhs(),
                                                                               start=(k == 0), stop=(k == 30)), reads=rd, writes=[pr])
                S.op("act", lambda e, c=c, ps=ps, t=t: e.activation(out=M[:, c, t.mcols], in_=ps[:, 0:t.n], func=AF.Identity,
                                                                  bias=cvs(ROW_BDW + j, c)), reads=[pr, rCV], writes=[rMt(t)])
        lnb = {}
        for t in tiles:
            slot, sres, _ = wslot()
            SQv = slot[:, 0:NCH * 512].rearrange("p (c w) -> p c w", c=NCH)
            S.op("act", lambda e, t=t, SQv=SQv: e.activation(out=SQv[:, :, 0:t.n], in_=M[:, :, t.mcols], func=AF.Square), reads=[rMt(t)], writes=[sres])
            S.op("dve", lambda e, t=t: e.tensor_copy(out=H[:, :, t.cols], in_=M[:, :, t.mcols]), reads=[rMt(t)], writes=[rHt(t)])
            lnb[t.ti] = (SQv, sres)
        for t in tiles:
            SQv, sres = lnb[t.ti]
            psq, prq = psum()
            pmu, prm = psum()
            for c in range(NCH):
                S.op("pe", lambda e, c=c, t=t, psq=psq, SQv=SQv: e.matmul(psq[:, 0:t.n], lhsT=ONB[:], rhs=SQv[:, c, 0:t.n], start=(c == 0), stop=(c == NCH - 1)),
                     reads=[sres, rm("ONB")], writes=[prq])
            for c in range(NCH):
                S.op("pe", lambda e, c=c, t=t, pmu=pmu: e.matmul(pmu[:, 0:t.n], lhsT=ONB[:], rhs=H[:, c, t.cols], start=(c == 0), stop=(c == NCH - 1)),
                     reads=[rHt(t), rm("ONB")], writes=[prm])
            lnb[t.ti] = (psq, prq, pmu, prm)
        for t in tiles:
            n = t.n
            psq, prq, pmu, prm = lnb[t.ti]
            mu, rmu, _ = tmp()
            tv, rtv, _ = tmp()
            S.op("dve", lambda e, t=t, mu=mu, pmu=pmu: e.tensor_scalar(out=mu[:, 0:t.n], in0=pmu[:, 0:t.n], scalar1=1.0 / D, scalar2=None, op0=ALU.mult),
                 reads=[prm], writes=[rmu])
            S.op("dve", lambda e, t=t, mu=mu, tv=tv: e.tensor_tensor(out=tv[:, 0:t.n], in0=mu[:, 0:t.n], in1=mu[:, 0:t.n], op=ALU.mult),
                 reads=[rmu], writes=[rtv])
            S.op("dve", lambda e, t=t, tv=tv, psq=psq: e.scalar_tensor_tensor(out=tv[:, 0:t.n], in0=psq[:, 0:t.n], scalar=1.0 / D, in1=tv[:, 0:t.n],
                                                                          op0=ALU.mult, op1=ALU.subtract), reads=[prq, rtv], writes=[rtv])
            RSb, rRSb, _ = tmp()
            S.op("act", lambda e, t=t, tv=tv, RSb=RSb: e.activation(out=RSb[:, 0:t.n], in_=tv[:, 0:t.n], func=AF.Sqrt, bias=EPSB[LN_EPS][:, 0:1]),
                 reads=[rtv, rCONST2], writes=[rRSb])
            S.op("dve", lambda e, t=t, RSb=RSb: e.reciprocal(out=RSb[:, 0:t.n], in_=RSb[:, 0:t.n]), reads=[rRSb], writes=[rRSb])
            S.op("dve", lambda e, t=t, mu=mu: e.tensor_tensor(out=M[:, :, t.mcols], in0=M[:, :, t.mcols],
                                                            in1=mu[:, 0:t.n].unsqueeze(1).to_broadcast([128, NCH, t.n]), op=ALU.subtract),
                 reads=[rMt(t), rmu], writes=[rMt(t)])
            for c in range(NCH):
                S.op("dve", lambda e, c=c, t=t, RSb=RSb: e.scalar_tensor_tensor(out=M[:, c, t.mcols], in0=M[:, c, t.mcols], scalar=cvs(ROW_LNG + j, c),
                                                                     in1=RSb[:, 0:t.n], op0=ALU.mult, op1=ALU.mult),
                     reads=[rMt(t), rRSb, rCV], writes=[rMt(t)])
                S.op("act", lambda e, c=c, t=t: e.activation(out=H[:, c, t.cols], in_=M[:, c, t.mcols], func=AF.Silu, bias=cvs(ROW_LNB + j, c)),
                     reads=[rMt(t), rCV], writes=[rHt(t)])

        def epi2(blk, ji, t, pss, prs):
            c = blk * 4 + ji
            S.op("act", lambda e: e.activation(out=M[:, c, t.mcols], in_=pss[0][:, 0:t.n], func=AF.Identity, bias=cvs(ROW_BPW2 + j, c)),
                 reads=[prs[0], rCV], writes=[rMt(t)])
        proj(tiles, lambda blk: [(a_w_pw2[j][:, blk * 512:(blk + 1) * 512], 0, 512)], 8, lambda k, t: H[:, k, t.cols],
             lambda t: [rHt(t)], lambda blk: [[0], [128], [256], [384]], epi2, 2)
        norm_boundary(tiles, row_norm(l, 1), nxt)

    PSH = GSF[:, :, 0:160].rearrange("p c (b w) -> p c b w", b=16)
    rPSH = rm("PSH")
    rCARP = rm("CARP")

    def mixer2(l, gi, tiles, pre_done, nxt, dl=False):
        flush_deferred()
        if not pre_done:
            for t in tiles:
                pre_norm(t, row_norm(l, 0))
        join_BIG()
        join_M()
        has_s = any(t.kind == "S" for t in tiles)
        rMh = rm("Mhist")
        if has_s:
            sem = S.dsem("stsc")
            S.op("sp", lambda e: e.dma_start(out=MF[0:32, 0:D], in_=st_sc.rearrange("b j d -> (b j) d")), writes=[rm("Mstage")], dma=sem)
            ps, pr = psum()
            for c in range(NCH):
                S.op("pe", lambda e, c=c, ps=ps: e.transpose(out=ps[:, c * 32:(c + 1) * 32], in_=MF[0:32, c * 128:(c + 1) * 128], identity=IDF[0:32, 0:32]),
                     reads=[rm("Mstage"), rCONST], writes=[pr])
            for c in range(NCH):
                any_copy(PSH[:, c, :, 0:2], ps[:, c * 32:(c + 1) * 32].rearrange("p (b w) -> p b w", b=16), [pr], [rPSH])
            join_M()
        if gi == 0:
            S.op("pool", lambda e: e.memset(M[:, :, 0:2], 0.0), writes=[rMh])
        else:
            S.op("pool", lambda e: e.tensor_copy(out=M[:, :, 0:2], in_=CARRY_P[:]), reads=[rCARP], writes=[rMh])

        def wparts(blk):
            return [(c_w_in[:, D + blk * 256:D + (blk + 1) * 256], 0, 256), (c_w_in[:, 2 * D + blk * 256:2 * D + (blk + 1) * 256], 256, 256)]

        def epi(blk, ji, t, pss, prs):
            c = blk * 2 + ji
            n = t.n
            tb, tr, _ = tmp()
            act_copy(tb[:, 0:n], pss[0][:, 0:n], [prs[0]], [tr])
            if t.kind == "P":
                S.op("dve", lambda e: e.tensor_tensor(out=M[:, c, t.mcols], in0=tb[:, 0:n], in1=pss[1][:, 0:n], op=ALU.mult),
                     reads=[tr, prs[1]], writes=[rMt(t)])
            else:
                S.op("dve", lambda e: e.tensor_tensor(out=PSH[:, c, :, 2:10], in0=tb[:, 0:n].rearrange("p (b t) -> p b t", b=16),
                                                    in1=pss[1][:, 0:n].rearrange("p (b t) -> p b t", b=16), op=ALU.mult),
                     reads=[tr, prs[1]], writes=[rPSH])
        proj(tiles, wparts, 8, lambda k, t: H[:, k, t.cols], lambda t: [rHt(t)], lambda blk: [[0, 256], [128, 384]], epi, 4)
        ptiles = [t for t in tiles if t.kind == "P"]
        lastp = ptiles[-1]
        lc = 2 + lastp.col0 + lastp.n
        if gi == 0:
            S.op("pool", lambda e: e.tensor_copy(out=CARRY_P[:], in_=M[:, :, lc - 2:lc]), reads=[rMt(lastp)], writes=[rCARP])
        if has_s:
            tb, tr, _ = tmp()
            tbv = tb[:, 0:256].rearrange("p (c b w) -> p c b w", c=8, b=16)
            S.op("dve", lambda e: e.tensor_copy(out=tbv, in_=PSH[:, :, :, 8:10]), reads=[rPSH], writes=[tr])
            for hh in range(2):
                ps, pr = psum()
                for cc in range(4):
                    c = hh * 4 + cc
                    S.op("pe", lambda e, c=c, cc=cc, ps=ps, tb=tb: e.transpose(out=ps[0:32, cc * 128:(cc + 1) * 128], in_=tb[:, c * 32:(c + 1) * 32],
                                                                             identity=IDF[:]), reads=[tr, rCONST], writes=[pr])
                t2, tr2, tsem2 = tmp()
                any_copy(t2[0:32, :], ps[0:32, :], [pr], [tr2])
                S.op("sp", lambda e, t2=t2, hh=hh: e.dma_start(out=nscs.rearrange("b j d -> (b j) d")[:, hh * 512:(hh + 1) * 512], in_=t2[0:32, :]),
                     reads=[tr2], dma=tsem2)
        if gi == 1:
            for hh in range(2):
                ps, pr = psum()
                for cc in range(4):
                    c = hh * 4 + cc
                    S.op("pe", lambda e, c=c, cc=cc, ps=ps: e.transpose(out=ps[0:2, cc * 128:(cc + 1) * 128], in_=M[:, c, lc - 2:lc], identity=IDF[:]),
                         reads=[rMt(lastp), rCONST], writes=[pr])
                t2, tr2, tsem2 = tmp()
                any_copy(t2[0:2, :], ps[0:2, :], [pr], [tr2])
                S.op("sp", lambda e, t2=t2, hh=hh: e.dma_start(out=nscp[:, hh * 512:(hh + 1) * 512], in_=t2[0:2, :]), reads=[tr2], dma=tsem2)

        U = BIG[:, 0:8 * GW].rearrange("p (c w) -> p c w", c=8)

        def epi_b(blk, ji, t, pss, prs):
            c = blk * 4 + ji
            n = t.n
            tb, tr, _ = tmp()
            w0, w1, w2 = cvs(ROW_WCONV, c), cvs(ROW_WCONV + 1, c), cvs(ROW_WCONV + 2, c)
            if t.kind == "P":
                m0 = 2 + t.col0
                rd = [rMt(t), rMh, rCV] + [rMt(x) for x in tiles if x.kind == "P" and x.ti == t.ti - 1]
                x2, x1, x0 = M[:, c, m0:m0 + n], M[:, c, m0 - 1:m0 - 1 + n], M[:, c, m0 - 2:m0 - 2 + n]
                tv = tb[:, 0:n]
                pv = pss[0][:, 0:n]
                ov = U[:, c, t.cols]
            else:
                rd = [rPSH, rCV]
                x2, x1, x0 = PSH[:, c, :, 2:10], PSH[:, c, :, 1:9], PSH[:, c, :, 0:8]
                tv = tb[:, 0:n].rearrange("p (b t) -> p b t", b=16)
                pv = pss[0][:, 0:n].rearrange("p (b t) -> p b t", b=16)
                ov = U[:, c, t.cols].rearrange("p (b t) -> p b t", b=16)
            S.op("dve", lambda e: e.tensor_scalar(out=tv, in0=x2, scalar1=w2, scalar2=None, op0=ALU.mult), reads=rd, writes=[tr])
            S.op("dve", lambda e: e.scalar_tensor_tensor(out=tv, in0=x1, scalar=w1, in1=tv, op0=ALU.mult, op1=ALU.add), reads=rd + [tr], writes=[tr])
            S.op("dve", lambda e: e.scalar_tensor_tensor(out=tv, in0=x0, scalar=w0, in1=tv, op0=ALU.mult, op1=ALU.add), reads=rd + [tr], writes=[tr])
            S.op("dve", lambda e: e.tensor_tensor(out=ov, in0=pv, in1=tv, op=ALU.mult), reads=[prs[0], tr], writes=[rBt(t)])
        proj(tiles, lambda blk: [(c_w_in[:, blk * 512:(blk + 1) * 512], 0, 512)], 8, lambda k, t: H[:, k, t.cols], lambda t: [rHt(t)],
             lambda blk: [[0], [128], [256], [384]], epi_b, 2)

        def epi_c(blk, ji, t, pss, prs):
            c = blk * 4 + ji
            act_copy(M[:, c, t.mcols], pss[0][:, 0:t.n], [prs[0]], [rMt(t)] + ([rMh] if False else []))
        S.op("dve", lambda e: e.memset(DUMMY[:, 1:2], 0.0), writes=[rMt(t) for t in tiles] + [rMh])
        proj(tiles, lambda blk: [(c_w_out[:, blk * 512:(blk + 1) * 512], 0, 512)], 8, lambda k, t: U[:, k, t.cols], lambda t: [rBt(t)],
             lambda blk: [[0], [128], [256], [384]], epi_c, 2)
        norm_boundary(tiles, row_norm(l, 1), nxt)

    P_LR, P_LI, P_DT, P_MAG, P_TH, P_ABR, P_ABI, P_FR, P_FI, P_T0, P_T1, P_T2 = range(12)
    rS5P = rm("S5P")
    rLB, rLC = rm("LB"), rm("LC")
    rCARS = rm("CARS")
    rAS0 = rm("AS0")
    AS0 = BIG[:, 9216:11264].bitcast(F32).rearrange("p (r j b) -> p r j b", r=2, j=32)
    SCL = 0.999999

    def sp_(i):
        return S5P[:, i, :]

    PI_LO = 3.1415925

    def sin_of(out, in0, mul, shift, ang, ki, reads, writes):
        S.op("dve", lambda e: e.tensor_scalar(out=ang, in0=in0, scalar1=mul, scalar2=shift + 8 * TWO_PI, op0=ALU.mult, op1=ALU.add),
             reads=reads, writes=writes)
        S.op("dve", lambda e: e.tensor_scalar(out=ki, in0=ang, scalar1=1.0 / TWO_PI, scalar2=None, op0=ALU.mult), reads=writes, writes=writes)
        S.op("dve", lambda e: e.scalar_tensor_tensor(out=ang, in0=ki, scalar=-TWO_PI, in1=ang, op0=ALU.mult, op1=ALU.add),
             reads=writes, writes=writes)
        S.op("dve", lambda e: e.tensor_scalar(out=ang, in0=ang, scalar1=-PI_LO, scalar2=PI_LO, op0=ALU.max, op1=ALU.min), reads=writes, writes=writes)
        S.op("act", lambda e: e.activation(out=out, in_=ang, func=AF.Sin), reads=writes, writes=writes)

    def s5_setup():
        rS = rm("Mstage")
        sem = S.dsem("s5p")
        for gl in range(2):
            for (src, idx) in ((b_lam_re, P_LR), (b_lam_im, P_LI)):
                S.op("sp", lambda e, gl=gl, src=src, idx=idx: e.dma_start(
                    out=S5P[gl * 64:(gl + 1) * 64, idx, :], in_=src.rearrange("(j g) p -> g p j", g=2)[gl], allow_slow_non_contiguous=True),
                    writes=[rS5P], dma=sem, par=True)
            S.op("sp", lambda e, gl=gl: e.dma_start(
                out=S5P[gl * 64:(gl + 1) * 64, P_DT, :], in_=b_log_dt.rearrange("(j g) -> g j", g=2)[gl:gl + 1, :].partition_broadcast(64),
                allow_slow_non_contiguous=True), writes=[rS5P], dma=sem, par=True)
        S.op("act", lambda e: e.activation(out=sp_(P_DT), in_=sp_(P_DT), func=AF.Exp), reads=[rS5P], writes=[rS5P])
        S.op("dve", lambda e: e.tensor_tensor(out=sp_(P_T0), in0=sp_(P_LR), in1=sp_(P_DT), op=ALU.mult), reads=[rS5P], writes=[rS5P])
        S.op("act", lambda e: e.activation(out=sp_(P_MAG), in_=sp_(P_T0), func=AF.Exp), reads=[rS5P], writes=[rS5P])
        S.op("dve", lambda e: e.tensor_tensor(out=sp_(P_TH), in0=sp_(P_LI), in1=sp_(P_DT), op=ALU.mult), reads=[rS5P], writes=[rS5P])
        sin_of(sp_(P_ABI), sp_(P_TH), 1.0, 0.0, sp_(P_T0), sp_(P_T1).bitcast(I32), [rS5P], [rS5P])
        sin_of(sp_(P_ABR), sp_(P_TH), 1.0, math.pi / 2, sp_(P_T0), sp_(P_T1).bitcast(I32), [rS5P], [rS5P])
        S.op("dve", lambda e: e.tensor_tensor(out=sp_(P_ABI), in0=sp_(P_ABI), in1=sp_(P_MAG), op=ALU.mult), reads=[rS5P], writes=[rS5P])
        S.op("dve", lambda e: e.tensor_tensor(out=sp_(P_ABR), in0=sp_(P_ABR), in1=sp_(P_MAG), op=ALU.mult), reads=[rS5P], writes=[rS5P])
        S.op("dve", lambda e: e.tensor_tensor(out=sp_(P_T0), in0=sp_(P_LR), in1=sp_(P_LR), op=ALU.mult), reads=[rS5P], writes=[rS5P])
        S.op("dve", lambda e: e.tensor_tensor(out=sp_(P_T1), in0=sp_(P_LI), in1=sp_(P_LI), op=ALU.mult), reads=[rS5P], writes=[rS5P])
        S.op("dve", lambda e: e.tensor_tensor(out=sp_(P_T0), in0=sp_(P_T0), in1=sp_(P_T1), op=ALU.add), reads=[rS5P], writes=[rS5P])
        S.op("dve", lambda e: e.reciprocal(out=sp_(P_T0), in_=sp_(P_T0)), reads=[rS5P], writes=[rS5P])
        S.op("dve", lambda e: e.tensor_scalar(out=sp_(P_T1), in0=sp_(P_ABR), scalar1=-1.0, scalar2=None, op0=ALU.add), reads=[rS5P], writes=[rS5P])
        S.op("dve", lambda e: e.tensor_tensor(out=sp_(P_FR), in0=sp_(P_T1), in1=sp_(P_LR), op=ALU.mult), reads=[rS5P], writes=[rS5P])
        S.op("dve", lambda e: e.tensor_tensor(out=sp_(P_T2), in0=sp_(P_ABI), in1=sp_(P_LI), op=ALU.mult), reads=[rS5P], writes=[rS5P])
        S.op("dve", lambda e: e.tensor_tensor(out=sp_(P_FR), in0=sp_(P_FR), in1=sp_(P_T2), op=ALU.add), reads=[rS5P], writes=[rS5P])
        S.op("dve", lambda e: e.tensor_tensor(out=sp_(P_FR), in0=sp_(P_FR), in1=sp_(P_T0), op=ALU.mult), reads=[rS5P], writes=[rS5P])
        S.op("dve", lambda e: e.tensor_tensor(out=sp_(P_FI), in0=sp_(P_ABI), in1=sp_(P_LR), op=ALU.mult), reads=[rS5P], writes=[rS5P])
        S.op("dve", lambda e: e.tensor_tensor(out=sp_(P_T2), in0=sp_(P_T1), in1=sp_(P_LI), op=ALU.mult), reads=[rS5P], writes=[rS5P])
        S.op("dve", lambda e: e.tensor_tensor(out=sp_(P_FI), in0=sp_(P_FI), in1=sp_(P_T2), op=ALU.subtract), reads=[rS5P], writes=[rS5P])
        S.op("dve", lambda e: e.tensor_tensor(out=sp_(P_FI), in0=sp_(P_FI), in1=sp_(P_T0), op=ALU.mult), reads=[rS5P], writes=[rS5P])
        BR0 = MF[:, 0:1024].rearrange("p (j w) -> p j w", j=32)
        BI0 = MF[:, 1024:2048].rearrange("p (j w) -> p j w", j=32)
        BBR = MF[:, 2048:3072].rearrange("p (j w) -> p j w", j=32)
        BBI = MF[:, 3072:4096].rearrange("p (j w) -> p j w", j=32)
        TMPB = MF[:, 4096:5120].rearrange("p (j w) -> p j w", j=32)
        S.op("pool", lambda e: e.memset(MF[:, 0:2048], 0.0), writes=[rS])
        semb = S.dsem("s5b")
        for gl in range(2):
            for (src, dstb) in ((b_B_re, BR0), (b_B_im, BI0)):
                S.op("sp", lambda e, gl=gl, src=src, dstb=dstb: e.dma_start(
                    out=dstb[gl * 64:(gl + 1) * 64, :, gl * 16:(gl + 1) * 16], in_=src.rearrange("(j g) p h -> g p j h", g=2)[gl]),
                    reads=[rS], writes=[rS], dma=semb, par=True)
        fr_b = sp_(P_FR).unsqueeze(2).to_broadcast([128, 32, 32])
        fi_b = sp_(P_FI).unsqueeze(2).to_broadcast([128, 32, 32])
        S.op("dve", lambda e: e.tensor_tensor(out=BBR, in0=BR0, in1=fr_b, op=ALU.mult), reads=[rS, rS5P], writes=[rS])
        S.op("dve", lambda e: e.tensor_tensor(out=TMPB, in0=BI0, in1=fi_b, op=ALU.mult), reads=[rS, rS5P], writes=[rS])
        S.op("dve", lambda e: e.tensor_tensor(out=BBR, in0=BBR, in1=TMPB, op=ALU.subtract), reads=[rS], writes=[rS])
        S.op("dve", lambda e: e.tensor_tensor(out=BBI, in0=BI0, in1=fr_b, op=ALU.mult), reads=[rS, rS5P], writes=[rS])
        S.op("dve", lambda e: e.tensor_tensor(out=TMPB, in0=BR0, in1=fi_b, op=ALU.mult), reads=[rS, rS5P], writes=[rS])
        S.op("dve", lambda e: e.tensor_tensor(out=BBI, in0=BBI, in1=TMPB, op=ALU.add), reads=[rS], writes=[rS])
        for ri, BB in enumerate((BBR, BBI)):
            for q in range(8):
                ps, pr = psum()
                S.op("pe", lambda e, q=q, BB=BB, ps=ps: e.transpose(out=ps[:, 0:128], in_=BB[:, 4 * q:4 * q + 4, :], identity=IDF[:]),
                     reads=[rS, rCONST], writes=[pr])
                any_copy(LB[:, q, ri, :], ps[:, 0:128], [pr], [rLB])
        CINR = MF[:, 5120:6144].rearrange("p (c w) -> p c w", c=8)
        CINI = MF[:, 6144:7168].rearrange("p (c w) -> p c w", c=8)
        S.op("pool", lambda e: e.memset(MF[:, 5120:7168], 0.0), writes=[rS])
        semc = S.dsem("s5c")
        for (src, dstc) in ((b_C_re, CINR), (b_C_im, CINI)):
            sv = src.rearrange("(c jj g) h p -> jj g h c p", c=8, jj=4, g=2)
            for jj in range(4):
                for gp in range(2):
                    S.op("sp", lambda e, jj=jj, gp=gp, sv=sv, dstc=dstc: e.dma_start(
                        out=dstc[32 * jj + 16 * gp:32 * jj + 16 * gp + 16, :, 64 * gp:64 * gp + 64], in_=sv[jj, gp]),
                        reads=[rS], writes=[rS], dma=semc, par=True)
        for ri, CIN in enumerate((CINR, CINI)):
            for cc in range(8):
                ps, pr = psum()
                S.op("pe", lambda e, cc=cc, CIN=CIN, ps=ps: e.transpose(out=ps[:, 0:128], in_=CIN[:, cc, :], identity=IDF[:]),
                     reads=[rS, rCONST], writes=[pr])
                act_copy(LC[:, 4 * cc:4 * cc + 4, ri, :], ps[:, 0:128].rearrange("p (jj w) -> p jj w", jj=4), [pr], [rLC],
                         scale=(1.0 if ri == 0 else -1.0))
        S.op("pool", lambda e: e.memset(MASK[:], 1.0), writes=[rm("MASK")])
        S.op("pool", lambda e: e.memset(MASK[:].rearrange("p (b t) -> p b t", b=16)[:, :, 0:1], 0.0), writes=[rm("MASK")])

    def mixer1(l, gi, tiles, pre_done, nxt, dl=False):
        flush_deferred()
        strict_prev = S.strict
        if not pre_done:
            for t in tiles:
                pre_norm(t, row_norm(l, 0))
        has_s = any(t.kind == "S" for t in tiles)
        rS = rm("Mstage")
        join_M()
        join_BIG()
        if has_s:
            sem = S.dsem("s5s0")
            S0 = MF[:, 4096:5120].rearrange("p (r j b) -> p r j b", r=2, j=32)
            for ri, src in enumerate((st_re, st_im)):
                S.op("sp", lambda e, src=src: e.dma_start(out=MF[0:16, 0:4096], in_=src), reads=[rS], writes=[rS], dma=sem)
                ps, pr = psum()
                for jx in range(32):
                    S.op("pe", lambda e, jx=jx, ps=ps: e.transpose(out=ps[:, jx * 16:(jx + 1) * 16], in_=MF[0:16, jx * 128:(jx + 1) * 128],
                                                                 identity=IDF[0:16, 0:16]), reads=[rS, rCONST], writes=[pr])
                any_copy(S0[:, ri], ps[:, 0:512].rearrange("p (j b) -> p j b", j=32), [pr, rS], [rS])
            abr_b = sp_(P_ABR).unsqueeze(2).to_broadcast([128, 32, 16])
            abi_b = sp_(P_ABI).unsqueeze(2).to_broadcast([128, 32, 16])
            TM = MF[:, 5120:5632].rearrange("p (j b) -> p j b", j=32)
            S.op("dve", lambda e: e.tensor_tensor(out=AS0[:, 0], in0=S0[:, 0], in1=abr_b, op=ALU.mult), reads=[rS, rS5P], writes=[rAS0])
            S.op("dve", lambda e: e.tensor_tensor(out=TM, in0=S0[:, 1], in1=abi_b, op=ALU.mult), reads=[rS, rS5P], writes=[rS])
            S.op("dve", lambda e: e.tensor_tensor(out=AS0[:, 0], in0=AS0[:, 0], in1=TM, op=ALU.subtract), reads=[rS, rAS0], writes=[rAS0])
            S.op("dve", lambda e: e.tensor_tensor(out=AS0[:, 1], in0=S0[:, 1], in1=abr_b, op=ALU.mult), reads=[rS, rS5P], writes=[rAS0])
            S.op("dve", lambda e: e.tensor_tensor(out=TM, in0=S0[:, 0], in1=abi_b, op=ALU.mult), reads=[rS, rS5P, rAS0], writes=[rS])
            S.op("dve", lambda e: e.tensor_tensor(out=AS0[:, 1], in0=AS0[:, 1], in1=TM, op=ALU.add), reads=[rS, rAS0], writes=[rAS0])
            join_M()
        def scr(i):
            return MF[:, i * 512:(i + 1) * 512], s5_res[i]
        (IOT, rIOT), (COS, rCOS), (SIN, rSIN), (A, rA), (B, rB), (VR, rVR), (VI, rVI), (RR, rRR), (RI, rRI), (CS8, rCS8) = [scr(i) for i in range(10)]
        MPb = [MF[:, 5120 + k * 256:5376 + k * 256].bitcast(BF16) for k in range(4)]
        NSN, rNSN = MF[:, 6144:6656], rm("NSN")
        MSKM = MF[:, 6656:6784]
        INI = MF[:, 6784:6792]
        SOUT = GSF[:].rearrange("p c w -> p (c w)")[:, 0:1024].rearrange("p (r j b) -> p r j b", r=2, j=32)
        rSRB, rSIB, rMSK, rINI, rSOUT = rm("SRB"), rm("SIB"), rm("MSKM"), rm("INI"), rm("SOUT")
        S.op("pool", lambda e: e.iota(IOT, [[1, 512]], base=0, channel_multiplier=0, allow_small_or_imprecise_dtypes=True), writes=[rIOT])
        YT = BIG[:, 0:8 * GW].rearrange("p (c w) -> p c w", c=8)
        hold = {}
        bankd = {}

        def emit_B(j, ti):
            t = tiles[ti]
            q, jj = j // 4, j % 4
            pxr, prr = psum()
            pxi, pri = psum()
            for (px, prx, ri) in ((pxr, prr, 0), (pxi, pri, 1)):
                S.op("pe", lambda e, px=px, ri=ri: e.matmul(
                    px[:, 0:t.n], lhsT=LB[32 * jj:32 * jj + 32, q, ri, :], rhs=H[32 * jj:32 * jj + 32, q, t.cols], start=True, stop=True,
                    tile_position=(32 * jj, 0)), reads=[rLB, rHt(t)], writes=[prx])
            bankd[(j, ti)] = (pxr, prr, pxi, pri)

        for j in range(32):
            q, jj = j // 4, j % 4
            thj = S5P[:, P_TH, j:j + 1]
            kI = VR.bitcast(I32)
            S.op("dve", lambda e: e.tensor_scalar(out=B, in0=IOT, scalar1=thj, scalar2=8 * TWO_PI, op0=ALU.mult, op1=ALU.add),
                 reads=[rIOT, rS5P], writes=[rB])
            S.op("dve", lambda e: e.tensor_scalar(out=kI, in0=B, scalar1=1.0 / TWO_PI, scalar2=None, op0=ALU.mult), reads=[rB], writes=[rVR])
            S.op("dve", lambda e: e.scalar_tensor_tensor(out=B, in0=kI, scalar=-TWO_PI, in1=B, op0=ALU.mult, op1=ALU.add),
                 reads=[rVR, rB], writes=[rB])
            S.op("dve", lambda e: e.tensor_scalar(out=B, in0=B, scalar1=-PI_LO, scalar2=PI_LO, op0=ALU.max, op1=ALU.min), reads=[rB], writes=[rB])
            S.op("act", lambda e: e.activation(out=SIN, in_=B, func=AF.Sin), reads=[rB], writes=[rSIN])
            S.op("act", lambda e: e.activation(out=A, in_=B, func=AF.Abs), reads=[rB], writes=[rA])
            S.op("act", lambda e: e.activation(out=COS, in_=A, func=AF.Sin, scale=-1.0, bias=HPI[:, 0:1]), reads=[rA, rCONST2], writes=[rCOS])
            S.op("act", lambda e: e.mul(out=NSN, in_=SIN, mul=-1.0), reads=[rSIN], writes=[rNSN])
            if has_s:
                S.op("dve", lambda e: e.tensor_copy(out=CS8[:, 0:128].rearrange("p (b t) -> p b t", b=16),
                                                    in_=COS[:, 0:8].unsqueeze(1).to_broadcast([128, 16, 8])), reads=[rCOS], writes=[rCS8])
                S.op("dve", lambda e: e.tensor_copy(out=CS8[:, 128:256].rearrange("p (b t) -> p b t", b=16),
                                                    in_=SIN[:, 0:8].unsqueeze(1).to_broadcast([128, 16, 8])), reads=[rSIN], writes=[rCS8])
                S.op("dve", lambda e: e.tensor_copy(out=CS8[:, 256:384].rearrange("p (b t) -> p b t", b=16),
                                                    in_=NSN[:, 0:8].unsqueeze(1).to_broadcast([128, 16, 8])), reads=[rNSN], writes=[rCS8])
                S.op("dve", lambda e, j=j: e.tensor_scalar(out=MSKM, in0=MASK[:], scalar1=S5P[:, P_MAG, j:j + 1], scalar2=None, op0=ALU.mult),
                     reads=[rm("MASK"), rS5P], writes=[rMSK])
            for ti, t in enumerate(tiles):
                n = t.n
                if jj == 0:
                    hold[ti] = psum_hold()
                if t.kind == "P":
                    cs, sn, nsn, rcs, rsn, rns = COS[:, 0:n], SIN[:, 0:n], NSN[:, 0:n], rCOS, rSIN, rNSN
                else:
                    cs, sn, nsn, rcs, rsn, rns = CS8[:, 0:128], CS8[:, 128:256], CS8[:, 256:384], rCS8, rCS8, rCS8
                if not bankd:
                    emit_B(j, ti)
                pxr, prr, pxi, pri = bankd.pop((j, ti))
                S.op("dve", lambda e, pxr=pxr, cs=cs, n=n: e.tensor_tensor(out=VR[:, 0:n], in0=pxr[:, 0:n], in1=cs, op=ALU.mult), reads=[prr, rcs], writes=[rVR])
                S.op("dve", lambda e, pxi=pxi, sn=sn, n=n: e.tensor_tensor(out=A[:, 0:n], in0=pxi[:, 0:n], in1=sn, op=ALU.mult), reads=[pri, rsn], writes=[rA])
                S.op("dve", lambda e, pxi=pxi, cs=cs, n=n: e.tensor_tensor(out=VI[:, 0:n], in0=pxi[:, 0:n], in1=cs, op=ALU.mult), reads=[pri, rcs], writes=[rVI])
                S.op("dve", lambda e, pxr=pxr, sn=sn, n=n: e.tensor_tensor(out=B[:, 0:n], in0=pxr[:, 0:n], in1=sn, op=ALU.mult), reads=[prr, rsn], writes=[rB])
                S.op("dve", lambda e, n=n: e.tensor_tensor(out=VR[:, 0:n], in0=VR[:, 0:n], in1=A[:, 0:n], op=ALU.add), reads=[rVR, rA], writes=[rVR])
                S.op("dve", lambda e, n=n: e.tensor_tensor(out=VI[:, 0:n], in0=VI[:, 0:n], in1=B[:, 0:n], op=ALU.subtract), reads=[rVI, rB], writes=[rVI])
                nxt_it = (j, ti + 1) if ti + 1 < len(tiles) else ((j + 1, 0) if j + 1 < 32 else None)
                if nxt_it is not None:
                    emit_B(*nxt_it)
                mag = S5P[:, P_MAG, j:j + 1]
                if t.kind == "P":
                    first = (t.tok0 == 0)
                    if not first:
                        c1, s1 = COS[:, 1:2], SIN[:, 1:2]
                        sr, si = CARRY_S[:, j, 0:1], CARRY_S[:, j, 1:2]
                        S.op("dve", lambda e, si=si, s1=s1: e.tensor_scalar(out=INI[:, 2:3], in0=si, scalar1=s1, scalar2=None, op0=ALU.mult),
                             reads=[rCARS, rSIN], writes=[rINI])
                        S.op("dve", lambda e, sr=sr, c1=c1: e.scalar_tensor_tensor(out=INI[:, 0:1], in0=sr, scalar=c1, in1=INI[:, 2:3], op0=ALU.mult,
                                                                                  op1=ALU.subtract), reads=[rCARS, rCOS, rINI], writes=[rINI])
                        S.op("dve", lambda e, si=si, c1=c1: e.tensor_scalar(out=INI[:, 2:3], in0=si, scalar1=c1, scalar2=None, op0=ALU.mult),
                             reads=[rCARS, rCOS, rINI], writes=[rINI])
                        S.op("dve", lambda e, sr=sr, s1=s1: e.scalar_tensor_tensor(out=INI[:, 1:2], in0=sr, scalar=s1, in1=INI[:, 2:3], op0=ALU.mult,
                                                                                  op1=ALU.add), reads=[rCARS, rSIN, rINI], writes=[rINI])
                        ir, ii = INI[:, 0:1], INI[:, 1:2]
                    else:
                        ir, ii = 0.0, 0.0
                    d0 = mag.to_broadcast([128, n])
                    S.op("dve", lambda e, d0=d0, ir=ir, n=n: e.tensor_tensor_scan(out=RR[:, 0:n], data0=d0, data1=VR[:, 0:n], initial=ir, op0=ALU.mult, op1=ALU.add),
                         reads=[rVR, rS5P, rINI], writes=[rRR])
                    S.op("dve", lambda e, d0=d0, ii=ii, n=n: e.tensor_tensor_scan(out=RI[:, 0:n], data0=d0, data1=VI[:, 0:n], initial=ii, op0=ALU.mult, op1=ALU.add),
                         reads=[rVI, rS5P, rINI], writes=[rRI])
                else:
                    vr3 = VR[:, 0:128].rearrange("p (b t) -> p b t", b=16)
                    vi3 = VI[:, 0:128].rearrange("p (b t) -> p b t", b=16)
                    S.op("dve", lambda e, j=j, vr3=vr3: e.tensor_tensor(out=vr3[:, :, 0], in0=vr3[:, :, 0], in1=AS0[:, 0, j, :], op=ALU.add),
                         reads=[rVR, rAS0], writes=[rVR])
                    S.op("dve", lambda e, j=j, vi3=vi3: e.tensor_tensor(out=vi3[:, :, 0], in0=vi3[:, :, 0], in1=AS0[:, 1, j, :], op=ALU.add),
                         reads=[rVI, rAS0], writes=[rVI])
                    S.op("dve", lambda e: e.tensor_tensor_scan(out=RR[:, 0:128], data0=MSKM, data1=VR[:, 0:128], initial=0.0, op0=ALU.mult, op1=ALU.add),
                         reads=[rVR, rMSK], writes=[rRR])
                    S.op("dve", lambda e: e.tensor_tensor_scan(out=RI[:, 0:128], data0=MSKM, data1=VI[:, 0:128], initial=0.0, op0=ALU.mult, op1=ALU.add),
                         reads=[rVI, rMSK], writes=[rRI])
                S.op("dve", lambda e, cs=cs, n=n: e.tensor_tensor(out=MPb[0][:, 0:n], in0=RR[:, 0:n], in1=cs, op=ALU.mult), reads=[rRR, rcs], writes=[rSRB])
                S.op("dve", lambda e, sn=sn, n=n: e.tensor_tensor(out=MPb[2][:, 0:n], in0=RR[:, 0:n], in1=sn, op=ALU.mult), reads=[rRR, rsn], writes=[rSIB])
                S.op("dve", lambda e, nsn=nsn, n=n: e.tensor_tensor(out=MPb[1][:, 0:n], in0=RI[:, 0:n], in1=nsn, op=ALU.mult), reads=[rRI, rns], writes=[rSRB])
                S.op("dve", lambda e, cs=cs, n=n: e.tensor_tensor(out=MPb[3][:, 0:n], in0=RI[:, 0:n], in1=cs, op=ALU.mult), reads=[rRI, rcs], writes=[rSIB])
                if t.kind == "P":
                    cl, sl = COS[:, n - 1:n], SIN[:, n - 1:n]
                    rl, il = RR[:, n - 1:n], RI[:, n - 1:n]
                    S.op("dve", lambda e, il=il, sl=sl: e.tensor_scalar(out=INI[:, 4:5], in0=il, scalar1=sl, scalar2=None, op0=ALU.mult),
                         reads=[rRI, rSIN], writes=[rINI])
                    S.op("dve", lambda e, j=j, rl=rl, cl=cl: e.scalar_tensor_tensor(out=CARRY_S[:, j, 0:1], in0=rl, scalar=cl, in1=INI[:, 4:5],
                                                                                  op0=ALU.mult, op1=ALU.subtract), reads=[rRR, rCOS, rINI], writes=[rCARS])
                    S.op("dve", lambda e, il=il, cl=cl: e.tensor_scalar(out=INI[:, 4:5], in0=il, scalar1=cl, scalar2=None, op0=ALU.mult),
                         reads=[rRI, rCOS, rINI, rCARS], writes=[rINI])
                    S.op("dve", lambda e, j=j, rl=rl, sl=sl: e.scalar_tensor_tensor(out=CARRY_S[:, j, 1:2], in0=rl, scalar=sl, in1=INI[:, 4:5],
                                                                                  op0=ALU.mult, op1=ALU.add), reads=[rRR, rSIN, rINI], writes=[rCARS])
                else:
                    c7, s7 = COS[:, 7:8], SIN[:, 7:8]
                    rr7 = RR[:, 0:128].rearrange("p (b t) -> p b t", b=16)[:, :, 7]
                    ri7 = RI[:, 0:128].rearrange("p (b t) -> p b t", b=16)[:, :, 7]
                    t16 = MF[:, 6792:6808]
                    S.op("dve", lambda e, ri7=ri7, s7=s7: e.tensor_scalar(out=t16, in0=ri7, scalar1=s7, scalar2=None, op0=ALU.mult),
                         reads=[rRI, rSIN], writes=[rINI])
                    S.op("dve", lambda e, j=j, rr7=rr7, c7=c7: e.scalar_tensor_tensor(out=SOUT[:, 0, j, :], in0=rr7, scalar=c7, in1=t16, op0=ALU.mult,
                                                                                    op1=ALU.subtract), reads=[rRR, rCOS, rINI], writes=[rSOUT])
                    S.op("dve", lambda e, ri7=ri7, c7=c7: e.tensor_scalar(out=t16, in0=ri7, scalar1=c7, scalar2=None, op0=ALU.mult),
                         reads=[rRI, rCOS, rINI, rSOUT], writes=[rINI])
                    S.op("dve", lambda e, j=j, rr7=rr7, s7=s7: e.scalar_tensor_tensor(out=SOUT[:, 1, j, :], in0=rr7, scalar=s7, in1=t16, op0=ALU.mult,
                                                                                    op1=ALU.add), reads=[rRR, rSIN, rINI], writes=[rSOUT])
                hb = hold[ti]
                for k4, (ri_, mk, rk) in enumerate(((0, MPb[0], rSRB), (1, MPb[2], rSIB), (0, MPb[1], rSRB), (1, MPb[3], rSIB))):
                    S.op("pe", lambda e, hb=hb, j=j, jj=jj, n=n, ri_=ri_, mk=mk, k4=k4: e.matmul(
                        PS[hb][32 * jj:32 * jj + 32, 0:n], lhsT=LC[:, j, ri_, :], rhs=mk[:, 0:n], start=(k4 == 0), stop=(k4 == 3),
                        tile_position=(0, 32 * jj)), reads=[rLC, rk], writes=[PSR[hb]])
                if jj == 3:
                    tb, tr, _ = tmp()
                    t2, tr2, _ = tmp()
                    S.op("dve", lambda e, hb=hb, q=q, t=t, tb=tb: e.scalar_tensor_tensor(out=tb[:, 0:t.n], in0=H[:, q, t.cols], scalar=cvs(ROW_BD, q),
                                                                                       in1=PS[hb][:, 0:t.n], op0=ALU.mult, op1=ALU.add),
                         reads=[rHt(t), rCV, PSR[hb]], writes=[tr])
                    S.op("dve", lambda e, t=t, tb=tb, t2=t2: e.tensor_tensor(out=t2[:, 0:t.n], in0=tb[:, 0:t.n], in1=tb[:, 0:t.n], op=ALU.mult), reads=[tr], writes=[tr2])
                    S.op("dve", lambda e, t=t, t2=t2: e.tensor_scalar(out=t2[:, 0:t.n], in0=t2[:, 0:t.n], scalar1=0.044715, scalar2=1.0, op0=ALU.mult, op1=ALU.add),
                         reads=[tr2], writes=[tr2])
                    S.op("dve", lambda e, t=t, tb=tb, t2=t2: e.tensor_tensor(out=t2[:, 0:t.n], in0=t2[:, 0:t.n], in1=tb[:, 0:t.n], op=ALU.mult), reads=[tr, tr2], writes=[tr2])
                    S.op("act", lambda e, t=t, t2=t2: e.activation(out=t2[:, 0:t.n], in_=t2[:, 0:t.n], func=AF.Sigmoid, scale=2.0 * math.sqrt(2.0 / math.pi)),
                         reads=[tr2], writes=[tr2])
                    S.op("dve", lambda e, q=q, t=t, tb=tb, t2=t2: e.tensor_tensor(out=YT[:, q, t.cols], in0=tb[:, 0:t.n], in1=t2[:, 0:t.n], op=ALU.mult),
                         reads=[tr, tr2], writes=[rBt(t)])
                    psum_release(hb)
        if has_s:
            for ri, dst in enumerate((nres, nims)):
                for g4 in range(8):
                    ps, pr = psum()
                    for jx in range(4):
                        j = g4 * 4 + jx
                        S.op("pe", lambda e, j=j, jx=jx, ri=ri, ps=ps: e.transpose(out=ps[0:16, jx * 128:(jx + 1) * 128], in_=SOUT[:, ri, j, :], identity=IDF[:]),
                             reads=[rSOUT, rCONST], writes=[pr])
                    tb, tr, tsem = tmp()
                    any_copy(tb[0:16, :], ps[0:16, :], [pr], [tr])
                    S.op("sp", lambda e, dst=dst, g4=g4, tb=tb: e.dma_start(out=dst[:, g4 * 512:(g4 + 1) * 512], in_=tb[0:16, :]), reads=[tr], dma=tsem)
        if gi == 1:
            for ri, dst in enumerate((nrep, nimp)):
                ps, pr = psum()
                S.op("pe", lambda e, ri=ri, ps=ps: e.transpose(out=ps[0:32, 0:128], in_=CARRY_S[:, :, ri], identity=IDF[:]), reads=[rCARS, rCONST], writes=[pr])
                tb, tr, tsem = tmp()
                any_copy(tb[0:32, 0:128], ps[0:32, 0:128], [pr], [tr])
                S.op("sp", lambda e, dst=dst, tb=tb: e.dma_start(out=dst.rearrange("o (j p) -> (o j) p", p=128), in_=tb[0:32, 0:128]), reads=[tr], dma=tsem)
        join_M()

        def wparts(blk):
            return [(b_w_glu[:, blk * 256:(blk + 1) * 256], 0, 256), (b_w_glu[:, D + blk * 256:D + (blk + 1) * 256], 256, 256)]

        def epi(blk, ji, t, pss, prs):
            c = blk * 2 + ji
            n = t.n
            tb, tr, _ = tmp()
            S.op("act", lambda e: e.activation(out=tb[:, 0:n], in_=pss[1][:, 0:n], func=AF.Sigmoid, bias=cvs(ROW_BGLU + 1, c)),
                 reads=[prs[1], rCV], writes=[tr])
            S.op("dve", lambda e: e.scalar_tensor_tensor(out=M[:, c, t.mcols], in0=pss[0][:, 0:n], scalar=cvs(ROW_BGLU, c), in1=tb[:, 0:n],
                                                       op0=ALU.add, op1=ALU.mult), reads=[prs[0], tr, rCV], writes=[rMt(t)])
        S.strict = strict_prev
        proj(tiles, wparts, 8, lambda k, t: YT[:, k, t.cols], lambda t: [rBt(t)], lambda blk: [[0, 256], [128, 384]], epi, 4)
        norm_boundary(tiles, row_norm(l, 1), nxt)

    MARKS = cfg.setdefault("_marks", [])
    setup_consts()
    setup_eps()
    subs = cfg.get("subs", ("mix", "attn", "ffn"))
    if any(l % 3 == 1 for l in layers) and "mix" in subs:
        s5_setup()
        join_M()
    if "attn" in subs:
        setup_memT()
        join_M()
    for gi in groups_sel:
        tiles = GROUPS[gi]
        if cfg.get("no_sample"):
            tiles = [t for t in tiles if t.kind == "P"]
        load_x(tiles)
        join_M()
        seq = []
        for l in layers:
            if "mix" in subs:
                seq.append(([mixer0, mixer1, mixer2][l % 3], l, row_norm(l, 0), "mix"))
            if "attn" in subs:
                seq.append((attention, l, row_norm(l, 2), "attn"))
            if "ffn" in subs:
                seq.append((ffn, l, row_norm(l, 4), "ffn"))
        for si, (fn, l, grow, kind) in enumerate(seq):
            nxt = seq[si + 1][2] if si + 1 < len(seq) else None
            dl = (si + 1 < len(seq)) and seq[si + 1][3] in ("attn", "ffn") and not cfg.get("no_defer")
            MARKS.append((f"g{gi} L{l} {kind}", len(S.ops["pe"]), len(S.ops["dve"]), len(S.ops["act"]), len(S.ops["pool"])))
            fn(l, gi, tiles, si > 0, nxt, dl)
        flush_deferred()
        MARKS.append((f"g{gi} store", len(S.ops["pe"]), len(S.ops["dve"]), len(S.ops["act"]), len(S.ops["pool"])))
        join_M()
        store_y(tiles)
        join_M()

    blk_ctx = es.enter_context(nc.Block())
    S.emit(blk_ctx)
    es.close()
    return nc


def make_vecs(inp):
    rows = []
    rows.append(inp["norm_g"].reshape(24, D))
    rows.append(inp["a_b_pw1"].reshape(4, D))
    rows.append(inp["a_w_dw"].reshape(62, D))
    rows.append(inp["a_b_dw"].reshape(2, D))
    rows.append(inp["a_ln_g"].reshape(2, D))
    rows.append(inp["a_ln_b"].reshape(2, D))
    rows.append(inp["a_b_pw2"].reshape(2, D))
    rows.append(inp["b_D"].reshape(1, D))
    rows.append(inp["b_b_glu"].reshape(2, D))
    rows.append(inp["c_w_conv"].reshape(3, D))
    v = np.concatenate(rows, axis=0).astype(np.float32)
    out = np.zeros((128, D), np.float32)
    out[:v.shape[0]] = v
    return out


def make_in_maps(inp, cores):
    vecs = make_vecs(inp)
    shared = dict(
        vecs=vecs,
        a_w_pw1=inp["a_w_pw1"], a_w_pw2=inp["a_w_pw2"],
        b_lam_re=inp["b_lam_re"][0], b_lam_im=inp["b_lam_im"][0], b_log_dt=inp["b_log_dt"][0],
        b_B_re=inp["b_B_re"][0], b_B_im=inp["b_B_im"][0], b_C_re=inp["b_C_re"][0], b_C_im=inp["b_C_im"][0],
        b_w_glu=inp["b_w_glu"][0], c_w_in=inp["c_w_in"][0], c_w_out=inp["c_w_out"][0],
        x_w_q=inp["x_w_q"], x_w_k=inp["x_w_k"], x_w_v=inp["x_w_v"], x_w_o=inp["x_w_o"],
        f_w_gu=inp["f_w_gu"], f_w_down=inp["f_w_down"],
    )
    shared = {k: np.ascontiguousarray(v, dtype=np.float32) for k, v in shared.items()}
    maps = []
    for c in cores:
        b0, b1 = 16 * c, 16 * c + 16
        m = dict(shared)
        m["xp"] = np.ascontiguousarray(inp["x_prompt"][c])
        m["xs"] = np.ascontiguousarray(inp["x_sample"][b0:b1].reshape(128, D))
        m["st_conv"] = np.ascontiguousarray(inp["state_conv_a"][:, b0:b1])
        m["st_re"] = np.ascontiguousarray(inp["state_ssm_re"][0, b0:b1].reshape(16, 4096))
        m["st_im"] = np.ascontiguousarray(inp["state_ssm_im"][0, b0:b1].reshape(16, 4096))
        m["st_sc"] = np.ascontiguousarray(inp["state_sconv"][0, b0:b1])
        m["ck"] = np.ascontiguousarray(inp["cache_mem_k"][:, b0:b1].reshape(4, 16, NMEM, D))
        m["cv"] = np.ascontiguousarray(inp["cache_mem_v"][:, b0:b1].reshape(4, 16, NMEM, D))
        m["memp"] = np.ascontiguousarray(inp["mem_prompt"][c])
        maps.append(m)
    return maps


def kernel(**inp):
    inp = {k: np.asarray(v) for k, v in inp.items()}
    nc = build_program({})
    maps = make_in_maps(inp, list(range(8)))
    res = run_bass_kernel_spmd(nc, maps, core_ids=list(range(8)))
    r = res.results
    f32 = np.float32
    y_prompt = np.stack([r[c]["yp"] for c in range(8)]).astype(f32)
    y_sample = np.concatenate([r[c]["ys"].reshape(16, 8, D) for c in range(8)]).astype(f32)
    nk = np.stack([r[c]["nk"] for c in range(8)], axis=1).reshape(4, 8, NMEM, 4, 256).astype(f32)
    nv = np.stack([r[c]["nv"] for c in range(8)], axis=1).reshape(4, 8, NMEM, 4, 256).astype(f32)
    ncap = np.stack([r[c]["ncap"] for c in range(8)], axis=1).astype(f32)
    ncas = np.concatenate([r[c]["ncas"] for c in range(8)], axis=1).astype(f32)
    nrep = np.stack([r[c]["nrep"].reshape(1, 64, 64) for c in range(8)], axis=1).astype(f32)
    nimp = np.stack([r[c]["nimp"].reshape(1, 64, 64) for c in range(8)], axis=1).astype(f32)
    nres = np.concatenate([r[c]["nres"].reshape(1, 16, 64, 64) for c in range(8)], axis=1).astype(f32)
    nims = np.concatenate([r[c]["nims"].reshape(1, 16, 64, 64) for c in range(8)], axis=1).astype(f32)
    nscp = np.stack([r[c]["nscp"].reshape(1, 2, D) for c in range(8)], axis=1).astype(f32)
    nscs = np.concatenate([r[c]["nscs"].reshape(1, 16, 2, D) for c in range(8)], axis=1).astype(f32)
    return (y_prompt, y_sample, nk, nv, ncap, ncas, nrep, nimp, nres, nims, nscp, nscs)
```

```python
import math
from contextlib import ExitStack

import numpy as np
import concourse.bass as bass
import concourse.mybir as mybir
from concourse.bass_utils import run_bass_kernel_spmd

F32 = mybir.dt.float32
BF16 = mybir.dt.bfloat16
I32 = mybir.dt.int32
ALU = mybir.AluOpType
AF = mybir.ActivationFunctionType

D = 1024
NCH = 8
DEPTH = 4
DFF = 2816
NMEM = 256
GW = 1152
RMS_EPS = 1e-6
LN_EPS = 1e-5
TWO_PI = 2.0 * math.pi

ENGS = ("pe", "act", "dve", "pool", "sp")


class Res:
    __slots__ = ("name", "w", "r")

    def __init__(self, name):
        self.name = name
        self.w = None
        self.r = {}


class DSem:
    def __init__(self, sem):
        self.sem = sem
        self.count = 0


class _Rec:
    def __init__(self):
        self.call = None

    def __getattr__(self, name):
        def f(*a, **k):
            self.call = (name, a, k)
            return self
        return f


class Sched:
    def __init__(self, nc, es):
        self.nc = nc
        self.es = es
        self.ops = {e: [] for e in ENGS}
        self.esem = {e: es.enter_context(nc.semaphore("sem_" + e)) for e in ("pe", "act", "dve", "pool")}
        self.dsems = []
        self.strict = False
        self.clock = 0
        self.pending = []
        self.pend_clock = 0
        self.deferring = False

    def dsem(self, name):
        d = DSem(self.es.enter_context(self.nc.semaphore(name)))
        self.dsems.append(d)
        return d

    def op(self, eng, fn, reads=(), writes=(), dma=None, par=False):
        deps = []
        for r in reads:
            if r.w is not None:
                deps.append(("raw", r.w))
        for w in writes:
            if w.w is not None:
                if not (par and dma is not None and w.w[0] == "d" and w.w[1] is dma):
                    deps.append(("waw", w.w))
            for t in w.r.values():
                deps.append(("war", t))
        self.clock += 1
        rc = _Rec()
        fn(rc)
        rec = dict(call=rc.call, deps=None, inc=False, dma=dma, clock=self.clock, eng=eng)
        if dma is None:
            tok = ("e", eng, rec)
        else:
            dma.count += 16
            tok = ("d", dma, dma.count)
        keep = []
        for kind, t in deps:
            if t[0] == "e" and t[1] == eng:
                if eng == "pe" or (kind != "raw" and not (self.strict or eng in ("pool", "act"))):
                    continue
            keep.append(t)
            if t[0] == "e":
                t[2]["inc"] = True
        rec["deps"] = keep
        if eng == "pe" and self.deferring:
            self.pending.append(rec)
        else:
            if eng == "pe" and self.pending and any(t[0] == "e" and t[2]["clock"] >= self.pend_clock for t in keep):
                self.pe_flush()
            self.ops[eng].append(rec)
        for r in reads:
            key = eng if dma is None else ("d", id(dma))
            r.r[key] = tok
        for w in writes:
            w.w = tok
            w.r = {}
        return tok

    def pe_defer_begin(self):
        if not self.pending:
            self.pend_clock = self.clock + 1
        self.deferring = True

    def pe_defer_end(self):
        self.deferring = False

    def pe_flush(self):
        self.ops["pe"].extend(self.pending)
        self.pending = []

    def emit(self, block):
        self.pe_flush()
        for e in ("pe", "act", "dve", "pool"):
            cnt = 0
            for rec in self.ops[e]:
                if rec["inc"] and rec["dma"] is None:
                    cnt += 1
                    rec["idx"] = cnt
        final = [(d.sem, d.count) for d in self.dsems if d.count > 0]

        def run(e, eng, tail=False):
            seen = {}
            for rec in self.ops[e]:
                need = {}
                for t in rec["deps"]:
                    if t[0] == "e":
                        sem = self.esem[t[1]]
                        val = t[2]["idx"]
                    else:
                        sem = t[1].sem
                        val = t[2]
                    k = id(sem)
                    if k not in need or need[k][1] < val:
                        need[k] = (sem, val)
                for k, (sem, val) in need.items():
                    if seen.get(k, 0) < val:
                        eng.wait_ge(sem, val)
                        seen[k] = val
                name, a, k = rec["call"]
                ins = getattr(eng, name)(*a, **k)
                if rec["dma"] is not None:
                    ins.then_inc(rec["dma"].sem, 16)
                elif rec["inc"]:
                    ins.then_inc(self.esem[e], 1)
            if tail:
                for sem, val in final:
                    eng.wait_ge(sem, val)

        @block.tensor
        def _(eng):
            run("pe", eng)

        @block.scalar
        def _(eng):
            run("act", eng)

        @block.vector
        def _(eng):
            run("dve", eng)

        @block.gpsimd
        def _(eng):
            run("pool", eng)

        @block.sync
        def _(eng):
            run("sp", eng, tail=True)


def build_program(cfg):
    nc = bass.Bass("TRN2", target_bir_lowering=False)
    es = ExitStack()
    S = Sched(nc, es)
    S.strict = bool(cfg.get("strict"))
    dbg = cfg.get("dbg")
    layers = cfg.get("layers", list(range(DEPTH)))
    groups_sel = cfg.get("groups", [0, 1])

    def din(name, shape):
        return nc.dram_tensor(name, list(shape), F32, kind="ExternalInput").ap()

    def dout(name, shape):
        return nc.dram_tensor(name, list(shape), F32, kind="ExternalOutput").ap()

    xp = din("xp", [2048, D])
    xs = din("xs", [128, D])
    st_conv = din("st_conv", [2, 16, 30, D])
    st_re = din("st_re", [16, 4096])
    st_im = din("st_im", [16, 4096])
    st_sc = din("st_sc", [16, 2, D])
    ck = din("ck", [4, 16, NMEM, D])
    cv = din("cv", [4, 16, NMEM, D])
    memp = din("memp", [NMEM, D])
    vecs = din("vecs", [128, D])
    a_w_pw1 = din("a_w_pw1", [2, D, 2 * D])
    a_w_pw2 = din("a_w_pw2", [2, D, D])
    b_lam_re = din("b_lam_re", [64, 64])
    b_lam_im = din("b_lam_im", [64, 64])
    b_log_dt = din("b_log_dt", [64])
    b_B_re = din("b_B_re", [64, 64, 16])
    b_B_im = din("b_B_im", [64, 64, 16])
    b_C_re = din("b_C_re", [64, 16, 64])
    b_C_im = din("b_C_im", [64, 16, 64])
    b_w_glu = din("b_w_glu", [D, 2 * D])
    c_w_in = din("c_w_in", [D, 3 * D])
    c_w_out = din("c_w_out", [D, D])
    x_w_q = din("x_w_q", [4, D, D])
    x_w_k = din("x_w_k", [4, D, D])
    x_w_v = din("x_w_v", [4, D, D])
    x_w_o = din("x_w_o", [4, D, D])
    f_w_gu = din("f_w_gu", [4, D, 2 * DFF])
    f_w_down = din("f_w_down", [4, DFF, D])

    yp = dout("yp", [2048, D])
    ys = dout("ys", [128, D])
    nk = dout("nk", [4, NMEM, D])
    nv = dout("nv", [4, NMEM, D])
    ncap = dout("ncap", [2, 30, D])
    ncas = dout("ncas", [2, 16, 30, D])
    nrep = dout("nrep", [1, 4096])
    nimp = dout("nimp", [1, 4096])
    nres = dout("nres", [16, 4096])
    nims = dout("nims", [16, 4096])
    nscp = dout("nscp", [2, D])
    nscs = dout("nscs", [16, 2, D])
    if dbg:
        dbg_out = dout("dbg_out", [128, NCH * GW])

    def sb(name, shape, dt):
        return es.enter_context(nc.sbuf_tensor(name, list(shape), dt))

    X = sb("X", [128, NCH, GW], F32)
    H = sb("H", [128, NCH, GW], BF16)
    MW = 2 + GW
    M = sb("M", [128, NCH, MW], F32)
    MF = M[:].rearrange("p c w -> p (c w)")
    BIGN = 13312
    BIG = sb("BIG", [128, BIGN], BF16)
    NSLOT = 3
    WSL = 4096
    WR = [sb(f"WR{i}", [128, WSL], BF16) for i in range(NSLOT)]
    KSB = [sb(f"KS{i}", [128, 2, D], BF16) for i in range(2)]
    KT = sb("KT", [128, NCH, NMEM], BF16)
    NV = cfg.get("nv", 2)
    VV = [sb(f"VV{i}", [128, 2, D], BF16) for i in range(NV)]
    NT = 6
    TT = [sb(f"TT{i}", [128, 512], F32) for i in range(NT)]
    PT = [sb(f"PT{i}", [128, 2, 512], BF16) for i in range(2)]
    CV = sb("CV", [128, NCH, 128], F32)
    IDF = sb("IDF", [128, 128], F32)
    IDB = sb("IDB", [128, 128], BF16)
    ONB = sb("ONB", [128, 128], BF16)
    MEMT = sb("MEMT", [128, NCH, NMEM], BF16)
    GSF = sb("GSF", [128, NCH, 160], F32)
    GPF = sb("GPF", [128, NCH, 32], F32)
    CARRY_G = [sb(f"CARG{i}", [128, NCH, 30], BF16) for i in range(2)]
    CARRY_P = sb("CARP", [128, NCH, 2], F32)
    CARRY_S = sb("CARS", [128, 32, 2], F32)
    LB = sb("LB", [128, 8, 2, 128], BF16)
    LC = sb("LC", [128, 32, 2, 32], BF16)
    S5P = sb("S5P", [128, 12, 32], F32)
    MASK = sb("MASK", [128, 128], F32)
    DUMMY = sb("DUMMY", [128, 4], F32)

    PS = [es.enter_context(nc.psum_tensor(f"ps{i}", [128, 512], F32)) for i in range(8)]
    PSR = [Res(f"ps{i}") for i in range(8)]
    ps_free = list(range(8))
    ps_rr = [0]

    ST_BANKS = [6, 7]
    stb_rr = [0]

    def psum_stat():
        i = ST_BANKS[stb_rr[0] % 2]
        stb_rr[0] += 1
        return PS[i], PSR[i]

    def psum():
        i = ps_free[ps_rr[0] % len(ps_free)]
        ps_rr[0] += 1
        return PS[i], PSR[i]

    def psum_hold():
        i = ps_free[ps_rr[0] % len(ps_free)]
        ps_free.remove(i)
        return i

    def psum_release(i):
        ps_free.append(i)
        ps_free.sort()

    class R:
        pass

    rX = {}
    rH = {}
    rM = {}

    def res(dct, key, name):
        if key not in dct:
            dct[key] = Res(f"{name}{key}")
        return dct[key]

    rMisc = {}

    def rm(name):
        return res(rMisc, name, "")

    wr_res = [Res(f"wr{i}") for i in range(NSLOT)]
    wr_sem = [S.dsem(f"wrs{i}") for i in range(NSLOT)]
    wr_rr = [0]

    def wslot():
        i = wr_rr[0] % NSLOT
        wr_rr[0] += 1
        return WR[i], wr_res[i], wr_sem[i]

    class Tile:
        def __init__(self, kind, col0, n, tok0, gi, ti):
            self.kind, self.col0, self.n, self.tok0, self.gi, self.ti = kind, col0, n, tok0, gi, ti
            self.cols = slice(col0, col0 + n)
            self.mcols = slice(2 + col0, 2 + col0 + n)

    GROUPS = [
        [Tile("P", 0, 512, 0, 0, 0), Tile("P", 512, 512, 512, 0, 1), Tile("S", 1024, 128, 0, 0, 2)],
        [Tile("P", 0, 512, 1024, 1, 0), Tile("P", 512, 512, 1536, 1, 1)],
    ]

    def rXt(t):
        return res(rX, t.ti, "X")

    def rHt(t):
        return res(rH, t.ti, "H")

    def rMt(t):
        return res(rM, t.ti, "M")

    def cvs(row, c):
        return CV[:, c, row:row + 1]

    rCV = rm("CV")
    rCONST = rm("CONST")

    def row_norm(l, i):
        return l * 6 + i
    ROW_BPW1 = 24
    ROW_WDW = 28
    ROW_BDW = 90
    ROW_LNG = 92
    ROW_LNB = 94
    ROW_BPW2 = 96
    ROW_BD = 98
    ROW_BGLU = 99
    ROW_WCONV = 101

    def setup_consts():
        S.op("pool", lambda e: e.memset(IDF[:], 0.0), writes=[rCONST])
        S.op("pool", lambda e: e.affine_select(out=IDF[:], in_=IDF[:], pattern=[[-1, 128]], compare_op=ALU.not_equal,
                                               fill=1.0, base=0, channel_multiplier=1), reads=[rCONST], writes=[rCONST])
        S.op("dve", lambda e: e.tensor_copy(out=IDB[:], in_=IDF[:]), reads=[rCONST], writes=[rm("IDB")])
        S.op("dve", lambda e: e.memset(ONB[:], 1.0), writes=[rm("ONB")])
        stg = M[:, 0:1, 0:D]
        sem = S.dsem("vecs")
        S.op("sp", lambda e: e.dma_start(out=M[:, 0, 0:D], in_=vecs), writes=[rm("Mstage")], dma=sem)
        for c in range(NCH):
            ps, pr = psum()
            S.op("pe", lambda e, c=c, ps=ps: e.transpose(out=ps[:, 0:128], in_=M[:, 0, c * 128:(c + 1) * 128], identity=IDF[:]),
                 reads=[rm("Mstage"), rCONST], writes=[pr])
            S.op("act", lambda e, c=c, ps=ps: e.copy(out=CV[:, c, :], in_=ps[:, 0:128]), reads=[pr], writes=[rCV])

    def load_x(tiles):
        for t in tiles:
            src = xp[t.tok0:t.tok0 + t.n, :] if t.kind == "P" else xs
            nb = t.n // 128
            stg = M[:, 0:4, 0:D]
            sem = S.dsem(f"ldx{t.gi}_{t.ti}")
            S.op("sp", lambda e, src=src, nb=nb: e.dma_start(out=M[:, 0:nb, 0:D], in_=src.rearrange("(b p) d -> p b d", p=128)),
                 writes=[rm("Mstage")], dma=sem)
            for c in range(NCH):
                ps, pr = psum()
                for b in range(nb):
                    S.op("pe", lambda e, c=c, b=b, ps=ps: e.transpose(out=ps[:, b * 128:(b + 1) * 128],
                                                                    in_=M[:, b, c * 128:(c + 1) * 128], identity=IDF[:]),
                         reads=[rm("Mstage"), rCONST], writes=[pr])
                S.op("act" if c % 2 else "dve",
                     (lambda e, c=c, ps=ps, t=t: e.copy(out=X[:, c, t.cols], in_=ps[:, 0:t.n])) if c % 2 else
                     (lambda e, c=c, ps=ps, t=t: e.tensor_copy(out=X[:, c, t.cols], in_=ps[:, 0:t.n])),
                     reads=[pr], writes=[rXt(t)])

    st_sems = [S.dsem("st0"), S.dsem("st1")]
    st_res = [Res("ost0"), Res("ost1")]
    st_rr = [0]

    def store_y(tiles):
        for t in tiles:
            dst = yp[t.tok0:t.tok0 + t.n, :] if t.kind == "P" else ys
            for b in range(t.n // 128):
                i = st_rr[0] % 2
                st_rr[0] += 1
                ost = M[:, i, 0:D]
                for hh in range(2):
                    ps, pr = psum()
                    for cc in range(4):
                        c = hh * 4 + cc
                        S.op("pe", lambda e, c=c, cc=cc, b=b, ps=ps, t=t: e.transpose(
                            out=ps[:, cc * 128:(cc + 1) * 128], in_=X[:, c, t.col0 + b * 128:t.col0 + (b + 1) * 128], identity=IDF[:]),
                            reads=[rXt(t), rCONST], writes=[pr])
                    S.op("act" if hh else "dve",
                         (lambda e, hh=hh, ps=ps, i=i: e.copy(out=M[:, i, hh * 512:(hh + 1) * 512], in_=ps[:, :])) if hh else
                         (lambda e, hh=hh, ps=ps, i=i: e.tensor_copy(out=M[:, i, hh * 512:(hh + 1) * 512], in_=ps[:, :])),
                         reads=[pr], writes=[st_res[i]])
                S.op("sp", lambda e, i=i, dst=dst, b=b: e.dma_start(out=dst[b * 128:(b + 1) * 128, :], in_=M[:, i, 0:D]),
                     reads=[st_res[i]], dma=st_sems[i])

    tt_res = [Res(f"tt{i}") for i in range(NT)]
    tt_sem = [S.dsem(f"tts{i}") for i in range(NT)]
    tt_rr = [0]

    def tmp():
        i = tt_rr[0] % NT
        tt_rr[0] += 1
        return TT[i], tt_res[i], tt_sem[i]

    def rstd_from(src_ap_fn, src_res, n, eps):
        slot, sres, _ = wslot()
        SQv = slot[:, 0:NCH * 512].rearrange("p (c w) -> p c w", c=NCH)
        S.op("act", lambda e: e.activation(out=SQv[:, :, 0:n], in_=src_ap_fn(), func=AF.Square), reads=[src_res], writes=[sres])
        ps, pr = psum()
        for c in range(NCH):
            S.op("pe", lambda e, c=c, ps=ps: e.matmul(ps[:, 0:n], lhsT=ONB[:], rhs=SQv[:, c, 0:n], start=(c == 0), stop=(c == NCH - 1)),
                 reads=[sres, rm("ONB")], writes=[pr])
        rb, rr_, _ = tmp()
        S.op("act", lambda e, ps=ps: e.activation(out=rb[:, 0:n], in_=ps[:, 0:n], func=AF.Sqrt, scale=1.0 / D, bias=EPSB[eps][:, 0:1]),
             reads=[pr, rCONST2], writes=[rr_])
        S.op("dve", lambda e: e.reciprocal(out=rb[:, 0:n], in_=rb[:, 0:n]), reads=[rr_], writes=[rr_])
        return rb, rr_

    EPSB = {RMS_EPS: sb("EPS1", [128, 1], F32), LN_EPS: sb("EPS2", [128, 1], F32)}
    HPI = sb("HPI", [128, 1], F32)
    rCONST2 = rm("CONST2")

    def setup_eps():
        S.op("dve", lambda e: e.memset(EPSB[RMS_EPS][:], RMS_EPS), writes=[rCONST2])
        S.op("dve", lambda e: e.memset(EPSB[LN_EPS][:], LN_EPS), writes=[rCONST2])
        S.op("dve", lambda e: e.memset(HPI[:], math.pi / 2), writes=[rCONST2])

    def pre_norm(t, grow):
        rb, rr_ = rstd_from(lambda: X[:, :, t.cols], rXt(t), t.n, RMS_EPS)
        for c in range(NCH):
            S.op("dve", lambda e, c=c: e.scalar_tensor_tensor(out=H[:, c, t.cols], in0=X[:, c, t.cols], scalar=cvs(grow, c),
                                                             in1=rb[:, 0:t.n], op0=ALU.mult, op1=ALU.mult),
                 reads=[rXt(t), rr_, rCV], writes=[rHt(t)])

    def norm_boundary(tiles, grow_post, grow_next):
        st = {}
        for t in tiles:
            S.op("act", lambda e, t=t: e.activation(out=H[:, :, t.cols], in_=M[:, :, t.mcols], func=AF.Square), reads=[rMt(t)], writes=[rHt(t)])
        for t in tiles:
            ps, pr = psum()
            for c in range(NCH):
                S.op("pe", lambda e, c=c, ps=ps, t=t: e.matmul(ps[:, 0:t.n], lhsT=ONB[:], rhs=H[:, c, t.cols], start=(c == 0), stop=(c == NCH - 1)),
                     reads=[rHt(t), rm("ONB")], writes=[pr])
            st[t.ti] = (ps, pr)
        rsb = {}
        for t in tiles:
            ps, pr = st[t.ti]
            rb, rr_, _ = tmp()
            rsb[t.ti] = (rb, rr_)
            S.op("act", lambda e, ps=ps, rb=rb, t=t: e.activation(out=rb[:, 0:t.n], in_=ps[:, 0:t.n], func=AF.Sqrt, scale=1.0 / D, bias=EPSB[RMS_EPS][:, 0:1]),
                 reads=[pr, rCONST2], writes=[rr_])

        def post_dve(t):
            rb, rr_ = rsb[t.ti]
            S.op("dve", lambda e: e.reciprocal(out=rb[:, 0:t.n], in_=rb[:, 0:t.n]), reads=[rr_], writes=[rr_])
            for c in range(NCH):
                S.op("dve", lambda e, c=c: e.scalar_tensor_tensor(out=M[:, c, t.mcols], in0=M[:, c, t.mcols], scalar=cvs(grow_post, c),
                                                                 in1=rb[:, 0:t.n], op0=ALU.mult, op1=ALU.mult),
                     reads=[rMt(t), rr_, rCV], writes=[rMt(t)])
            S.op("dve", lambda e: e.tensor_tensor(out=X[:, :, t.cols], in0=X[:, :, t.cols], in1=M[:, :, t.mcols], op=ALU.add),
                 reads=[rMt(t), rXt(t)], writes=[rXt(t)])
            if grow_next is None:
                return
            S.op("act", lambda e: e.activation(out=H[:, :, t.cols], in_=X[:, :, t.cols], func=AF.Square), reads=[rXt(t)], writes=[rHt(t)])
            ps, pr = psum()
            for c in range(NCH):
                S.op("pe", lambda e, c=c, ps=ps: e.matmul(ps[:, 0:t.n], lhsT=ONB[:], rhs=H[:, c, t.cols], start=(c == 0), stop=(c == NCH - 1)),
                     reads=[rHt(t), rm("ONB")], writes=[pr])
            rb2, rr2, _ = tmp()
            rsb[("n", t.ti)] = (rb2, rr2)
            S.op("act", lambda e: e.activation(out=rb2[:, 0:t.n], in_=ps[:, 0:t.n], func=AF.Sqrt, scale=1.0 / D, bias=EPSB[RMS_EPS][:, 0:1]),
                 reads=[pr, rCONST2], writes=[rr2])

        def pre_dve(t):
            rb2, rr2 = rsb[("n", t.ti)]
            S.op("dve", lambda e: e.reciprocal(out=rb2[:, 0:t.n], in_=rb2[:, 0:t.n]), reads=[rr2], writes=[rr2])
            for c in range(NCH):
                S.op("dve", lambda e, c=c: e.scalar_tensor_tensor(out=H[:, c, t.cols], in0=X[:, c, t.cols], scalar=cvs(grow_next, c),
                                                                 in1=rb2[:, 0:t.n], op0=ALU.mult, op1=ALU.mult),
                     reads=[rXt(t), rr2, rCV], writes=[rHt(t)])

        prev = None
        for t in tiles:
            post_dve(t)
            if prev is not None and grow_next is not None:
                pre_dve(prev)
            prev = t
        if grow_next is not None:
            pre_dve(prev)

    deferred = []

    def flush_deferred(t=None):
        while deferred and (t is None or deferred[0][0] is t):
            tt_, fn_ = deferred.pop(0)
            fn_(tt_)

    def proj(tiles, wparts, KC, rhs_fn, rhs_res_fn, jobs, epi, nblk, tile_outer=False, after_tile=None, defer_last=False):
        def load(blk):
            parts = wparts(blk)
            wtot = sum(p[2] for p in parts)
            slot, sres, ssem = wslot()
            sv = slot[:, 0:KC * wtot].rearrange("p (k w) -> p k w", k=KC)
            for pi, (src, off, w) in enumerate(parts):
                S.op("pool", lambda e, src=src, off=off, w=w, sv=sv: e.dma_start(
                    out=sv[:, :, off:off + w], in_=src.rearrange("(k p) w -> p k w", p=128)),
                    writes=[sres], dma=ssem, par=(pi > 0))
            return sv, sres

        def run(blk, sv, sres, t):
            for ji, job in enumerate(jobs(blk)):
                banks = [psum() for _ in job]
                for (ps, pr), off in zip(banks, job):
                    for k in range(KC):
                        S.op("pe", lambda e, k=k, off=off, ps=ps, t=t, sv=sv: e.matmul(
                            ps[:, 0:t.n], lhsT=sv[:, k, off:off + 128], rhs=rhs_fn(k, t), start=(k == 0), stop=(k == KC - 1)),
                            reads=[sres] + rhs_res_fn(t), writes=[pr])
                epi(blk, ji, t, [b[0] for b in banks], [b[1] for b in banks])

        if tile_outer:
            assert nblk <= NSLOT - 1
            loaded = [load(blk) for blk in range(nblk)]
            prev = None
            for t in tiles:
                flush_deferred(t)
                for blk in range(nblk):
                    run(blk, loaded[blk][0], loaded[blk][1], t)
                if prev is not None and after_tile is not None:
                    after_tile(prev)
                prev = t
            if after_tile is not None:
                if defer_last:
                    deferred.append((prev, after_tile))
                else:
                    after_tile(prev)
        else:
            for blk in range(nblk):
                sv, sres = load(blk)
                for t in tiles:
                    flush_deferred(t)
                    run(blk, sv, sres, t)
            if after_tile is not None:
                for t in tiles:
                    after_tile(t)

    S5N = 10
    s5_res = [Res(f"s5scr{i}") for i in range(S5N)]

    def all_M_res():
        extra = [rm(x) for x in ("SRB", "SIB", "MSKM", "INI", "NSN", "COS1", "SIN1", "NSN1")]
        return [rm("Mstage")] + [res(rM, k, "M") for k in range(3)] + s5_res + [rm("Mhist")] + extra

    def join_M():
        S.op("dve", lambda e: e.memset(DUMMY[:, 0:1], 0.0), writes=all_M_res())

    def join_BIG():
        rl = [res(rMisc, ("BIG", k), "BIG") for k in range(3)] + [res(rMisc, ("Q", k, h), "Q") for k in range(3) for h in range(4)]
        rl += [res(rMisc, ("G", k), "G") for k in range(3)] + [rm("GPhist"), rm("GShist"), rm("AS0")]
        rl += [rm("GSF"), rm("PSH"), rm("SOUT")]
        S.op("dve", lambda e: e.memset(DUMMY[:, 2:3], 0.0), writes=rl)

    out_sems = {}

    def osem(name):
        if name not in out_sems:
            out_sems[name] = S.dsem("o_" + name)
        return out_sems[name]

    def act_copy(out, in_, reads, writes, scale=None):
        if scale is None:
            S.op("act", lambda e: e.copy(out=out, in_=in_), reads=reads, writes=writes)
        else:
            S.op("act", lambda e: e.mul(out=out, in_=in_, mul=scale), reads=reads, writes=writes)

    def dve_copy(out, in_, reads, writes):
        S.op("dve", lambda e: e.tensor_copy(out=out, in_=in_), reads=reads, writes=writes)

    cp_rr = [0]

    def any_copy(out, in_, reads, writes):
        act_copy(out, in_, reads, writes)

    def transpose_out(src_fn, nrows, nchunks_src_res, dst_ap, name):
        for hh in range(2):
            ps, pr = psum()
            for cc in range(4):
                c = hh * 4 + cc
                S.op("pe", lambda e, c=c, cc=cc, ps=ps: e.transpose(out=ps[0:nrows, cc * 128:(cc + 1) * 128], in_=src_fn(c),
                                                                  identity=IDF[:]),
                     reads=nchunks_src_res + [rCONST], writes=[pr])
            tb, tr, tsem = tmp()
            any_copy(tb[0:nrows, :], ps[0:nrows, :], [pr], [tr])
            S.op("sp", lambda e, tb=tb, hh=hh: e.dma_start(out=dst_ap[:, hh * 512:(hh + 1) * 512], in_=tb[0:nrows, :]),
                 reads=[tr], dma=tsem)

    def rBt(t):
        return res(rMisc, ("BIG", t.ti), "BIG")

    def ffn(l, gi, tiles, pre_done, nxt, dl=False):
        if not pre_done:
            for t in tiles:
                pre_norm(t, row_norm(l, 4))
        join_BIG()
        AT = BIG[:, 0:11 * GW].rearrange("p (f w) -> p f w", f=11)
        for half in range(2):
            def wparts(blk, half=half):
                f0 = half * 11 + blk * 2
                nf = min(2, 11 - blk * 2)
                w = nf * 128
                return [(f_w_gu[l][:, f0 * 128:f0 * 128 + w], 0, w), (f_w_gu[l][:, DFF + f0 * 128:DFF + f0 * 128 + w], w, w)]

            def jobs(blk):
                nf = min(2, 11 - blk * 2)
                return [[i * 128, nf * 128 + i * 128] for i in range(nf)]

            def epi(blk, ji, t, pss, prs):
                fi = blk * 2 + ji
                tb, tr, _ = tmp()
                S.op("act", lambda e: e.activation(out=tb[:, 0:t.n], in_=pss[0][:, 0:t.n], func=AF.Silu), reads=[prs[0]], writes=[tr])
                S.op("dve", lambda e: e.tensor_tensor(out=AT[:, fi, t.cols], in0=tb[:, 0:t.n], in1=pss[1][:, 0:t.n], op=ALU.mult),
                     reads=[tr, prs[1]], writes=[rBt(t)])
            proj(tiles, wparts, 8, lambda k, t: H[:, k, t.cols], lambda t: [rHt(t)], jobs, epi, 6)

            def wparts2(blk, half=half):
                return [(f_w_down[l][half * 1408:(half + 1) * 1408, blk * 256:(blk + 1) * 256], 0, 256)]

            def epi2(blk, ji, t, pss, prs, half=half):
                c = blk * 2 + ji
                if half == 0:
                    act_copy(M[:, c, t.mcols], pss[0][:, 0:t.n], [prs[0]], [rMt(t)])
                else:
                    S.op("dve", lambda e: e.tensor_tensor(out=M[:, c, t.mcols], in0=M[:, c, t.mcols], in1=pss[0][:, 0:t.n], op=ALU.add),
                         reads=[prs[0], rMt(t)], writes=[rMt(t)])
            proj(tiles, wparts2, 11, lambda k, t: AT[:, k, t.cols], lambda t: [rBt(t)], lambda blk: [[0], [128]], epi2, 4)
        norm_boundary(tiles, row_norm(l, 5), nxt)

    rMEMT = rm("MEMT")
    rNK = [[Res(f"nk{l}_{i}") for i in range(4)] for l in range(DEPTH)]
    rNV = [[Res(f"nv{l}_{i}") for i in range(4)] for l in range(DEPTH)]
    rKSB, rKT = [Res("ks0"), Res("ks1")], rm("KT")
    rVV = [Res(f"vv{i}") for i in range(NV)]
    ks_sems = [S.dsem("ks0"), S.dsem("ks1")]
    ks_rr = [0]
    vv_sem = [S.dsem(f"vvs{i}") for i in range(NV)]
    vv_rr = [0]
    rPT = [Res("pt0"), Res("pt1")]
    pt_rr = [0]

    def setup_memT():
        sem = S.dsem("memp")
        S.op("sp", lambda e: e.dma_start(out=M[:, 0:2, 0:D], in_=memp.rearrange("(b p) d -> p b d", p=128)),
             writes=[rm("Mstage")], dma=sem)
        for c in range(NCH):
            ps, pr = psum()
            for b in range(2):
                S.op("pe", lambda e, c=c, b=b, ps=ps: e.transpose(out=ps[:, b * 128:(b + 1) * 128], in_=M[:, b, c * 128:(c + 1) * 128],
                                                                identity=IDF[:]), reads=[rm("Mstage"), rCONST], writes=[pr])
            any_copy(MEMT[:, c, :], ps[:, 0:NMEM], [pr], [rMEMT])

    def kv_project(l):
        for (W, dst, rdst) in ((x_w_k, nk, rNK), (x_w_v, nv, rNV)):
            for blk in range(2):
                slot, sres, ssem = wslot()
                sv = slot[:, 0:8 * 512].rearrange("p (k w) -> p k w", k=8)
                S.op("pool", lambda e, W=W, blk=blk, sv=sv: e.dma_start(
                    out=sv, in_=W[l][:, blk * 512:(blk + 1) * 512].rearrange("(k p) w -> p k w", p=128)), writes=[sres], dma=ssem)
                for mc in range(2):
                    ps, pr = psum()
                    for k in range(NCH):
                        S.op("pe", lambda e, k=k, mc=mc, ps=ps, sv=sv: e.matmul(ps[:, :], lhsT=MEMT[:, k, mc * 128:(mc + 1) * 128],
                                                                             rhs=sv[:, k, :], start=(k == 0), stop=(k == NCH - 1)),
                             reads=[sres, rMEMT], writes=[pr])
                    tb, tr, tsem = tmp()
                    any_copy(tb[:, :], ps[:, :], [pr], [tr])
                    S.op("sp", lambda e, dst=dst, mc=mc, blk=blk, tb=tb: e.dma_start(
                        out=dst[l][mc * 128:(mc + 1) * 128, blk * 512:(blk + 1) * 512], in_=tb[:, :]),
                        reads=[tr], writes=[rdst[l][mc * 2 + blk]], dma=tsem)

    def load_kv(ksrc, vsrc, src_res):
        vi = vv_rr[0] % NV
        vv_rr[0] += 1
        ki = ks_rr[0] % 2
        ks_rr[0] += 1
        KS, rKS, ks_sem = KSB[ki], rKSB[ki], ks_sems[ki]
        S.op("pool", lambda e: e.dma_start(out=KS[:], in_=ksrc.rearrange("(b p) d -> p b d", p=128)), reads=src_res, writes=[rKS], dma=ks_sem)
        S.op("pool", lambda e: e.dma_start(out=VV[vi][:], in_=vsrc.rearrange("(b p) d -> p b d", p=128)), reads=src_res, writes=[rVV[vi]],
             dma=vv_sem[vi])
        for half in range(2):
            ps, pr = psum()
            psb = ps[:].bitcast(BF16).rearrange("p (f m) -> p f m", f=4)
            for fcc in range(4):
                for mc in range(2):
                    S.op("pe", lambda e, fcc=fcc, mc=mc, psb=psb, half=half: e.transpose(
                        out=psb[:, fcc, mc * 128:(mc + 1) * 128], in_=KS[:, mc, (half * 4 + fcc) * 128:(half * 4 + fcc + 1) * 128],
                        identity=IDB[:]), reads=[rKS, rm("IDB")], writes=[pr])
            any_copy(KT[:, half * 4:(half + 1) * 4, :], psb, [pr], [rKT])
        return vi

    def rQ(t, h):
        return res(rMisc, ("Q", t.ti, h), "Q")

    def attention(l, gi, tiles, pre_done, nxt, dl=False):
        if gi == 0:
            kv_project(l)
        if not pre_done:
            for t in tiles:
                pre_norm(t, row_norm(l, 2))
        join_BIG()
        QT = BIG[:, 0:8 * GW].rearrange("p (c w) -> p c w", c=8)

        def epi(blk, ji, t, pss, prs):
            c = blk * 4 + ji
            act_copy(QT[:, c, t.cols], pss[0][:, 0:t.n], [prs[0]], [rQ(t, c // 2)], scale=0.0625)
        proj(tiles, lambda blk: [(x_w_q[l][:, blk * 512:(blk + 1) * 512], 0, 512)], 8, lambda k, t: H[:, k, t.cols],
             lambda t: [rHt(t)], lambda blk: [[0], [128], [256], [384]], epi, 2)

        ptiles = [t for t in tiles if t.kind == "P"]
        stiles = [t for t in tiles if t.kind == "S"]
        if ptiles:
            vi = load_kv(nk[l], nv[l], rNK[l] + rNV[l])
            for t in ptiles:
                n = t.n
                for h in range(4):
                    pi = pt_rr[0] % 2
                    pt_rr[0] += 1
                    sc = [psum(), psum()]
                    for mc in range(2):
                        for dc in range(2):
                            S.op("pe", lambda e, mc=mc, dc=dc, ps=sc[mc][0]: e.matmul(
                                ps[:, 0:n], lhsT=KT[:, 2 * h + dc, mc * 128:(mc + 1) * 128], rhs=QT[:, 2 * h + dc, t.cols],
                                start=(dc == 0), stop=(dc == 1)), reads=[rKT, rQ(t, h)], writes=[sc[mc][1]])
                        S.op("act", lambda e, mc=mc, ps=sc[mc][0], pi=pi: e.activation(out=PT[pi][:, mc, 0:n], in_=ps[:, 0:n], func=AF.Exp),
                             reads=[sc[mc][1]], writes=[rPT[pi]])
                    pss, prs = psum()
                    for mc in range(2):
                        S.op("pe", lambda e, mc=mc, pi=pi: e.matmul(pss[:, 0:n], lhsT=ONB[:], rhs=PT[pi][:, mc, 0:n],
                                                                  start=(mc == 0), stop=(mc == 1)), reads=[rPT[pi], rm("ONB")], writes=[prs])
                    tb, tr, _ = tmp()
                    S.op("dve", lambda e, tb=tb: e.reciprocal(out=tb[:, 0:n], in_=pss[:, 0:n]), reads=[prs], writes=[tr])
                    for dc in range(2):
                        po, pro = psum()
                        for mc in range(2):
                            S.op("pe", lambda e, mc=mc, dc=dc, po=po, pi=pi: e.matmul(
                                po[:, 0:n], lhsT=VV[vi][:, mc, (2 * h + dc) * 128:(2 * h + dc + 1) * 128], rhs=PT[pi][:, mc, 0:n],
                                start=(mc == 0), stop=(mc == 1)), reads=[rVV[vi], rPT[pi]], writes=[pro])
                        S.op("dve", lambda e, dc=dc, po=po, tb=tb: e.tensor_tensor(out=QT[:, 2 * h + dc, t.cols], in0=po[:, 0:n],
                                                                               in1=tb[:, 0:n], op=ALU.mult),
                             reads=[pro, tr], writes=[rQ(t, h)])
        for t in stiles:
            hb = [psum_hold() for _ in range(3)]
            rhold = [PSR[i] for i in hb]
            for b in range(16):
                vi = load_kv(ck[l][b], cv[l][b], [])
                pi = pt_rr[0] % 2
                pt_rr[0] += 1
                pts = PT[pi][:, 0, 0:64]
                ps, pr = psum()
                for h in range(4):
                    for mc in range(2):
                        for dc in range(2):
                            S.op("pe", lambda e, h=h, mc=mc, dc=dc, ps=ps, b=b: e.matmul(
                                ps[:, (mc * 4 + h) * 8:(mc * 4 + h + 1) * 8], lhsT=KT[:, 2 * h + dc, mc * 128:(mc + 1) * 128],
                                rhs=QT[:, 2 * h + dc, t.col0 + b * 8:t.col0 + (b + 1) * 8], start=(dc == 0), stop=(dc == 1)),
                                reads=[rKT, rQ(t, h)], writes=[pr])
                S.op("act", lambda e, ps=ps, pts=pts: e.activation(out=pts, in_=ps[:, 0:64], func=AF.Exp), reads=[pr], writes=[rPT[pi]])
                for mc in range(2):
                    S.op("pe", lambda e, mc=mc, b=b, pts=pts: e.matmul(PS[hb[2]][:, b * 32:(b + 1) * 32], lhsT=ONB[:],
                                                                      rhs=pts[:, mc * 32:(mc + 1) * 32], start=(mc == 0), stop=(mc == 1)),
                         reads=[rPT[pi], rm("ONB")], writes=[rhold[2]])
                for h in range(4):
                    for dc in range(2):
                        col = ((h % 2) * 2 + dc) * 128 + b * 8
                        for mc in range(2):
                            S.op("pe", lambda e, h=h, dc=dc, mc=mc, col=col, vi=vi, pts=pts: e.matmul(
                                PS[hb[h // 2]][:, col:col + 8], lhsT=VV[vi][:, mc, (2 * h + dc) * 128:(2 * h + dc + 1) * 128],
                                rhs=pts[:, (mc * 4 + h) * 8:(mc * 4 + h + 1) * 8], start=(mc == 0), stop=(mc == 1)),
                                reads=[rVV[vi], rPT[pi]], writes=[rhold[h // 2]])
            tb, tr, _ = tmp()
            S.op("dve", lambda e, tb=tb: e.reciprocal(out=tb[:, :], in_=PS[hb[2]][:, :]), reads=[rhold[2]], writes=[tr])
            rsv = tb[:, :].rearrange("p (b h t) -> p b h t", b=16, h=4)
            for h in range(4):
                for dc in range(2):
                    col = ((h % 2) * 2 + dc) * 128
                    S.op("dve", lambda e, h=h, dc=dc, col=col, rsv=rsv: e.tensor_tensor(
                        out=QT[:, 2 * h + dc, t.cols].rearrange("p (b t) -> p b t", b=16),
                        in0=PS[hb[h // 2]][:, col:col + 128].rearrange("p (b t) -> p b t", b=16), in1=rsv[:, :, h, :], op=ALU.mult),
                        reads=[rhold[h // 2], tr], writes=[rQ(t, h)])
            for i in hb:
                psum_release(i)

        def epi2(blk, ji, t, pss, prs):
            c = blk * 4 + ji
            act_copy(M[:, c, t.mcols], pss[0][:, 0:t.n], [prs[0]], [rMt(t)])
        proj(tiles, lambda blk: [(x_w_o[l][:, blk * 512:(blk + 1) * 512], 0, 512)], 8, lambda k, t: QT[:, k, t.cols],
             lambda t: [rQ(t, h) for h in range(4)], lambda blk: [[0], [128], [256], [384]], epi2, 2)
        norm_boundary(tiles, row_norm(l, 3), nxt)

    GPW = 30 + 1024
    GP = BIG[:, 0:8 * GPW].rearrange("p (c w) -> p c w", c=8)
    GS = BIG[:, 8 * GPW:8 * GPW + 8 * 16 * 38].rearrange("p (c b w) -> p c b w", c=8, b=16)
    rGPh, rGSh = rm("GPhist"), rm("GShist")
    rGSF, rGPF = rm("GSF"), rm("GPF")
    rCARG = [Res("carg0"), Res("carg1")]

    def rG(t):
        return res(rMisc, ("G", t.ti), "G")

    def mixer0(l, gi, tiles, pre_done, nxt, dl=False):
        flush_deferred()
        j = l // 3
        if not pre_done:
            for t in tiles:
                pre_norm(t, row_norm(l, 0))
        join_BIG()
        join_M()
        has_s = any(t.kind == "S" for t in tiles)
        if gi == 0:
            S.op("pool", lambda e: e.memset(GP[:, :, 0:30], 0.0), writes=[rGPh])
        else:
            S.op("pool", lambda e: e.tensor_copy(out=GP[:, :, 0:30], in_=CARRY_G[j][:]), reads=[rCARG[j]], writes=[rGPh])
        if has_s:
            sem = S.dsem(f"stc{j}")
            MS = MF[:, 0:4 * D].rearrange("p (b d) -> p b d", b=4)
            src = st_conv[j].rearrange("b j d -> (b j) d")
            S.op("sp", lambda e: e.dma_start(out=MS[:, 0:3, :], in_=src[0:384, :].rearrange("(b p) d -> p b d", p=128)),
                 writes=[rm("Mstage")], dma=sem)
            S.op("sp", lambda e: e.dma_start(out=MS[0:96, 3, :], in_=src[384:480, :]), writes=[rm("Mstage")], dma=sem, par=True)
            for c in range(NCH):
                ps, pr = psum()
                for rb in range(4):
                    nr = 128 if rb < 3 else 96
                    S.op("pe", lambda e, c=c, rb=rb, nr=nr, ps=ps: e.transpose(out=ps[:, rb * 128:rb * 128 + nr],
                                                                             in_=MS[0:nr, rb, c * 128:(c + 1) * 128], identity=IDF[0:nr, 0:nr]),
                         reads=[rm("Mstage"), rCONST], writes=[pr])
                any_copy(GS[:, c, :, 0:30], ps[:, 0:480].rearrange("p (b w) -> p b w", b=16), [pr], [rGSh])
            osm = osem(f"ncas_cp{j}")
            S.op("sp", lambda e: e.dma_start(out=ncas[j][:, 0:22, :], in_=st_conv[j][:, 8:30, :]), dma=osm)
            join_M()

        def wparts(blk):
            return [(a_w_pw1[j][:, blk * 256:(blk + 1) * 256], 0, 256), (a_w_pw1[j][:, D + blk * 256:D + (blk + 1) * 256], 256, 256)]

        def epi(blk, ji, t, pss, prs):
            c = blk * 2 + ji
            n = t.n
            tb, tr, _ = tmp()
            S.op("act", lambda e: e.activation(out=tb[:, 0:n], in_=pss[1][:, 0:n], func=AF.Sigmoid, bias=cvs(ROW_BPW1 + 2 * j + 1, c)),
                 reads=[prs[1], rCV], writes=[tr])
            if t.kind == "P":
                S.op("dve", lambda e: e.scalar_tensor_tensor(out=GP[:, c, 30 + t.col0:30 + t.col0 + n], in0=pss[0][:, 0:n],
                                                           scalar=cvs(ROW_BPW1 + 2 * j, c), in1=tb[:, 0:n], op0=ALU.add, op1=ALU.mult),
                     reads=[prs[0], tr, rCV], writes=[rG(t)])
                if t.tok0 + n == 2048:
                    S.op("dve", lambda e: e.scalar_tensor_tensor(out=GPF[:, c, 0:30], in0=pss[0][:, n - 30:n], scalar=cvs(ROW_BPW1 + 2 * j, c),
                                                               in1=tb[:, n - 30:n], op0=ALU.add, op1=ALU.mult),
                         reads=[prs[0], tr, rCV], writes=[rGPF])
            else:
                S.op("dve", lambda e: e.scalar_tensor_tensor(out=GSF[:, c, 0:128], in0=pss[0][:, 0:n], scalar=cvs(ROW_BPW1 + 2 * j, c),
                                                           in1=tb[:, 0:n], op0=ALU.add, op1=ALU.mult),
                     reads=[prs[0], tr, rCV], writes=[rGSF])
                S.op("dve", lambda e: e.tensor_copy(out=GS[:, c, :, 30:38], in_=GSF[:, c, 0:128].rearrange("p (b t) -> p b t", b=16)),
                     reads=[rGSF], writes=[rG(t)])
        proj(tiles, wparts, 8, lambda k, t: H[:, k, t.cols], lambda t: [rHt(t)], lambda blk: [[0, 256], [128, 384]], epi, 4)

        ptiles = [t for t in tiles if t.kind == "P"]
        if gi == 0:
            last = ptiles[-1]
            S.op("pool", lambda e: e.tensor_copy(out=CARRY_G[j][:], in_=GP[:, :, 1024:1054]), reads=[rG(last)], writes=[rCARG[j]])
        if has_s:
            for hh in range(2):
                ps, pr = psum()
                for cc in range(4):
                    c = hh * 4 + cc
                    S.op("pe", lambda e, c=c, cc=cc, ps=ps: e.transpose(out=ps[:, cc * 128:(cc + 1) * 128], in_=GSF[:, c, 0:128], identity=IDF[:]),
                         reads=[rGSF, rCONST], writes=[pr])
                tb, tr, tsem = tmp()
                any_copy(tb[:, :], ps[:, :], [pr], [tr])
                for b in range(16):
                    S.op("sp", lambda e, tb=tb, hh=hh, b=b: e.dma_start(out=ncas[j][b, 22:30, hh * 512:(hh + 1) * 512], in_=tb[b * 8:(b + 1) * 8, :]),
                         reads=[tr], dma=tsem, par=(b > 0))
        if gi == 1:
            for hh in range(2):
                ps, pr = psum()
                for cc in range(4):
                    c = hh * 4 + cc
                    S.op("pe", lambda e, c=c, cc=cc, ps=ps: e.transpose(out=ps[0:30, cc * 128:(cc + 1) * 128], in_=GPF[:, c, 0:30], identity=IDF[:]),
                         reads=[rGPF, rCONST], writes=[pr])
                tb, tr, tsem = tmp()
                any_copy(tb[0:30, :], ps[0:30, :], [pr], [tr])
                S.op("sp", lambda e, tb=tb, hh=hh: e.dma_start(out=ncap[j][:, hh * 512:(hh + 1) * 512], in_=tb[0:30, :]), reads=[tr], dma=tsem)

        hist_res = {"P": rGPh, "S": rGSh}
        for c in range(NCH):
            slot, sres, ssem = wslot()
            DG = slot[:, 0:31 * 128].rearrange("p (k m) -> p k m", k=31)
            r0 = ROW_WDW + 31 * j
            S.op("dve", lambda e, c=c, DG=DG: e.tensor_tensor(out=DG, in0=IDB[:].unsqueeze(1).to_broadcast([128, 31, 128]),
                                                            in1=CV[:, c, r0:r0 + 31].unsqueeze(2).to_broadcast([128, 31, 128]), op=ALU.mult),
                 reads=[rm("IDB"), rCV], writes=[sres])
            for ti, t in enumerate(tiles):
                ps, pr = psum()
                rd = [sres, rG(t), hist_res[t.kind]] + ([rG(tiles[ti - 1])] if (t.kind == "P" and ti > 0) else [])
                for k in range(31):
                    if t.kind == "P":
                        rhs = (lambda k=k, c=c, t=t: GP[:, c, t.col0 + k:t.col0 + k + t.n])
                    else:
                        rhs = (lambda k=k, c=c: GS[:, c, :, k:k + 8])
                    S.op("pe", lambda e, k=k, ps=ps, rhs=rhs, DG=DG, t=t: e.matmul(ps[:, 0:t.n], lhsT=DG[:, k, :], rhs=rhs(),
                                                                               start=(k == 0), stop=(k == 30)), reads=rd, writes=[pr])
                S.op("act", lambda e, c=c, ps=ps, t=t: e.activation(out=M[:, c, t.mcols], in_=ps[:, 0:t.n], func=AF.Identity,
                                                                  bias=cvs(ROW_BDW + j, c)), reads=[pr, rCV], writes=[rMt(t)])
        lnb = {}
        for t in tiles:
            slot, sres, _ = wslot()
            SQv = slot[:, 0:NCH * 512].rearrange("p (c w) -> p c w", c=NCH)
            S.op("act", lambda e, t=t, SQv=SQv: e.activation(out=SQv[:, :, 0:t.n], in_=M[:, :, t.mcols], func=AF.Square), reads=[rMt(t)], writes=[sres])
            S.op("dve", lambda e, t=t: e.tensor_copy(out=H[:, :, t.cols], in_=M[:, :, t.mcols]), reads=[rMt(t)], writes=[rHt(t)])
            lnb[t.ti] = (SQv, sres)
        for t in tiles:
            SQv, sres = lnb[t.ti]
            psq, prq = psum()
            pmu, prm = psum()
            for c in range(NCH):
                S.op("pe", lambda e, c=c, t=t, psq=psq, SQv=SQv: e.matmul(psq[:, 0:t.n], lhsT=ONB[:], rhs=SQv[:, c, 0:t.n], start=(c == 0), stop=(c == NCH - 1)),
                     reads=[sres, rm("ONB")], writes=[prq])
            for c in range(NCH):
                S.op("pe", lambda e, c=c, t=t, pmu=pmu: e.matmul(pmu[:, 0:t.n], lhsT=ONB[:], rhs=H[:, c, t.cols], start=(c == 0), stop=(c == NCH - 1)),
                     reads=[rHt(t), rm("ONB")], writes=[prm])
            lnb[t.ti] = (psq, prq, pmu, prm)
        for t in tiles:
            n = t.n
            psq, prq, pmu, prm = lnb[t.ti]
            mu, rmu, _ = tmp()
            tv, rtv, _ = tmp()
            S.op("dve", lambda e, t=t, mu=mu, pmu=pmu: e.tensor_scalar(out=mu[:, 0:t.n], in0=pmu[:, 0:t.n], scalar1=1.0 / D, scalar2=None, op0=ALU.mult),
                 reads=[prm], writes=[rmu])
            S.op("dve", lambda e, t=t, mu=mu, tv=tv: e.tensor_tensor(out=tv[:, 0:t.n], in0=mu[:, 0:t.n], in1=mu[:, 0:t.n], op=ALU.mult),
                 reads=[rmu], writes=[rtv])
            S.op("dve", lambda e, t=t, tv=tv, psq=psq: e.scalar_tensor_tensor(out=tv[:, 0:t.n], in0=psq[:, 0:t.n], scalar=1.0 / D, in1=tv[:, 0:t.n],
                                                                          op0=ALU.mult, op1=ALU.subtract), reads=[prq, rtv], writes=[rtv])
            RSb, rRSb, _ = tmp()
            S.op("act", lambda e, t=t, tv=tv, RSb=RSb: e.activation(out=RSb[:, 0:t.n], in_=tv[:, 0:t.n], func=AF.Sqrt, bias=EPSB[LN_EPS][:, 0:1]),
                 reads=[rtv, rCONST2], writes=[rRSb])
            S.op("dve", lambda e, t=t, RSb=RSb: e.reciprocal(out=RSb[:, 0:t.n], in_=RSb[:, 0:t.n]), reads=[rRSb], writes=[rRSb])
            S.op("dve", lambda e, t=t, mu=mu: e.tensor_tensor(out=M[:, :, t.mcols], in0=M[:, :, t.mcols],
                                                            in1=mu[:, 0:t.n].unsqueeze(1).to_broadcast([128, NCH, t.n]), op=ALU.subtract),
                 reads=[rMt(t), rmu], writes=[rMt(t)])
            for c in range(NCH):
                S.op("dve", lambda e, c=c, t=t, RSb=RSb: e.scalar_tensor_tensor(out=M[:, c, t.mcols], in0=M[:, c, t.mcols], scalar=cvs(ROW_LNG + j, c),
                                                                     in1=RSb[:, 0:t.n], op0=ALU.mult, op1=ALU.mult),
                     reads=[rMt(t), rRSb, rCV], writes=[rMt(t)])
                S.op("act", lambda e, c=c, t=t: e.activation(out=H[:, c, t.cols], in_=M[:, c, t.mcols], func=AF.Silu, bias=cvs(ROW_LNB + j, c)),
                     reads=[rMt(t), rCV], writes=[rHt(t)])

        def epi2(blk, ji, t, pss, prs):
            c = blk * 4 + ji
            S.op("act", lambda e: e.activation(out=M[:, c, t.mcols], in_=pss[0][:, 0:t.n], func=AF.Identity, bias=cvs(ROW_BPW2 + j, c)),
                 reads=[prs[0], rCV], writes=[rMt(t)])
        proj(tiles, lambda blk: [(a_w_pw2[j][:, blk * 512:(blk + 1) * 512], 0, 512)], 8, lambda k, t: H[:, k, t.cols],
             lambda t: [rHt(t)], lambda blk: [[0], [128], [256], [384]], epi2, 2)
        norm_boundary(tiles, row_norm(l, 1), nxt)

    PSH = GSF[:, :, 0:160].rearrange("p c (b w) -> p c b w", b=16)
    rPSH = rm("PSH")
    rCARP = rm("CARP")

    def mixer2(l, gi, tiles, pre_done, nxt, dl=False):
        flush_deferred()
        if not pre_done:
            for t in tiles:
                pre_norm(t, row_norm(l, 0))
        join_BIG()
        join_M()
        has_s = any(t.kind == "S" for t in tiles)
        rMh = rm("Mhist")
        if has_s:
            sem = S.dsem("stsc")
            S.op("sp", lambda e: e.dma_start(out=MF[0:32, 0:D], in_=st_sc.rearrange("b j d -> (b j) d")), writes=[rm("Mstage")], dma=sem)
            ps, pr = psum()
            for c in range(NCH):
                S.op("pe", lambda e, c=c, ps=ps: e.transpose(out=ps[:, c * 32:(c + 1) * 32], in_=MF[0:32, c * 128:(c + 1) * 128], identity=IDF[0:32, 0:32]),
                     reads=[rm("Mstage"), rCONST], writes=[pr])
            for c in range(NCH):
                any_copy(PSH[:, c, :, 0:2], ps[:, c * 32:(c + 1) * 32].rearrange("p (b w) -> p b w", b=16), [pr], [rPSH])
            join_M()
        if gi == 0:
            S.op("pool", lambda e: e.memset(M[:, :, 0:2], 0.0), writes=[rMh])
        else:
            S.op("pool", lambda e: e.tensor_copy(out=M[:, :, 0:2], in_=CARRY_P[:]), reads=[rCARP], writes=[rMh])

        def wparts(blk):
            return [(c_w_in[:, D + blk * 256:D + (blk + 1) * 256], 0, 256), (c_w_in[:, 2 * D + blk * 256:2 * D + (blk + 1) * 256], 256, 256)]

        def epi(blk, ji, t, pss, prs):
            c = blk * 2 + ji
            n = t.n
            tb, tr, _ = tmp()
            act_copy(tb[:, 0:n], pss[0][:, 0:n], [prs[0]], [tr])
            if t.kind == "P":
                S.op("dve", lambda e: e.tensor_tensor(out=M[:, c, t.mcols], in0=tb[:, 0:n], in1=pss[1][:, 0:n], op=ALU.mult),
                     reads=[tr, prs[1]], writes=[rMt(t)])
            else:
                S.op("dve", lambda e: e.tensor_tensor(out=PSH[:, c, :, 2:10], in0=tb[:, 0:n].rearrange("p (b t) -> p b t", b=16),
                                                    in1=pss[1][:, 0:n].rearrange("p (b t) -> p b t", b=16), op=ALU.mult),
                     reads=[tr, prs[1]], writes=[rPSH])
        proj(tiles, wparts, 8, lambda k, t: H[:, k, t.cols], lambda t: [rHt(t)], lambda blk: [[0, 256], [128, 384]], epi, 4)
        ptiles = [t for t in tiles if t.kind == "P"]
        lastp = ptiles[-1]
        lc = 2 + lastp.col0 + lastp.n
        if gi == 0:
            S.op("pool", lambda e: e.tensor_copy(out=CARRY_P[:], in_=M[:, :, lc - 2:lc]), reads=[rMt(lastp)], writes=[rCARP])
        if has_s:
            tb, tr, _ = tmp()
            tbv = tb[:, 0:256].rearrange("p (c b w) -> p c b w", c=8, b=16)
            S.op("dve", lambda e: e.tensor_copy(out=tbv, in_=PSH[:, :, :, 8:10]), reads=[rPSH], writes=[tr])
            for hh in range(2):
                ps, pr = psum()
                for cc in range(4):
                    c = hh * 4 + cc
                    S.op("pe", lambda e, c=c, cc=cc, ps=ps, tb=tb: e.transpose(out=ps[0:32, cc * 128:(cc + 1) * 128], in_=tb[:, c * 32:(c + 1) * 32],
                                                                             identity=IDF[:]), reads=[tr, rCONST], writes=[pr])
                t2, tr2, tsem2 = tmp()
                any_copy(t2[0:32, :], ps[0:32, :], [pr], [tr2])
                S.op("sp", lambda e, t2=t2, hh=hh: e.dma_start(out=nscs.rearrange("b j d -> (b j) d")[:, hh * 512:(hh + 1) * 512], in_=t2[0:32, :]),
                     reads=[tr2], dma=tsem2)
        if gi == 1:
            for hh in range(2):
                ps, pr = psum()
                for cc in range(4):
                    c = hh * 4 + cc
                    S.op("pe", lambda e, c=c, cc=cc, ps=ps: e.transpose(out=ps[0:2, cc * 128:(cc + 1) * 128], in_=M[:, c, lc - 2:lc], identity=IDF[:]),
                         reads=[rMt(lastp), rCONST], writes=[pr])
                t2, tr2, tsem2 = tmp()
                any_copy(t2[0:2, :], ps[0:2, :], [pr], [tr2])
                S.op("sp", lambda e, t2=t2, hh=hh: e.dma_start(out=nscp[:, hh * 512:(hh + 1) * 512], in_=t2[0:2, :]), reads=[tr2], dma=tsem2)

        U = BIG[:, 0:8 * GW].rearrange("p (c w) -> p c w", c=8)

        def epi_b(blk, ji, t, pss, prs):
            c = blk * 4 + ji
            n = t.n
            tb, tr, _ = tmp()
            w0, w1, w2 = cvs(ROW_WCONV, c), cvs(ROW_WCONV + 1, c), cvs(ROW_WCONV + 2, c)
            if t.kind == "P":
                m0 = 2 + t.col0
                rd = [rMt(t), rMh, rCV] + [rMt(x) for x in tiles if x.kind == "P" and x.ti == t.ti - 1]
                x2, x1, x0 = M[:, c, m0:m0 + n], M[:, c, m0 - 1:m0 - 1 + n], M[:, c, m0 - 2:m0 - 2 + n]
                tv = tb[:, 0:n]
                pv = pss[0][:, 0:n]
                ov = U[:, c, t.cols]
            else:
                rd = [rPSH, rCV]
                x2, x1, x0 = PSH[:, c, :, 2:10], PSH[:, c, :, 1:9], PSH[:, c, :, 0:8]
                tv = tb[:, 0:n].rearrange("p (b t) -> p b t", b=16)
                pv = pss[0][:, 0:n].rearrange("p (b t) -> p b t", b=16)
                ov = U[:, c, t.cols].rearrange("p (b t) -> p b t", b=16)
            S.op("dve", lambda e: e.tensor_scalar(out=tv, in0=x2, scalar1=w2, scalar2=None, op0=ALU.mult), reads=rd, writes=[tr])
            S.op("dve", lambda e: e.scalar_tensor_tensor(out=tv, in0=x1, scalar=w1, in1=tv, op0=ALU.mult, op1=ALU.add), reads=rd + [tr], writes=[tr])
            S.op("dve", lambda e: e.scalar_tensor_tensor(out=tv, in0=x0, scalar=w0, in1=tv, op0=ALU.mult, op1=ALU.add), reads=rd + [tr], writes=[tr])
            S.op("dve", lambda e: e.tensor_tensor(out=ov, in0=pv, in1=tv, op=ALU.mult), reads=[prs[0], tr], writes=[rBt(t)])
        proj(tiles, lambda blk: [(c_w_in[:, blk * 512:(blk + 1) * 512], 0, 512)], 8, lambda k, t: H[:, k, t.cols], lambda t: [rHt(t)],
             lambda blk: [[0], [128], [256], [384]], epi_b, 2)

        def epi_c(blk, ji, t, pss, prs):
            c = blk * 4 + ji
            act_copy(M[:, c, t.mcols], pss[0][:, 0:t.n], [prs[0]], [rMt(t)] + ([rMh] if False else []))
        S.op("dve", lambda e: e.memset(DUMMY[:, 1:2], 0.0), writes=[rMt(t) for t in tiles] + [rMh])
        proj(tiles, lambda blk: [(c_w_out[:, blk * 512:(blk + 1) * 512], 0, 512)], 8, lambda k, t: U[:, k, t.cols], lambda t: [rBt(t)],
             lambda blk: [[0], [128], [256], [384]], epi_c, 2)
        norm_boundary(tiles, row_norm(l, 1), nxt)

    P_LR, P_LI, P_DT, P_MAG, P_TH, P_ABR, P_ABI, P_FR, P_FI, P_T0, P_T1, P_T2 = range(12)
    rS5P = rm("S5P")
    rLB, rLC = rm("LB"), rm("LC")
    rCARS = rm("CARS")
    rAS0 = rm("AS0")
    AS0 = BIG[:, 9216:11264].bitcast(F32).rearrange("p (r j b) -> p r j b", r=2, j=32)
    SCL = 0.999999

    def sp_(i):
        return S5P[:, i, :]

    PI_LO = 3.1415925

    def sin_of(out, in0, mul, shift, ang, ki, reads, writes):
        S.op("dve", lambda e: e.tensor_scalar(out=ang, in0=in0, scalar1=mul, scalar2=shift + 8 * TWO_PI, op0=ALU.mult, op1=ALU.add),
             reads=reads, writes=writes)
        S.op("dve", lambda e: e.tensor_scalar(out=ki, in0=ang, scalar1=1.0 / TWO_PI, scalar2=None, op0=ALU.mult), reads=writes, writes=writes)
        S.op("dve", lambda e: e.scalar_tensor_tensor(out=ang, in0=ki, scalar=-TWO_PI, in1=ang, op0=ALU.mult, op1=ALU.add),
             reads=writes, writes=writes)
        S.op("dve", lambda e: e.tensor_scalar(out=ang, in0=ang, scalar1=-PI_LO, scalar2=PI_LO, op0=ALU.max, op1=ALU.min), reads=writes, writes=writes)
        S.op("act", lambda e: e.activation(out=out, in_=ang, func=AF.Sin), reads=writes, writes=writes)

    def s5_setup():
        rS = rm("Mstage")
        sem = S.dsem("s5p")
        for gl in range(2):
            for (src, idx) in ((b_lam_re, P_LR), (b_lam_im, P_LI)):
                S.op("sp", lambda e, gl=gl, src=src, idx=idx: e.dma_start(
                    out=S5P[gl * 64:(gl + 1) * 64, idx, :], in_=src.rearrange("(j g) p -> g p j", g=2)[gl], allow_slow_non_contiguous=True),
                    writes=[rS5P], dma=sem, par=True)
            S.op("sp", lambda e, gl=gl: e.dma_start(
                out=S5P[gl * 64:(gl + 1) * 64, P_DT, :], in_=b_log_dt.rearrange("(j g) -> g j", g=2)[gl:gl + 1, :].partition_broadcast(64),
                allow_slow_non_contiguous=True), writes=[rS5P], dma=sem, par=True)
        S.op("act", lambda e: e.activation(out=sp_(P_DT), in_=sp_(P_DT), func=AF.Exp), reads=[rS5P], writes=[rS5P])
        S.op("dve", lambda e: e.tensor_tensor(out=sp_(P_T0), in0=sp_(P_LR), in1=sp_(P_DT), op=ALU.mult), reads=[rS5P], writes=[rS5P])
        S.op("act", lambda e: e.activation(out=sp_(P_MAG), in_=sp_(P_T0), func=AF.Exp), reads=[rS5P], writes=[rS5P])
        S.op("dve", lambda e: e.tensor_tensor(out=sp_(P_TH), in0=sp_(P_LI), in1=sp_(P_DT), op=ALU.mult), reads=[rS5P], writes=[rS5P])
        sin_of(sp_(P_ABI), sp_(P_TH), 1.0, 0.0, sp_(P_T0), sp_(P_T1).bitcast(I32), [rS5P], [rS5P])
        sin_of(sp_(P_ABR), sp_(P_TH), 1.0, math.pi / 2, sp_(P_T0), sp_(P_T1).bitcast(I32), [rS5P], [rS5P])
        S.op("dve", lambda e: e.tensor_tensor(out=sp_(P_ABI), in0=sp_(P_ABI), in1=sp_(P_MAG), op=ALU.mult), reads=[rS5P], writes=[rS5P])
        S.op("dve", lambda e: e.tensor_tensor(out=sp_(P_ABR), in0=sp_(P_ABR), in1=sp_(P_MAG), op=ALU.mult), reads=[rS5P], writes=[rS5P])
        S.op("dve", lambda e: e.tensor_tensor(out=sp_(P_T0), in0=sp_(P_LR), in1=sp_(P_LR), op=ALU.mult), reads=[rS5P], writes=[rS5P])
        S.op("dve", lambda e: e.tensor_tensor(out=sp_(P_T1), in0=sp_(P_LI), in1=sp_(P_LI), op=ALU.mult), reads=[rS5P], writes=[rS5P])
        S.op("dve", lambda e: e.tensor_tensor(out=sp_(P_T0), in0=sp_(P_T0), in1=sp_(P_T1), op=ALU.add), reads=[rS5P], writes=[rS5P])
        S.op("dve", lambda e: e.reciprocal(out=sp_(P_T0), in_=sp_(P_T0)), reads=[rS5P], writes=[rS5P])
        S.op("dve", lambda e: e.tensor_scalar(out=sp_(P_T1), in0=sp_(P_ABR), scalar1=-1.0, scalar2=None, op0=ALU.add), reads=[rS5P], writes=[rS5P])
        S.op("dve", lambda e: e.tensor_tensor(out=sp_(P_FR), in0=sp_(P_T1), in1=sp_(P_LR), op=ALU.mult), reads=[rS5P], writes=[rS5P])
        S.op("dve", lambda e: e.tensor_tensor(out=sp_(P_T2), in0=sp_(P_ABI), in1=sp_(P_LI), op=ALU.mult), reads=[rS5P], writes=[rS5P])
        S.op("dve", lambda e: e.tensor_tensor(out=sp_(P_FR), in0=sp_(P_FR), in1=sp_(P_T2), op=ALU.add), reads=[rS5P], writes=[rS5P])
        S.op("dve", lambda e: e.tensor_tensor(out=sp_(P_FR), in0=sp_(P_FR), in1=sp_(P_T0), op=ALU.mult), reads=[rS5P], writes=[rS5P])
        S.op("dve", lambda e: e.tensor_tensor(out=sp_(P_FI), in0=sp_(P_ABI), in1=sp_(P_LR), op=ALU.mult), reads=[rS5P], writes=[rS5P])
        S.op("dve", lambda e: e.tensor_tensor(out=sp_(P_T2), in0=sp_(P_T1), in1=sp_(P_LI), op=ALU.mult), reads=[rS5P], writes=[rS5P])
        S.op("dve", lambda e: e.tensor_tensor(out=sp_(P_FI), in0=sp_(P_FI), in1=sp_(P_T2), op=ALU.subtract), reads=[rS5P], writes=[rS5P])
        S.op("dve", lambda e: e.tensor_tensor(out=sp_(P_FI), in0=sp_(P_FI), in1=sp_(P_T0), op=ALU.mult), reads=[rS5P], writes=[rS5P])
        BR0 = MF[:, 0:1024].rearrange("p (j w) -> p j w", j=32)
        BI0 = MF[:, 1024:2048].rearrange("p (j w) -> p j w", j=32)
        BBR = MF[:, 2048:3072].rearrange("p (j w) -> p j w", j=32)
        BBI = MF[:, 3072:4096].rearrange("p (j w) -> p j w", j=32)
        TMPB = MF[:, 4096:5120].rearrange("p (j w) -> p j w", j=32)
        S.op("pool", lambda e: e.memset(MF[:, 0:2048], 0.0), writes=[rS])
        semb = S.dsem("s5b")
        for gl in range(2):
            for (src, dstb) in ((b_B_re, BR0), (b_B_im, BI0)):
                S.op("sp", lambda e, gl=gl, src=src, dstb=dstb: e.dma_start(
                    out=dstb[gl * 64:(gl + 1) * 64, :, gl * 16:(gl + 1) * 16], in_=src.rearrange("(j g) p h -> g p j h", g=2)[gl]),
                    reads=[rS], writes=[rS], dma=semb, par=True)
        fr_b = sp_(P_FR).unsqueeze(2).to_broadcast([128, 32, 32])
        fi_b = sp_(P_FI).unsqueeze(2).to_broadcast([128, 32, 32])
        S.op("dve", lambda e: e.tensor_tensor(out=BBR, in0=BR0, in1=fr_b, op=ALU.mult), reads=[rS, rS5P], writes=[rS])
        S.op("dve", lambda e: e.tensor_tensor(out=TMPB, in0=BI0, in1=fi_b, op=ALU.mult), reads=[rS, rS5P], writes=[rS])
        S.op("dve", lambda e: e.tensor_tensor(out=BBR, in0=BBR, in1=TMPB, op=ALU.subtract), reads=[rS], writes=[rS])
        S.op("dve", lambda e: e.tensor_tensor(out=BBI, in0=BI0, in1=fr_b, op=ALU.mult), reads=[rS, rS5P], writes=[rS])
        S.op("dve", lambda e: e.tensor_tensor(out=TMPB, in0=BR0, in1=fi_b, op=ALU.mult), reads=[rS, rS5P], writes=[rS])
        S.op("dve", lambda e: e.tensor_tensor(out=BBI, in0=BBI, in1=TMPB, op=ALU.add), reads=[rS], writes=[rS])
        for ri, BB in enumerate((BBR, BBI)):
            for q in range(8):
                ps, pr = psum()
                S.op("pe", lambda e, q=q, BB=BB, ps=ps: e.transpose(out=ps[:, 0:128], in_=BB[:, 4 * q:4 * q + 4, :], identity=IDF[:]),
                     reads=[rS, rCONST], writes=[pr])
                any_copy(LB[:, q, ri, :], ps[:, 0:128], [pr], [rLB])
        CINR = MF[:, 5120:6144].rearrange("p (c w) -> p c w", c=8)
        CINI = MF[:, 6144:7168].rearrange("p (c w) -> p c w", c=8)
        S.op("pool", lambda e: e.memset(MF[:, 5120:7168], 0.0), writes=[rS])
        semc = S.dsem("s5c")
        for (src, dstc) in ((b_C_re, CINR), (b_C_im, CINI)):
            sv = src.rearrange("(c jj g) h p -> jj g h c p", c=8, jj=4, g=2)
            for jj in range(4):
                for gp in range(2):
                    S.op("sp", lambda e, jj=jj, gp=gp, sv=sv, dstc=dstc: e.dma_start(
                        out=dstc[32 * jj + 16 * gp:32 * jj + 16 * gp + 16, :, 64 * gp:64 * gp + 64], in_=sv[jj, gp]),
                        reads=[rS], writes=[rS], dma=semc, par=True)
        for ri, CIN in enumerate((CINR, CINI)):
            for cc in range(8):
                ps, pr = psum()
                S.op("pe", lambda e, cc=cc, CIN=CIN, ps=ps: e.transpose(out=ps[:, 0:128], in_=CIN[:, cc, :], identity=IDF[:]),
                     reads=[rS, rCONST], writes=[pr])
                act_copy(LC[:, 4 * cc:4 * cc + 4, ri, :], ps[:, 0:128].rearrange("p (jj w) -> p jj w", jj=4), [pr], [rLC],
                         scale=(1.0 if ri == 0 else -1.0))
        S.op("pool", lambda e: e.memset(MASK[:], 1.0), writes=[rm("MASK")])
        S.op("pool", lambda e: e.memset(MASK[:].rearrange("p (b t) -> p b t", b=16)[:, :, 0:1], 0.0), writes=[rm("MASK")])

    def mixer1(l, gi, tiles, pre_done, nxt, dl=False):
        flush_deferred()
        strict_prev = S.strict
        if not pre_done:
            for t in tiles:
                pre_norm(t, row_norm(l, 0))
        has_s = any(t.kind == "S" for t in tiles)
        rS = rm("Mstage")
        join_M()
        join_BIG()
        if has_s:
            sem = S.dsem("s5s0")
            S0 = MF[:, 4096:5120].rearrange("p (r j b) -> p r j b", r=2, j=32)
            for ri, src in enumerate((st_re, st_im)):
                S.op("sp", lambda e, src=src: e.dma_start(out=MF[0:16, 0:4096], in_=src), reads=[rS], writes=[rS], dma=sem)
                ps, pr = psum()
                for jx in range(32):
                    S.op("pe", lambda e, jx=jx, ps=ps: e.transpose(out=ps[:, jx * 16:(jx + 1) * 16], in_=MF[0:16, jx * 128:(jx + 1) * 128],
                                                                 identity=IDF[0:16, 0:16]), reads=[rS, rCONST], writes=[pr])
                any_copy(S0[:, ri], ps[:, 0:512].rearrange("p (j b) -> p j b", j=32), [pr, rS], [rS])
            abr_b = sp_(P_ABR).unsqueeze(2).to_broadcast([128, 32, 16])
            abi_b = sp_(P_ABI).unsqueeze(2).to_broadcast([128, 32, 16])
            TM = MF[:, 5120:5632].rearrange("p (j b) -> p j b", j=32)
            S.op("dve", lambda e: e.tensor_tensor(out=AS0[:, 0], in0=S0[:, 0], in1=abr_b, op=ALU.mult), reads=[rS, rS5P], writes=[rAS0])
            S.op("dve", lambda e: e.tensor_tensor(out=TM, in0=S0[:, 1], in1=abi_b, op=ALU.mult), reads=[rS, rS5P], writes=[rS])
            S.op("dve", lambda e: e.tensor_tensor(out=AS0[:, 0], in0=AS0[:, 0], in1=TM, op=ALU.subtract), reads=[rS, rAS0], writes=[rAS0])
            S.op("dve", lambda e: e.tensor_tensor(out=AS0[:, 1], in0=S0[:, 1], in1=abr_b, op=ALU.mult), reads=[rS, rS5P], writes=[rAS0])
            S.op("dve", lambda e: e.tensor_tensor(out=TM, in0=S0[:, 0], in1=abi_b, op=ALU.mult), reads=[rS, rS5P, rAS0], writes=[rS])
            S.op("dve", lambda e: e.tensor_tensor(out=AS0[:, 1], in0=AS0[:, 1], in1=TM, op=ALU.add), reads=[rS, rAS0], writes=[rAS0])
            join_M()
        def scr(i):
            return MF[:, i * 512:(i + 1) * 512], s5_res[i]
        (IOT, rIOT), (COS, rCOS), (SIN, rSIN), (A, rA), (B, rB), (VR, rVR), (VI, rVI), (RR, rRR), (RI, rRI), (CS8, rCS8) = [scr(i) for i in range(10)]
        MPb = [MF[:, 5120 + k * 256:5376 + k * 256].bitcast(BF16) for k in range(4)]
        NSN, rNSN = MF[:, 6144:6656], rm("NSN")
        MSKM = MF[:, 6656:6784]
        INI = MF[:, 6784:6792]
        SOUT = GSF[:].rearrange("p c w -> p (c w)")[:, 0:1024].rearrange("p (r j b) -> p r j b", r=2, j=32)
        rSRB, rSIB, rMSK, rINI, rSOUT = rm("SRB"), rm("SIB"), rm("MSKM"), rm("INI"), rm("SOUT")
        S.op("pool", lambda e: e.iota(IOT, [[1, 512]], base=0, channel_multiplier=0, allow_small_or_imprecise_dtypes=True), writes=[rIOT])
        YT = BIG[:, 0:8 * GW].rearrange("p (c w) -> p c w", c=8)
        hold = {}
        bankd = {}

        def emit_B(j, ti):
            t = tiles[ti]
            q, jj = j // 4, j % 4
            pxr, prr = psum()
            pxi, pri = psum()
            for (px, prx, ri) in ((pxr, prr, 0), (pxi, pri, 1)):
                S.op("pe", lambda e, px=px, ri=ri: e.matmul(
                    px[:, 0:t.n], lhsT=LB[32 * jj:32 * jj + 32, q, ri, :], rhs=H[32 * jj:32 * jj + 32, q, t.cols], start=True, stop=True,
                    tile_position=(32 * jj, 0)), reads=[rLB, rHt(t)], writes=[prx])
            bankd[(j, ti)] = (pxr, prr, pxi, pri)

        TBL = [(COS, SIN, NSN, rCOS, rSIN, rNSN),
               (MF[:, 6912:7424], MF[:, 7424:7936], MF[:, 7936:8448], rm("COS1"), rm("SIN1"), rm("NSN1"))]

        def gen_tables(jn):
            COSn, SINn, NSNn, rCn, rSn, rNn = TBL[jn % 2]
            thj = S5P[:, P_TH, jn:jn + 1]
            kI = VR.bitcast(I32)
            S.op("dve", lambda e: e.tensor_scalar(out=B, in0=IOT, scalar1=thj, scalar2=8 * TWO_PI, op0=ALU.mult, op1=ALU.add),
                 reads=[rIOT, rS5P], writes=[rB])
            S.op("dve", lambda e: e.tensor_scalar(out=kI, in0=B, scalar1=1.0 / TWO_PI, scalar2=None, op0=ALU.mult), reads=[rB], writes=[rVR])
            S.op("dve", lambda e: e.scalar_tensor_tensor(out=B, in0=kI, scalar=-TWO_PI, in1=B, op0=ALU.mult, op1=ALU.add),
                 reads=[rVR, rB], writes=[rB])
            S.op("dve", lambda e: e.tensor_scalar(out=B, in0=B, scalar1=-PI_LO, scalar2=PI_LO, op0=ALU.max, op1=ALU.min), reads=[rB], writes=[rB])
            S.op("act", lambda e: e.activation(out=SINn, in_=B, func=AF.Sin), reads=[rB], writes=[rSn])
            S.op("act", lambda e: e.activation(out=A, in_=B, func=AF.Abs), reads=[rB], writes=[rA])
            S.op("act", lambda e: e.activation(out=COSn, in_=A, func=AF.Sin, scale=-1.0, bias=HPI[:, 0:1]), reads=[rA, rCONST2], writes=[rCn])
            S.op("act", lambda e: e.mul(out=NSNn, in_=SINn, mul=-1.0), reads=[rSn], writes=[rNn])

        pend1, pend2 = [], []
        gen_tables(0)
        for j in range(32):
            q, jj = j // 4, j % 4
            COS, SIN, NSN, rCOS, rSIN, rNSN = TBL[j % 2]
            if has_s:
                S.op("dve", lambda e: e.tensor_copy(out=CS8[:, 0:128].rearrange("p (b t) -> p b t", b=16),
                                                    in_=COS[:, 0:8].unsqueeze(1).to_broadcast([128, 16, 8])), reads=[rCOS], writes=[rCS8])
                S.op("dve", lambda e: e.tensor_copy(out=CS8[:, 128:256].rearrange("p (b t) -> p b t", b=16),
                                                    in_=SIN[:, 0:8].unsqueeze(1).to_broadcast([128, 16, 8])), reads=[rSIN], writes=[rCS8])
                S.op("dve", lambda e: e.tensor_copy(out=CS8[:, 256:384].rearrange("p (b t) -> p b t", b=16),
                                                    in_=NSN[:, 0:8].unsqueeze(1).to_broadcast([128, 16, 8])), reads=[rNSN], writes=[rCS8])
                S.op("dve", lambda e, j=j: e.tensor_scalar(out=MSKM, in0=MASK[:], scalar1=S5P[:, P_MAG, j:j + 1], scalar2=None, op0=ALU.mult),
                     reads=[rm("MASK"), rS5P], writes=[rMSK])
            for ti, t in enumerate(tiles):
                n = t.n
                if jj == 0:
                    hold[ti] = psum_hold()
                if t.kind == "P":
                    cs, sn, nsn, rcs, rsn, rns = COS[:, 0:n], SIN[:, 0:n], NSN[:, 0:n], rCOS, rSIN, rNSN
                else:
                    cs, sn, nsn, rcs, rsn, rns = CS8[:, 0:128], CS8[:, 128:256], CS8[:, 256:384], rCS8, rCS8, rCS8
                if not bankd:
                    emit_B(j, ti)
                pxr, prr, pxi, pri = bankd.pop((j, ti))
                S.op("dve", lambda e, pxr=pxr, cs=cs, n=n: e.tensor_tensor(out=VR[:, 0:n], in0=pxr[:, 0:n], in1=cs, op=ALU.mult), reads=[prr, rcs], writes=[rVR])
                S.op("dve", lambda e, pxi=pxi, sn=sn, n=n: e.tensor_tensor(out=A[:, 0:n], in0=pxi[:, 0:n], in1=sn, op=ALU.mult), reads=[pri, rsn], writes=[rA])
                S.op("dve", lambda e, pxi=pxi, cs=cs, n=n: e.tensor_tensor(out=VI[:, 0:n], in0=pxi[:, 0:n], in1=cs, op=ALU.mult), reads=[pri, rcs], writes=[rVI])
                S.op("dve", lambda e, pxr=pxr, sn=sn, n=n: e.tensor_tensor(out=B[:, 0:n], in0=pxr[:, 0:n], in1=sn, op=ALU.mult), reads=[prr, rsn], writes=[rB])
                S.op("dve", lambda e, n=n: e.tensor_tensor(out=VR[:, 0:n], in0=VR[:, 0:n], in1=A[:, 0:n], op=ALU.add), reads=[rVR, rA], writes=[rVR])
                S.op("dve", lambda e, n=n: e.tensor_tensor(out=VI[:, 0:n], in0=VI[:, 0:n], in1=B[:, 0:n], op=ALU.subtract), reads=[rVI, rB], writes=[rVI])
                nxt_it = (j, ti + 1) if ti + 1 < len(tiles) else ((j + 1, 0) if j + 1 < 32 else None)
                if nxt_it is not None:
                    emit_B(*nxt_it)
                while pend1:
                    pend1.pop(0)()
                mag = S5P[:, P_MAG, j:j + 1]
                if t.kind == "P":
                    first = (t.tok0 == 0)
                    if not first:
                        c1, s1 = COS[:, 1:2], SIN[:, 1:2]
                        sr, si = CARRY_S[:, j, 0:1], CARRY_S[:, j, 1:2]
                        S.op("dve", lambda e, si=si, s1=s1: e.tensor_scalar(out=INI[:, 2:3], in0=si, scalar1=s1, scalar2=None, op0=ALU.mult),
                             reads=[rCARS, rSIN], writes=[rINI])
                        S.op("dve", lambda e, sr=sr, c1=c1: e.scalar_tensor_tensor(out=INI[:, 0:1], in0=sr, scalar=c1, in1=INI[:, 2:3], op0=ALU.mult,
                                                                                  op1=ALU.subtract), reads=[rCARS, rCOS, rINI], writes=[rINI])
                        S.op("dve", lambda e, si=si, c1=c1: e.tensor_scalar(out=INI[:, 2:3], in0=si, scalar1=c1, scalar2=None, op0=ALU.mult),
                             reads=[rCARS, rCOS, rINI], writes=[rINI])
                        S.op("dve", lambda e, sr=sr, s1=s1: e.scalar_tensor_tensor(out=INI[:, 1:2], in0=sr, scalar=s1, in1=INI[:, 2:3], op0=ALU.mult,
                                                                                  op1=ALU.add), reads=[rCARS, rSIN, rINI], writes=[rINI])
                        ir, ii = INI[:, 0:1], INI[:, 1:2]
                    else:
                        ir, ii = 0.0, 0.0
                    d0 = mag.to_broadcast([128, n])
                    S.op("dve", lambda e, d0=d0, ir=ir, n=n: e.tensor_tensor_scan(out=RR[:, 0:n], data0=d0, data1=VR[:, 0:n], initial=ir, op0=ALU.mult, op1=ALU.add),
                         reads=[rVR, rS5P, rINI], writes=[rRR])
                    S.op("dve", lambda e, d0=d0, ii=ii, n=n: e.tensor_tensor_scan(out=RI[:, 0:n], data0=d0, data1=VI[:, 0:n], initial=ii, op0=ALU.mult, op1=ALU.add),
                         reads=[rVI, rS5P, rINI], writes=[rRI])
                else:
                    vr3 = VR[:, 0:128].rearrange("p (b t) -> p b t", b=16)
                    vi3 = VI[:, 0:128].rearrange("p (b t) -> p b t", b=16)
                    S.op("dve", lambda e, j=j, vr3=vr3: e.tensor_tensor(out=vr3[:, :, 0], in0=vr3[:, :, 0], in1=AS0[:, 0, j, :], op=ALU.add),
                         reads=[rVR, rAS0], writes=[rVR])
                    S.op("dve", lambda e, j=j, vi3=vi3: e.tensor_tensor(out=vi3[:, :, 0], in0=vi3[:, :, 0], in1=AS0[:, 1, j, :], op=ALU.add),
                         reads=[rVI, rAS0], writes=[rVI])
                    S.op("dve", lambda e: e.tensor_tensor_scan(out=RR[:, 0:128], data0=MSKM, data1=VR[:, 0:128], initial=0.0, op0=ALU.mult, op1=ALU.add),
                         reads=[rVR, rMSK], writes=[rRR])
                    S.op("dve", lambda e: e.tensor_tensor_scan(out=RI[:, 0:128], data0=MSKM, data1=VI[:, 0:128], initial=0.0, op0=ALU.mult, op1=ALU.add),
                         reads=[rVI, rMSK], writes=[rRI])
                if ti == len(tiles) - 1 and j + 1 < 32:
                    gen_tables(j + 1)
                S.op("dve", lambda e, cs=cs, n=n: e.tensor_tensor(out=MPb[0][:, 0:n], in0=RR[:, 0:n], in1=cs, op=ALU.mult), reads=[rRR, rcs], writes=[rSRB])
                S.op("dve", lambda e, sn=sn, n=n: e.tensor_tensor(out=MPb[2][:, 0:n], in0=RR[:, 0:n], in1=sn, op=ALU.mult), reads=[rRR, rsn], writes=[rSIB])
                S.op("dve", lambda e, nsn=nsn, n=n: e.tensor_tensor(out=MPb[1][:, 0:n], in0=RI[:, 0:n], in1=nsn, op=ALU.mult), reads=[rRI, rns], writes=[rSRB])
                S.op("dve", lambda e, cs=cs, n=n: e.tensor_tensor(out=MPb[3][:, 0:n], in0=RI[:, 0:n], in1=cs, op=ALU.mult), reads=[rRI, rcs], writes=[rSIB])
                while pend2:
                    pend2.pop(0)()
                if t.kind == "P":
                    cl, sl = COS[:, n - 1:n], SIN[:, n - 1:n]
                    rl, il = RR[:, n - 1:n], RI[:, n - 1:n]
                    S.op("dve", lambda e, il=il, sl=sl: e.tensor_scalar(out=INI[:, 4:5], in0=il, scalar1=sl, scalar2=None, op0=ALU.mult),
                         reads=[rRI, rSIN], writes=[rINI])
                    S.op("dve", lambda e, j=j, rl=rl, cl=cl: e.scalar_tensor_tensor(out=CARRY_S[:, j, 0:1], in0=rl, scalar=cl, in1=INI[:, 4:5],
                                                                                  op0=ALU.mult, op1=ALU.subtract), reads=[rRR, rCOS, rINI], writes=[rCARS])
                    S.op("dve", lambda e, il=il, cl=cl: e.tensor_scalar(out=INI[:, 4:5], in0=il, scalar1=cl, scalar2=None, op0=ALU.mult),
                         reads=[rRI, rCOS, rINI, rCARS], writes=[rINI])
                    S.op("dve", lambda e, j=j, rl=rl, sl=sl: e.scalar_tensor_tensor(out=CARRY_S[:, j, 1:2], in0=rl, scalar=sl, in1=INI[:, 4:5],
                                                                                  op0=ALU.mult, op1=ALU.add), reads=[rRR, rSIN, rINI], writes=[rCARS])
                else:
                    c7, s7 = COS[:, 7:8], SIN[:, 7:8]
                    rr7 = RR[:, 0:128].rearrange("p (b t) -> p b t", b=16)[:, :, 7]
                    ri7 = RI[:, 0:128].rearrange("p (b t) -> p b t", b=16)[:, :, 7]
                    t16 = MF[:, 6792:6808]
                    S.op("dve", lambda e, ri7=ri7, s7=s7: e.tensor_scalar(out=t16, in0=ri7, scalar1=s7, scalar2=None, op0=ALU.mult),
                         reads=[rRI, rSIN], writes=[rINI])
                    S.op("dve", lambda e, j=j, rr7=rr7, c7=c7: e.scalar_tensor_tensor(out=SOUT[:, 0, j, :], in0=rr7, scalar=c7, in1=t16, op0=ALU.mult,
                                                                                    op1=ALU.subtract), reads=[rRR, rCOS, rINI], writes=[rSOUT])
                    S.op("dve", lambda e, ri7=ri7, c7=c7: e.tensor_scalar(out=t16, in0=ri7, scalar1=c7, scalar2=None, op0=ALU.mult),
                         reads=[rRI, rCOS, rINI, rSOUT], writes=[rINI])
                    S.op("dve", lambda e, j=j, rr7=rr7, s7=s7: e.scalar_tensor_tensor(out=SOUT[:, 1, j, :], in0=rr7, scalar=s7, in1=t16, op0=ALU.mult,
                                                                                    op1=ALU.add), reads=[rRR, rSIN, rINI], writes=[rSOUT])
                hb = hold[ti]
                for k4, (ri_, mk, rk) in enumerate(((0, MPb[0], rSRB), (1, MPb[2], rSIB), (0, MPb[1], rSRB), (1, MPb[3], rSIB))):
                    S.op("pe", lambda e, hb=hb, j=j, jj=jj, n=n, ri_=ri_, mk=mk, k4=k4: e.matmul(
                        PS[hb][32 * jj:32 * jj + 32, 0:n], lhsT=LC[:, j, ri_, :], rhs=mk[:, 0:n], start=(k4 == 0), stop=(k4 == 3),
                        tile_position=(0, 32 * jj)), reads=[rLC, rk], writes=[PSR[hb]])
                if jj == 3:
                    def g1(hb=hb, q=q, t=t):
                        tb, tr, _ = tmp()
                        t2, tr2, _ = tmp()
                        S.op("dve", lambda e: e.scalar_tensor_tensor(out=tb[:, 0:t.n], in0=H[:, q, t.cols], scalar=cvs(ROW_BD, q),
                                                                    in1=PS[hb][:, 0:t.n], op0=ALU.mult, op1=ALU.add),
                             reads=[rHt(t), rCV, PSR[hb]], writes=[tr])
                        S.op("dve", lambda e: e.tensor_tensor(out=t2[:, 0:t.n], in0=tb[:, 0:t.n], in1=tb[:, 0:t.n], op=ALU.mult), reads=[tr], writes=[tr2])
                        S.op("dve", lambda e: e.tensor_scalar(out=t2[:, 0:t.n], in0=t2[:, 0:t.n], scalar1=0.044715, scalar2=1.0, op0=ALU.mult, op1=ALU.add),
                             reads=[tr2], writes=[tr2])
                        S.op("dve", lambda e: e.tensor_tensor(out=t2[:, 0:t.n], in0=t2[:, 0:t.n], in1=tb[:, 0:t.n], op=ALU.mult), reads=[tr, tr2], writes=[tr2])
                        S.op("act", lambda e: e.activation(out=t2[:, 0:t.n], in_=t2[:, 0:t.n], func=AF.Sigmoid, scale=2.0 * math.sqrt(2.0 / math.pi)),
                             reads=[tr2], writes=[tr2])
                        psum_release(hb)

                        def g2():
                            S.op("dve", lambda e: e.tensor_tensor(out=YT[:, q, t.cols], in0=tb[:, 0:t.n], in1=t2[:, 0:t.n], op=ALU.mult),
                                 reads=[tr, tr2], writes=[rBt(t)])
                        pend2.append(g2)
                    pend1.append(g1)
        while pend1:
            pend1.pop(0)()
        while pend2:
            pend2.pop(0)()
        if has_s:
            for ri, dst in enumerate((nres, nims)):
                for g4 in range(8):
                    ps, pr = psum()
                    for jx in range(4):
                        j = g4 * 4 + jx
                        S.op("pe", lambda e, j=j, jx=jx, ri=ri, ps=ps: e.transpose(out=ps[0:16, jx * 128:(jx + 1) * 128], in_=SOUT[:, ri, j, :], identity=IDF[:]),
                             reads=[rSOUT, rCONST], writes=[pr])
                    tb, tr, tsem = tmp()
                    any_copy(tb[0:16, :], ps[0:16, :], [pr], [tr])
                    S.op("sp", lambda e, dst=dst, g4=g4, tb=tb: e.dma_start(out=dst[:, g4 * 512:(g4 + 1) * 512], in_=tb[0:16, :]), reads=[tr], dma=tsem)
        if gi == 1:
            for ri, dst in enumerate((nrep, nimp)):
                ps, pr = psum()
                S.op("pe", lambda e, ri=ri, ps=ps: e.transpose(out=ps[0:32, 0:128], in_=CARRY_S[:, :, ri], identity=IDF[:]), reads=[rCARS, rCONST], writes=[pr])
                tb, tr, tsem = tmp()
                any_copy(tb[0:32, 0:128], ps[0:32, 0:128], [pr], [tr])
                S.op("sp", lambda e, dst=dst, tb=tb: e.dma_start(out=dst.rearrange("o (j p) -> (o j) p", p=128), in_=tb[0:32, 0:128]), reads=[tr], dma=tsem)
        join_M()

        def wparts(blk):
            return [(b_w_glu[:, blk * 256:(blk + 1) * 256], 0, 256), (b_w_glu[:, D + blk * 256:D + (blk + 1) * 256], 256, 256)]

        def epi(blk, ji, t, pss, prs):
            c = blk * 2 + ji
            n = t.n
            tb, tr, _ = tmp()
            S.op("act", lambda e: e.activation(out=tb[:, 0:n], in_=pss[1][:, 0:n], func=AF.Sigmoid, bias=cvs(ROW_BGLU + 1, c)),
                 reads=[prs[1], rCV], writes=[tr])
            S.op("dve", lambda e: e.scalar_tensor_tensor(out=M[:, c, t.mcols], in0=pss[0][:, 0:n], scalar=cvs(ROW_BGLU, c), in1=tb[:, 0:n],
                                                       op0=ALU.add, op1=ALU.mult), reads=[prs[0], tr, rCV], writes=[rMt(t)])
        S.strict = strict_prev
        proj(tiles, wparts, 8, lambda k, t: YT[:, k, t.cols], lambda t: [rBt(t)], lambda blk: [[0, 256], [128, 384]], epi, 4)
        norm_boundary(tiles, row_norm(l, 1), nxt)

    MARKS = cfg.setdefault("_marks", [])
    setup_consts()
    setup_eps()
    subs = cfg.get("subs", ("mix", "attn", "ffn"))
    if any(l % 3 == 1 for l in layers) and "mix" in subs:
        s5_setup()
        join_M()
    if "attn" in subs:
        setup_memT()
        join_M()
    for gi in groups_sel:
        tiles = GROUPS[gi]
        if cfg.get("no_sample"):
            tiles = [t for t in tiles if t.kind == "P"]
        load_x(tiles)
        join_M()
        seq = []
        for l in layers:
            if "mix" in subs:
                seq.append(([mixer0, mixer1, mixer2][l % 3], l, row_norm(l, 0), "mix"))
            if "attn" in subs:
                seq.append((attention, l, row_norm(l, 2), "attn"))
            if "ffn" in subs:
                seq.append((ffn, l, row_norm(l, 4), "ffn"))
        for si, (fn, l, grow, kind) in enumerate(seq):
            nxt = seq[si + 1][2] if si + 1 < len(seq) else None
            dl = (si + 1 < len(seq)) and seq[si + 1][3] in ("attn", "ffn") and not cfg.get("no_defer")
            MARKS.append((f"g{gi} L{l} {kind}", len(S.ops["pe"]), len(S.ops["dve"]), len(S.ops["act"]), len(S.ops["pool"])))
            fn(l, gi, tiles, si > 0, nxt, dl)
        flush_deferred()
        MARKS.append((f"g{gi} store", len(S.ops["pe"]), len(S.ops["dve"]), len(S.ops["act"]), len(S.ops["pool"])))
        join_M()
        store_y(tiles)
        join_M()

    blk_ctx = es.enter_context(nc.Block())
    S.emit(blk_ctx)
    es.close()
    return nc


def make_vecs(inp):
    rows = []
    rows.append(inp["norm_g"].reshape(24, D))
    rows.append(inp["a_b_pw1"].reshape(4, D))
    rows.append(inp["a_w_dw"].reshape(62, D))
    rows.append(inp["a_b_dw"].reshape(2, D))
    rows.append(inp["a_ln_g"].reshape(2, D))
    rows.append(inp["a_ln_b"].reshape(2, D))
    rows.append(inp["a_b_pw2"].reshape(2, D))
    rows.append(inp["b_D"].reshape(1, D))
    rows.append(inp["b_b_glu"].reshape(2, D))
    rows.append(inp["c_w_conv"].reshape(3, D))
    v = np.concatenate(rows, axis=0).astype(np.float32)
    out = np.zeros((128, D), np.float32)
    out[:v.shape[0]] = v
    return out


def make_in_maps(inp, cores):
    vecs = make_vecs(inp)
    shared = dict(
        vecs=vecs,
        a_w_pw1=inp["a_w_pw1"], a_w_pw2=inp["a_w_pw2"],
        b_lam_re=inp["b_lam_re"][0], b_lam_im=inp["b_lam_im"][0], b_log_dt=inp["b_log_dt"][0],
        b_B_re=inp["b_B_re"][0], b_B_im=inp["b_B_im"][0], b_C_re=inp["b_C_re"][0], b_C_im=inp["b_C_im"][0],
        b_w_glu=inp["b_w_glu"][0], c_w_in=inp["c_w_in"][0], c_w_out=inp["c_w_out"][0],
        x_w_q=inp["x_w_q"], x_w_k=inp["x_w_k"], x_w_v=inp["x_w_v"], x_w_o=inp["x_w_o"],
        f_w_gu=inp["f_w_gu"], f_w_down=inp["f_w_down"],
    )
    shared = {k: np.ascontiguousarray(v, dtype=np.float32) for k, v in shared.items()}
    maps = []
    for c in cores:
        b0, b1 = 16 * c, 16 * c + 16
        m = dict(shared)
        m["xp"] = np.ascontiguousarray(inp["x_prompt"][c])
        m["xs"] = np.ascontiguousarray(inp["x_sample"][b0:b1].reshape(128, D))
        m["st_conv"] = np.ascontiguousarray(inp["state_conv_a"][:, b0:b1])
        m["st_re"] = np.ascontiguousarray(inp["state_ssm_re"][0, b0:b1].reshape(16, 4096))
        m["st_im"] = np.ascontiguousarray(inp["state_ssm_im"][0, b0:b1].reshape(16, 4096))
        m["st_sc"] = np.ascontiguousarray(inp["state_sconv"][0, b0:b1])
        m["ck"] = np.ascontiguousarray(inp["cache_mem_k"][:, b0:b1].reshape(4, 16, NMEM, D))
        m["cv"] = np.ascontiguousarray(inp["cache_mem_v"][:, b0:b1].reshape(4, 16, NMEM, D))
        m["memp"] = np.ascontiguousarray(inp["mem_prompt"][c])
        maps.append(m)
    return maps


def kernel(**inp):
    inp = {k: np.asarray(v) for k, v in inp.items()}
    nc = build_program({})
    maps = make_in_maps(inp, list(range(8)))
    res = run_bass_kernel_spmd(nc, maps, core_ids=list(range(8)))
    r = res.results
    f32 = np.float32
    y_prompt = np.stack([r[c]["yp"] for c in range(8)]).astype(f32)
    y_sample = np.concatenate([r[c]["ys"].reshape(16, 8, D) for c in range(8)]).astype(f32)
    nk = np.stack([r[c]["nk"] for c in range(8)], axis=1).reshape(4, 8, NMEM, 4, 256).astype(f32)
    nv = np.stack([r[c]["nv"] for c in range(8)], axis=1).reshape(4, 8, NMEM, 4, 256).astype(f32)
    ncap = np.stack([r[c]["ncap"] for c in range(8)], axis=1).astype(f32)
    ncas = np.concatenate([r[c]["ncas"] for c in range(8)], axis=1).astype(f32)
    nrep = np.stack([r[c]["nrep"].reshape(1, 64, 64) for c in range(8)], axis=1).astype(f32)
    nimp = np.stack([r[c]["nimp"].reshape(1, 64, 64) for c in range(8)], axis=1).astype(f32)
    nres = np.concatenate([r[c]["nres"].reshape(1, 16, 64, 64) for c in range(8)], axis=1).astype(f32)
    nims = np.concatenate([r[c]["nims"].reshape(1, 16, 64, 64) for c in range(8)], axis=1).astype(f32)
    nscp = np.stack([r[c]["nscp"].reshape(1, 2, D) for c in range(8)], axis=1).astype(f32)
    nscs = np.concatenate([r[c]["nscs"].reshape(1, 16, 2, D) for c in range(8)], axis=1).astype(f32)
    return (y_prompt, y_sample, nk, nv, ncap, ncas, nrep, nimp, nres, nims, nscp, nscs)
```

```python
import math
from contextlib import ExitStack

import numpy as np
import concourse.bass as bass
import concourse.mybir as mybir
from concourse.bass_utils import run_bass_kernel_spmd

F32 = mybir.dt.float32
BF16 = mybir.dt.bfloat16
I32 = mybir.dt.int32
ALU = mybir.AluOpType
AF = mybir.ActivationFunctionType

D = 1024
NCH = 8
DEPTH = 4
DFF = 2816
NMEM = 256
GW = 1152
RMS_EPS = 1e-6
LN_EPS = 1e-5
TWO_PI = 2.0 * math.pi

ENGS = ("pe", "act", "dve", "pool", "sp")


class Res:
    __slots__ = ("name", "w", "r")

    def __init__(self, name):
        self.name = name
        self.w = None
        self.r = {}


class DSem:
    def __init__(self, sem):
        self.sem = sem
        self.count = 0


class _Rec:
    def __init__(self):
        self.call = None

    def __getattr__(self, name):
        def f(*a, **k):
            self.call = (name, a, k)
            return self
        return f


class Sched:
    def __init__(self, nc, es):
        self.nc = nc
        self.es = es
        self.ops = {e: [] for e in ENGS}
        self.esem = {e: es.enter_context(nc.semaphore("sem_" + e)) for e in ("pe", "act", "dve", "pool")}
        self.dsems = []
        self.strict = False
        self.clock = 0
        self.pending = []
        self.pend_clock = 0
        self.deferring = False

    def dsem(self, name):
        d = DSem(self.es.enter_context(self.nc.semaphore(name)))
        self.dsems.append(d)
        return d

    def op(self, eng, fn, reads=(), writes=(), dma=None, par=False):
        deps = []
        for r in reads:
            if r.w is not None:
                deps.append(("raw", r.w))
        for w in writes:
            if w.w is not None:
                if not (par and dma is not None and w.w[0] == "d" and w.w[1] is dma):
                    deps.append(("waw", w.w))
            for t in w.r.values():
                deps.append(("war", t))
        self.clock += 1
        rc = _Rec()
        fn(rc)
        rec = dict(call=rc.call, deps=None, inc=False, dma=dma, clock=self.clock, eng=eng)
        if dma is None:
            tok = ("e", eng, rec)
        else:
            dma.count += 16
            tok = ("d", dma, dma.count)
        keep = []
        for kind, t in deps:
            if t[0] == "e" and t[1] == eng:
                if eng == "pe" or (kind != "raw" and not (self.strict or eng in ("pool", "act"))):
                    continue
            keep.append(t)
            if t[0] == "e":
                t[2]["inc"] = True
        rec["deps"] = keep
        if eng == "pe" and self.deferring:
            self.pending.append(rec)
        else:
            if eng == "pe" and self.pending and any(t[0] == "e" and t[2]["clock"] >= self.pend_clock for t in keep):
                self.pe_flush()
            self.ops[eng].append(rec)
        for r in reads:
            key = eng if dma is None else ("d", id(dma))
            r.r[key] = tok
        for w in writes:
            w.w = tok
            w.r = {}
        return tok

    def pe_defer_begin(self):
        if not self.pending:
            self.pend_clock = self.clock + 1
        self.deferring = True

    def pe_defer_end(self):
        self.deferring = False

    def pe_flush(self):
        self.ops["pe"].extend(self.pending)
        self.pending = []

    def emit(self, block):
        self.pe_flush()
        for e in ("pe", "act", "dve", "pool"):
            cnt = 0
            for rec in self.ops[e]:
                if rec["inc"] and rec["dma"] is None:
                    cnt += 1
                    rec["idx"] = cnt
        final = [(d.sem, d.count) for d in self.dsems if d.count > 0]

        def run(e, eng, tail=False):
            seen = {}
            for rec in self.ops[e]:
                need = {}
                for t in rec["deps"]:
                    if t[0] == "e":
                        sem = self.esem[t[1]]
                        val = t[2]["idx"]
                    else:
                        sem = t[1].sem
                        val = t[2]
                    k = id(sem)
                    if k not in need or need[k][1] < val:
                        need[k] = (sem, val)
                for k, (sem, val) in need.items():
                    if seen.get(k, 0) < val:
                        eng.wait_ge(sem, val)
                        seen[k] = val
                name, a, k = rec["call"]
                ins = getattr(eng, name)(*a, **k)
                if rec["dma"] is not None:
                    ins.then_inc(rec["dma"].sem, 16)
                elif rec["inc"]:
                    ins.then_inc(self.esem[e], 1)
            if tail:
                for sem, val in final:
                    eng.wait_ge(sem, val)

        @block.tensor
        def _(eng):
            run("pe", eng)

        @block.scalar
        def _(eng):
            run("act", eng)

        @block.vector
        def _(eng):
            run("dve", eng)

        @block.gpsimd
        def _(eng):
            run("pool", eng)

        @block.sync
        def _(eng):
            run("sp", eng, tail=True)


def build_program(cfg):
    nc = bass.Bass("TRN2", target_bir_lowering=False)
    es = ExitStack()
    S = Sched(nc, es)
    S.strict = bool(cfg.get("strict"))
    dbg = cfg.get("dbg")
    layers = cfg.get("layers", list(range(DEPTH)))
    groups_sel = cfg.get("groups", [0, 1])

    def din(name, shape):
        return nc.dram_tensor(name, list(shape), F32, kind="ExternalInput").ap()

    def dout(name, shape):
        return nc.dram_tensor(name, list(shape), F32, kind="ExternalOutput").ap()

    xp = din("xp", [2048, D])
    xs = din("xs", [128, D])
    st_conv = din("st_conv", [2, 16, 30, D])
    st_re = din("st_re", [16, 4096])
    st_im = din("st_im", [16, 4096])
    st_sc = din("st_sc", [16, 2, D])
    ck = din("ck", [4, 16, NMEM, D])
    cv = din("cv", [4, 16, NMEM, D])
    memp = din("memp", [NMEM, D])
    vecs = din("vecs", [128, D])
    a_w_pw1 = din("a_w_pw1", [2, D, 2 * D])
    a_w_pw2 = din("a_w_pw2", [2, D, D])
    b_lam_re = din("b_lam_re", [64, 64])
    b_lam_im = din("b_lam_im", [64, 64])
    b_log_dt = din("b_log_dt", [64])
    b_B_re = din("b_B_re", [64, 64, 16])
    b_B_im = din("b_B_im", [64, 64, 16])
    b_C_re = din("b_C_re", [64, 16, 64])
    b_C_im = din("b_C_im", [64, 16, 64])
    b_w_glu = din("b_w_glu", [D, 2 * D])
    c_w_in = din("c_w_in", [D, 3 * D])
    c_w_out = din("c_w_out", [D, D])
    x_w_q = din("x_w_q", [4, D, D])
    x_w_k = din("x_w_k", [4, D, D])
    x_w_v = din("x_w_v", [4, D, D])
    x_w_o = din("x_w_o", [4, D, D])
    f_w_gu = din("f_w_gu", [4, D, 2 * DFF])
    f_w_down = din("f_w_down", [4, DFF, D])

    yp = dout("yp", [2048, D])
    ys = dout("ys", [128, D])
    nk = dout("nk", [4, NMEM, D])
    nv = dout("nv", [4, NMEM, D])
    ncap = dout("ncap", [2, 30, D])
    ncas = dout("ncas", [2, 16, 30, D])
    nrep = dout("nrep", [1, 4096])
    nimp = dout("nimp", [1, 4096])
    nres = dout("nres", [16, 4096])
    nims = dout("nims", [16, 4096])
    nscp = dout("nscp", [2, D])
    nscs = dout("nscs", [16, 2, D])
    if dbg:
        dbg_out = dout("dbg_out", [128, NCH * GW])

    def sb(name, shape, dt):
        return es.enter_context(nc.sbuf_tensor(name, list(shape), dt))

    X = sb("X", [128, NCH, GW], F32)
    H = sb("H", [128, NCH, GW], BF16)
    MW = 2 + GW
    M = sb("M", [128, NCH, MW], F32)
    MF = M[:].rearrange("p c w -> p (c w)")
    BIGN = 13312
    BIG = sb("BIG", [128, BIGN], BF16)
    NSLOT = 3
    WSL = 4096
    WR = [sb(f"WR{i}", [128, WSL], BF16) for i in range(NSLOT)]
    KSB = [sb(f"KS{i}", [128, 2, D], BF16) for i in range(2)]
    KT = sb("KT", [128, NCH, NMEM], BF16)
    NV = cfg.get("nv", 2)
    VV = [sb(f"VV{i}", [128, 2, D], BF16) for i in range(NV)]
    NT = 6
    TT = [sb(f"TT{i}", [128, 512], F32) for i in range(NT)]
    PT = [sb(f"PT{i}", [128, 2, 512], BF16) for i in range(2)]
    CV = sb("CV", [128, NCH, 128], F32)
    IDF = sb("IDF", [128, 128], F32)
    IDB = sb("IDB", [128, 128], BF16)
    ONB = sb("ONB", [128, 128], BF16)
    MEMT = sb("MEMT", [128, NCH, NMEM], BF16)
    GSF = sb("GSF", [128, NCH, 160], F32)
    GPF = sb("GPF", [128, NCH, 32], F32)
    CARRY_G = [sb(f"CARG{i}", [128, NCH, 30], BF16) for i in range(2)]
    CARRY_P = sb("CARP", [128, NCH, 2], F32)
    CARRY_S = sb("CARS", [128, 32, 2], F32)
    LB = sb("LB", [128, 8, 2, 128], BF16)
    LC = sb("LC", [128, 32, 2, 32], BF16)
    S5P = sb("S5P", [128, 12, 32], F32)
    MASK = sb("MASK", [128, 128], F32)
    DUMMY = sb("DUMMY", [128, 4], F32)

    PS = [es.enter_context(nc.psum_tensor(f"ps{i}", [128, 512], F32)) for i in range(8)]
    PSR = [Res(f"ps{i}") for i in range(8)]
    ps_free = list(range(8))
    ps_rr = [0]

    ST_BANKS = [6, 7]
    stb_rr = [0]

    def psum_stat():
        i = ST_BANKS[stb_rr[0] % 2]
        stb_rr[0] += 1
        return PS[i], PSR[i]

    def psum():
        i = ps_free[ps_rr[0] % len(ps_free)]
        ps_rr[0] += 1
        return PS[i], PSR[i]

    def psum_hold():
        i = ps_free[ps_rr[0] % len(ps_free)]
        ps_free.remove(i)
        return i

    def psum_release(i):
        ps_free.append(i)
        ps_free.sort()

    class R:
        pass

    rX = {}
    rH = {}
    rM = {}

    def res(dct, key, name):
        if key not in dct:
            dct[key] = Res(f"{name}{key}")
        return dct[key]

    rMisc = {}

    def rm(name):
        return res(rMisc, name, "")

    wr_res = [Res(f"wr{i}") for i in range(NSLOT)]
    wr_sem = [S.dsem(f"wrs{i}") for i in range(NSLOT)]
    wr_rr = [0]

    def wslot():
        i = wr_rr[0] % NSLOT
        wr_rr[0] += 1
        return WR[i], wr_res[i], wr_sem[i]

    class Tile:
        def __init__(self, kind, col0, n, tok0, gi, ti):
            self.kind, self.col0, self.n, self.tok0, self.gi, self.ti = kind, col0, n, tok0, gi, ti
            self.cols = slice(col0, col0 + n)
            self.mcols = slice(2 + col0, 2 + col0 + n)

    GROUPS = [
        [Tile("P", 0, 512, 0, 0, 0), Tile("P", 512, 512, 512, 0, 1), Tile("S", 1024, 128, 0, 0, 2)],
        [Tile("P", 0, 512, 1024, 1, 0), Tile("P", 512, 512, 1536, 1, 1)],
    ]

    def rXt(t):
        return res(rX, t.ti, "X")

    def rHt(t):
        return res(rH, t.ti, "H")

    def rMt(t):
        return res(rM, t.ti, "M")

    def cvs(row, c):
        return CV[:, c, row:row + 1]

    rCV = rm("CV")
    rCONST = rm("CONST")

    def row_norm(l, i):
        return l * 6 + i
    ROW_BPW1 = 24
    ROW_WDW = 28
    ROW_BDW = 90
    ROW_LNG = 92
    ROW_LNB = 94
    ROW_BPW2 = 96
    ROW_BD = 98
    ROW_BGLU = 99
    ROW_WCONV = 101

    def setup_consts():
        S.op("pool", lambda e: e.memset(IDF[:], 0.0), writes=[rCONST])
        S.op("pool", lambda e: e.affine_select(out=IDF[:], in_=IDF[:], pattern=[[-1, 128]], compare_op=ALU.not_equal,
                                               fill=1.0, base=0, channel_multiplier=1), reads=[rCONST], writes=[rCONST])
        S.op("dve", lambda e: e.tensor_copy(out=IDB[:], in_=IDF[:]), reads=[rCONST], writes=[rm("IDB")])
        S.op("dve", lambda e: e.memset(ONB[:], 1.0), writes=[rm("ONB")])
        stg = M[:, 0:1, 0:D]
        sem = S.dsem("vecs")
        S.op("sp", lambda e: e.dma_start(out=M[:, 0, 0:D], in_=vecs), writes=[rm("Mstage")], dma=sem)
        for c in range(NCH):
            ps, pr = psum()
            S.op("pe", lambda e, c=c, ps=ps: e.transpose(out=ps[:, 0:128], in_=M[:, 0, c * 128:(c + 1) * 128], identity=IDF[:]),
                 reads=[rm("Mstage"), rCONST], writes=[pr])
            S.op("act", lambda e, c=c, ps=ps: e.copy(out=CV[:, c, :], in_=ps[:, 0:128]), reads=[pr], writes=[rCV])

    def load_x(tiles):
        for t in tiles:
            src = xp[t.tok0:t.tok0 + t.n, :] if t.kind == "P" else xs
            nb = t.n // 128
            stg = M[:, 0:4, 0:D]
            sem = S.dsem(f"ldx{t.gi}_{t.ti}")
            S.op("sp", lambda e, src=src, nb=nb: e.dma_start(out=M[:, 0:nb, 0:D], in_=src.rearrange("(b p) d -> p b d", p=128)),
                 writes=[rm("Mstage")], dma=sem)
            for c in range(NCH):
                ps, pr = psum()
                for b in range(nb):
                    S.op("pe", lambda e, c=c, b=b, ps=ps: e.transpose(out=ps[:, b * 128:(b + 1) * 128],
                                                                    in_=M[:, b, c * 128:(c + 1) * 128], identity=IDF[:]),
                         reads=[rm("Mstage"), rCONST], writes=[pr])
                S.op("act" if c % 2 else "dve",
                     (lambda e, c=c, ps=ps, t=t: e.copy(out=X[:, c, t.cols], in_=ps[:, 0:t.n])) if c % 2 else
                     (lambda e, c=c, ps=ps, t=t: e.tensor_copy(out=X[:, c, t.cols], in_=ps[:, 0:t.n])),
                     reads=[pr], writes=[rXt(t)])

    st_sems = [S.dsem("st0"), S.dsem("st1")]
    st_res = [Res("ost0"), Res("ost1")]
    st_rr = [0]

    def store_y(tiles):
        for t in tiles:
            dst = yp[t.tok0:t.tok0 + t.n, :] if t.kind == "P" else ys
            for b in range(t.n // 128):
                i = st_rr[0] % 2
                st_rr[0] += 1
                ost = M[:, i, 0:D]
                for hh in range(2):
                    ps, pr = psum()
                    for cc in range(4):
                        c = hh * 4 + cc
                        S.op("pe", lambda e, c=c, cc=cc, b=b, ps=ps, t=t: e.transpose(
                            out=ps[:, cc * 128:(cc + 1) * 128], in_=X[:, c, t.col0 + b * 128:t.col0 + (b + 1) * 128], identity=IDF[:]),
                            reads=[rXt(t), rCONST], writes=[pr])
                    S.op("act" if hh else "dve",
                         (lambda e, hh=hh, ps=ps, i=i: e.copy(out=M[:, i, hh * 512:(hh + 1) * 512], in_=ps[:, :])) if hh else
                         (lambda e, hh=hh, ps=ps, i=i: e.tensor_copy(out=M[:, i, hh * 512:(hh + 1) * 512], in_=ps[:, :])),
                         reads=[pr], writes=[st_res[i]])
                S.op("sp", lambda e, i=i, dst=dst, b=b: e.dma_start(out=dst[b * 128:(b + 1) * 128, :], in_=M[:, i, 0:D]),
                     reads=[st_res[i]], dma=st_sems[i])

    tt_res = [Res(f"tt{i}") for i in range(NT)]
    tt_sem = [S.dsem(f"tts{i}") for i in range(NT)]
    tt_rr = [0]

    def tmp():
        i = tt_rr[0] % NT
        tt_rr[0] += 1
        return TT[i], tt_res[i], tt_sem[i]

    def rstd_from(src_ap_fn, src_res, n, eps):
        slot, sres, _ = wslot()
        SQv = slot[:, 0:NCH * 512].rearrange("p (c w) -> p c w", c=NCH)
        S.op("act", lambda e: e.activation(out=SQv[:, :, 0:n], in_=src_ap_fn(), func=AF.Square), reads=[src_res], writes=[sres])
        ps, pr = psum()
        for c in range(NCH):
            S.op("pe", lambda e, c=c, ps=ps: e.matmul(ps[:, 0:n], lhsT=ONB[:], rhs=SQv[:, c, 0:n], start=(c == 0), stop=(c == NCH - 1)),
                 reads=[sres, rm("ONB")], writes=[pr])
        rb, rr_, _ = tmp()
        S.op("act", lambda e, ps=ps: e.activation(out=rb[:, 0:n], in_=ps[:, 0:n], func=AF.Sqrt, scale=1.0 / D, bias=EPSB[eps][:, 0:1]),
             reads=[pr, rCONST2], writes=[rr_])
        S.op("dve", lambda e: e.reciprocal(out=rb[:, 0:n], in_=rb[:, 0:n]), reads=[rr_], writes=[rr_])
        return rb, rr_

    EPSB = {RMS_EPS: sb("EPS1", [128, 1], F32), LN_EPS: sb("EPS2", [128, 1], F32)}
    HPI = sb("HPI", [128, 1], F32)
    rCONST2 = rm("CONST2")

    def setup_eps():
        S.op("dve", lambda e: e.memset(EPSB[RMS_EPS][:], RMS_EPS), writes=[rCONST2])
        S.op("dve", lambda e: e.memset(EPSB[LN_EPS][:], LN_EPS), writes=[rCONST2])
        S.op("dve", lambda e: e.memset(HPI[:], math.pi / 2), writes=[rCONST2])

    def pre_norm(t, grow):
        rb, rr_ = rstd_from(lambda: X[:, :, t.cols], rXt(t), t.n, RMS_EPS)
        for c in range(NCH):
            S.op("dve", lambda e, c=c: e.scalar_tensor_tensor(out=H[:, c, t.cols], in0=X[:, c, t.cols], scalar=cvs(grow, c),
                                                             in1=rb[:, 0:t.n], op0=ALU.mult, op1=ALU.mult),
                 reads=[rXt(t), rr_, rCV], writes=[rHt(t)])

    def norm_boundary(tiles, grow_post, grow_next):
        st = {}
        for t in tiles:
            S.op("act", lambda e, t=t: e.activation(out=H[:, :, t.cols], in_=M[:, :, t.mcols], func=AF.Square), reads=[rMt(t)], writes=[rHt(t)])
        for t in tiles:
            ps, pr = psum()
            for c in range(NCH):
                S.op("pe", lambda e, c=c, ps=ps, t=t: e.matmul(ps[:, 0:t.n], lhsT=ONB[:], rhs=H[:, c, t.cols], start=(c == 0), stop=(c == NCH - 1)),
                     reads=[rHt(t), rm("ONB")], writes=[pr])
            st[t.ti] = (ps, pr)
        rsb = {}
        for t in tiles:
            ps, pr = st[t.ti]
            rb, rr_, _ = tmp()
            rsb[t.ti] = (rb, rr_)
            S.op("act", lambda e, ps=ps, rb=rb, t=t: e.activation(out=rb[:, 0:t.n], in_=ps[:, 0:t.n], func=AF.Sqrt, scale=1.0 / D, bias=EPSB[RMS_EPS][:, 0:1]),
                 reads=[pr, rCONST2], writes=[rr_])

        def post_dve(t):
            rb, rr_ = rsb[t.ti]
            S.op("dve", lambda e: e.reciprocal(out=rb[:, 0:t.n], in_=rb[:, 0:t.n]), reads=[rr_], writes=[rr_])
            for c in range(NCH):
                S.op("dve", lambda e, c=c: e.scalar_tensor_tensor(out=M[:, c, t.mcols], in0=M[:, c, t.mcols], scalar=cvs(grow_post, c),
                                                                 in1=rb[:, 0:t.n], op0=ALU.mult, op1=ALU.mult),
                     reads=[rMt(t), rr_, rCV], writes=[rMt(t)])
            S.op("dve", lambda e: e.tensor_tensor(out=X[:, :, t.cols], in0=X[:, :, t.cols], in1=M[:, :, t.mcols], op=ALU.add),
                 reads=[rMt(t), rXt(t)], writes=[rXt(t)])
            if grow_next is None:
                return
            S.op("act", lambda e: e.activation(out=H[:, :, t.cols], in_=X[:, :, t.cols], func=AF.Square), reads=[rXt(t)], writes=[rHt(t)])
            ps, pr = psum()
            for c in range(NCH):
                S.op("pe", lambda e, c=c, ps=ps: e.matmul(ps[:, 0:t.n], lhsT=ONB[:], rhs=H[:, c, t.cols], start=(c == 0), stop=(c == NCH - 1)),
                     reads=[rHt(t), rm("ONB")], writes=[pr])
            rb2, rr2, _ = tmp()
            rsb[("n", t.ti)] = (rb2, rr2)
            S.op("act", lambda e: e.activation(out=rb2[:, 0:t.n], in_=ps[:, 0:t.n], func=AF.Sqrt, scale=1.0 / D, bias=EPSB[RMS_EPS][:, 0:1]),
                 reads=[pr, rCONST2], writes=[rr2])

        def pre_dve(t):
            rb2, rr2 = rsb[("n", t.ti)]
            S.op("dve", lambda e: e.reciprocal(out=rb2[:, 0:t.n], in_=rb2[:, 0:t.n]), reads=[rr2], writes=[rr2])
            for c in range(NCH):
                S.op("dve", lambda e, c=c: e.scalar_tensor_tensor(out=H[:, c, t.cols], in0=X[:, c, t.cols], scalar=cvs(grow_next, c),
                                                                 in1=rb2[:, 0:t.n], op0=ALU.mult, op1=ALU.mult),
                     reads=[rXt(t), rr2, rCV], writes=[rHt(t)])

        prev = None
        for t in tiles:
            post_dve(t)
            if prev is not None and grow_next is not None:
                pre_dve(prev)
            prev = t
        if grow_next is not None:
            pre_dve(prev)

    deferred = []

    def flush_deferred(t=None):
        while deferred and (t is None or deferred[0][0] is t):
            tt_, fn_ = deferred.pop(0)
            fn_(tt_)

    def proj(tiles, wparts, KC, rhs_fn, rhs_res_fn, jobs, epi, nblk, tile_outer=False, after_tile=None, defer_last=False):
        def load(blk):
            parts = wparts(blk)
            wtot = sum(p[2] for p in parts)
            slot, sres, ssem = wslot()
            sv = slot[:, 0:KC * wtot].rearrange("p (k w) -> p k w", k=KC)
            for pi, (src, off, w) in enumerate(parts):
                S.op("pool", lambda e, src=src, off=off, w=w, sv=sv: e.dma_start(
                    out=sv[:, :, off:off + w], in_=src.rearrange("(k p) w -> p k w", p=128)),
                    writes=[sres], dma=ssem, par=(pi > 0))
            return sv, sres

        def run(blk, sv, sres, t):
            for ji, job in enumerate(jobs(blk)):
                banks = [psum() for _ in job]
                for (ps, pr), off in zip(banks, job):
                    for k in range(KC):
                        S.op("pe", lambda e, k=k, off=off, ps=ps, t=t, sv=sv: e.matmul(
                            ps[:, 0:t.n], lhsT=sv[:, k, off:off + 128], rhs=rhs_fn(k, t), start=(k == 0), stop=(k == KC - 1)),
                            reads=[sres] + rhs_res_fn(t), writes=[pr])
                epi(blk, ji, t, [b[0] for b in banks], [b[1] for b in banks])

        if tile_outer:
            assert nblk <= NSLOT - 1
            loaded = [load(blk) for blk in range(nblk)]
            prev = None
            for t in tiles:
                flush_deferred(t)
                for blk in range(nblk):
                    run(blk, loaded[blk][0], loaded[blk][1], t)
                if prev is not None and after_tile is not None:
                    after_tile(prev)
                prev = t
            if after_tile is not None:
                if defer_last:
                    deferred.append((prev, after_tile))
                else:
                    after_tile(prev)
        else:
            for blk in range(nblk):
                sv, sres = load(blk)
                for t in tiles:
                    flush_deferred(t)
                    run(blk, sv, sres, t)
            if after_tile is not None:
                for t in tiles:
                    after_tile(t)

    S5N = 10
    s5_res = [Res(f"s5scr{i}") for i in range(S5N)]

    def all_M_res():
        extra = [rm(x) for x in ("SRB", "SIB", "MSKM", "INI", "NSN", "COS1", "SIN1", "NSN1", "INI2", "INI3", "INI4", "INI5", "INIr", "INIi")]
        return [rm("Mstage")] + [res(rM, k, "M") for k in range(3)] + s5_res + [rm("Mhist")] + extra

    def join_M():
        S.op("dve", lambda e: e.memset(DUMMY[:, 0:1], 0.0), writes=all_M_res())

    def join_BIG():
        rl = [res(rMisc, ("BIG", k), "BIG") for k in range(3)] + [res(rMisc, ("Q", k, h), "Q") for k in range(3) for h in range(4)]
        rl += [res(rMisc, ("G", k), "G") for k in range(3)] + [rm("GPhist"), rm("GShist"), rm("AS0")]
        rl += [rm("GSF"), rm("PSH"), rm("SOUT")]
        S.op("dve", lambda e: e.memset(DUMMY[:, 2:3], 0.0), writes=rl)

    out_sems = {}

    def osem(name):
        if name not in out_sems:
            out_sems[name] = S.dsem("o_" + name)
        return out_sems[name]

    def act_copy(out, in_, reads, writes, scale=None):
        if scale is None:
            S.op("act", lambda e: e.copy(out=out, in_=in_), reads=reads, writes=writes)
        else:
            S.op("act", lambda e: e.mul(out=out, in_=in_, mul=scale), reads=reads, writes=writes)

    def dve_copy(out, in_, reads, writes):
        S.op("dve", lambda e: e.tensor_copy(out=out, in_=in_), reads=reads, writes=writes)

    cp_rr = [0]

    def any_copy(out, in_, reads, writes):
        act_copy(out, in_, reads, writes)

    def transpose_out(src_fn, nrows, nchunks_src_res, dst_ap, name):
        for hh in range(2):
            ps, pr = psum()
            for cc in range(4):
                c = hh * 4 + cc
                S.op("pe", lambda e, c=c, cc=cc, ps=ps: e.transpose(out=ps[0:nrows, cc * 128:(cc + 1) * 128], in_=src_fn(c),
                                                                  identity=IDF[:]),
                     reads=nchunks_src_res + [rCONST], writes=[pr])
            tb, tr, tsem = tmp()
            any_copy(tb[0:nrows, :], ps[0:nrows, :], [pr], [tr])
            S.op("sp", lambda e, tb=tb, hh=hh: e.dma_start(out=dst_ap[:, hh * 512:(hh + 1) * 512], in_=tb[0:nrows, :]),
                 reads=[tr], dma=tsem)

    def rBt(t):
        return res(rMisc, ("BIG", t.ti), "BIG")

    def ffn(l, gi, tiles, pre_done, nxt, dl=False):
        if not pre_done:
            for t in tiles:
                pre_norm(t, row_norm(l, 4))
        join_BIG()
        AT = BIG[:, 0:11 * GW].rearrange("p (f w) -> p f w", f=11)
        for half in range(2):
            def wparts(blk, half=half):
                f0 = half * 11 + blk * 2
                nf = min(2, 11 - blk * 2)
                w = nf * 128
                return [(f_w_gu[l][:, f0 * 128:f0 * 128 + w], 0, w), (f_w_gu[l][:, DFF + f0 * 128:DFF + f0 * 128 + w], w, w)]

            def jobs(blk):
                nf = min(2, 11 - blk * 2)
                return [[i * 128, nf * 128 + i * 128] for i in range(nf)]

            def epi(blk, ji, t, pss, prs):
                fi = blk * 2 + ji
                tb, tr, _ = tmp()
                S.op("act", lambda e: e.activation(out=tb[:, 0:t.n], in_=pss[0][:, 0:t.n], func=AF.Silu), reads=[prs[0]], writes=[tr])
                S.op("dve", lambda e: e.tensor_tensor(out=AT[:, fi, t.cols], in0=tb[:, 0:t.n], in1=pss[1][:, 0:t.n], op=ALU.mult),
                     reads=[tr, prs[1]], writes=[rBt(t)])
            proj(tiles, wparts, 8, lambda k, t: H[:, k, t.cols], lambda t: [rHt(t)], jobs, epi, 6)

            def wparts2(blk, half=half):
                return [(f_w_down[l][half * 1408:(half + 1) * 1408, blk * 256:(blk + 1) * 256], 0, 256)]

            def epi2(blk, ji, t, pss, prs, half=half):
                c = blk * 2 + ji
                if half == 0:
                    act_copy(M[:, c, t.mcols], pss[0][:, 0:t.n], [prs[0]], [rMt(t)])
                else:
                    S.op("dve", lambda e: e.tensor_tensor(out=M[:, c, t.mcols], in0=M[:, c, t.mcols], in1=pss[0][:, 0:t.n], op=ALU.add),
                         reads=[prs[0], rMt(t)], writes=[rMt(t)])
            proj(tiles, wparts2, 11, lambda k, t: AT[:, k, t.cols], lambda t: [rBt(t)], lambda blk: [[0], [128]], epi2, 4)
        norm_boundary(tiles, row_norm(l, 5), nxt)

    rMEMT = rm("MEMT")
    rNK = [[Res(f"nk{l}_{i}") for i in range(4)] for l in range(DEPTH)]
    rNV = [[Res(f"nv{l}_{i}") for i in range(4)] for l in range(DEPTH)]
    rKSB, rKT = [Res("ks0"), Res("ks1")], rm("KT")
    rVV = [Res(f"vv{i}") for i in range(NV)]
    ks_sems = [S.dsem("ks0"), S.dsem("ks1")]
    ks_rr = [0]
    vv_sem = [S.dsem(f"vvs{i}") for i in range(NV)]
    vv_rr = [0]
    rPT = [Res("pt0"), Res("pt1")]
    pt_rr = [0]

    def setup_memT():
        sem = S.dsem("memp")
        S.op("sp", lambda e: e.dma_start(out=M[:, 0:2, 0:D], in_=memp.rearrange("(b p) d -> p b d", p=128)),
             writes=[rm("Mstage")], dma=sem)
        for c in range(NCH):
            ps, pr = psum()
            for b in range(2):
                S.op("pe", lambda e, c=c, b=b, ps=ps: e.transpose(out=ps[:, b * 128:(b + 1) * 128], in_=M[:, b, c * 128:(c + 1) * 128],
                                                                identity=IDF[:]), reads=[rm("Mstage"), rCONST], writes=[pr])
            any_copy(MEMT[:, c, :], ps[:, 0:NMEM], [pr], [rMEMT])

    def kv_project(l):
        for (W, dst, rdst) in ((x_w_k, nk, rNK), (x_w_v, nv, rNV)):
            for blk in range(2):
                slot, sres, ssem = wslot()
                sv = slot[:, 0:8 * 512].rearrange("p (k w) -> p k w", k=8)
                S.op("pool", lambda e, W=W, blk=blk, sv=sv: e.dma_start(
                    out=sv, in_=W[l][:, blk * 512:(blk + 1) * 512].rearrange("(k p) w -> p k w", p=128)), writes=[sres], dma=ssem)
                for mc in range(2):
                    ps, pr = psum()
                    for k in range(NCH):
                        S.op("pe", lambda e, k=k, mc=mc, ps=ps, sv=sv: e.matmul(ps[:, :], lhsT=MEMT[:, k, mc * 128:(mc + 1) * 128],
                                                                             rhs=sv[:, k, :], start=(k == 0), stop=(k == NCH - 1)),
                             reads=[sres, rMEMT], writes=[pr])
                    tb, tr, tsem = tmp()
                    any_copy(tb[:, :], ps[:, :], [pr], [tr])
                    S.op("sp", lambda e, dst=dst, mc=mc, blk=blk, tb=tb: e.dma_start(
                        out=dst[l][mc * 128:(mc + 1) * 128, blk * 512:(blk + 1) * 512], in_=tb[:, :]),
                        reads=[tr], writes=[rdst[l][mc * 2 + blk]], dma=tsem)

    def load_kv(ksrc, vsrc, src_res):
        vi = vv_rr[0] % NV
        vv_rr[0] += 1
        ki = ks_rr[0] % 2
        ks_rr[0] += 1
        KS, rKS, ks_sem = KSB[ki], rKSB[ki], ks_sems[ki]
        S.op("pool", lambda e: e.dma_start(out=KS[:], in_=ksrc.rearrange("(b p) d -> p b d", p=128)), reads=src_res, writes=[rKS], dma=ks_sem)
        S.op("pool", lambda e: e.dma_start(out=VV[vi][:], in_=vsrc.rearrange("(b p) d -> p b d", p=128)), reads=src_res, writes=[rVV[vi]],
             dma=vv_sem[vi])
        for half in range(2):
            ps, pr = psum()
            psb = ps[:].bitcast(BF16).rearrange("p (f m) -> p f m", f=4)
            for fcc in range(4):
                for mc in range(2):
                    S.op("pe", lambda e, fcc=fcc, mc=mc, psb=psb, half=half: e.transpose(
                        out=psb[:, fcc, mc * 128:(mc + 1) * 128], in_=KS[:, mc, (half * 4 + fcc) * 128:(half * 4 + fcc + 1) * 128],
                        identity=IDB[:]), reads=[rKS, rm("IDB")], writes=[pr])
            any_copy(KT[:, half * 4:(half + 1) * 4, :], psb, [pr], [rKT])
        return vi

    def rQ(t, h):
        return res(rMisc, ("Q", t.ti, h), "Q")

    def attention(l, gi, tiles, pre_done, nxt, dl=False):
        if gi == 0:
            kv_project(l)
        if not pre_done:
            for t in tiles:
                pre_norm(t, row_norm(l, 2))
        join_BIG()
        QT = BIG[:, 0:8 * GW].rearrange("p (c w) -> p c w", c=8)

        def epi(blk, ji, t, pss, prs):
            c = blk * 4 + ji
            act_copy(QT[:, c, t.cols], pss[0][:, 0:t.n], [prs[0]], [rQ(t, c // 2)], scale=0.0625)
        proj(tiles, lambda blk: [(x_w_q[l][:, blk * 512:(blk + 1) * 512], 0, 512)], 8, lambda k, t: H[:, k, t.cols],
             lambda t: [rHt(t)], lambda blk: [[0], [128], [256], [384]], epi, 2)

        ptiles = [t for t in tiles if t.kind == "P"]
        stiles = [t for t in tiles if t.kind == "S"]
        if ptiles:
            vi = load_kv(nk[l], nv[l], rNK[l] + rNV[l])
            for t in ptiles:
                n = t.n
                for h in range(4):
                    pi = pt_rr[0] % 2
                    pt_rr[0] += 1
                    sc = [psum(), psum()]
                    for mc in range(2):
                        for dc in range(2):
                            S.op("pe", lambda e, mc=mc, dc=dc, ps=sc[mc][0]: e.matmul(
                                ps[:, 0:n], lhsT=KT[:, 2 * h + dc, mc * 128:(mc + 1) * 128], rhs=QT[:, 2 * h + dc, t.cols],
                                start=(dc == 0), stop=(dc == 1)), reads=[rKT, rQ(t, h)], writes=[sc[mc][1]])
                        S.op("act", lambda e, mc=mc, ps=sc[mc][0], pi=pi: e.activation(out=PT[pi][:, mc, 0:n], in_=ps[:, 0:n], func=AF.Exp),
                             reads=[sc[mc][1]], writes=[rPT[pi]])
                    pss, prs = psum()
                    for mc in range(2):
                        S.op("pe", lambda e, mc=mc, pi=pi: e.matmul(pss[:, 0:n], lhsT=ONB[:], rhs=PT[pi][:, mc, 0:n],
                                                                  start=(mc == 0), stop=(mc == 1)), reads=[rPT[pi], rm("ONB")], writes=[prs])
                    tb, tr, _ = tmp()
                    S.op("dve", lambda e, tb=tb: e.reciprocal(out=tb[:, 0:n], in_=pss[:, 0:n]), reads=[prs], writes=[tr])
                    for dc in range(2):
                        po, pro = psum()
                        for mc in range(2):
                            S.op("pe", lambda e, mc=mc, dc=dc, po=po, pi=pi: e.matmul(
                                po[:, 0:n], lhsT=VV[vi][:, mc, (2 * h + dc) * 128:(2 * h + dc + 1) * 128], rhs=PT[pi][:, mc, 0:n],
                                start=(mc == 0), stop=(mc == 1)), reads=[rVV[vi], rPT[pi]], writes=[pro])
                        S.op("dve", lambda e, dc=dc, po=po, tb=tb: e.tensor_tensor(out=QT[:, 2 * h + dc, t.cols], in0=po[:, 0:n],
                                                                               in1=tb[:, 0:n], op=ALU.mult),
                             reads=[pro, tr], writes=[rQ(t, h)])
        for t in stiles:
            hb = [psum_hold() for _ in range(3)]
            rhold = [PSR[i] for i in hb]
            for b in range(16):
                vi = load_kv(ck[l][b], cv[l][b], [])
                pi = pt_rr[0] % 2
                pt_rr[0] += 1
                pts = PT[pi][:, 0, 0:64]
                ps, pr = psum()
                for h in range(4):
                    for mc in range(2):
                        for dc in range(2):
                            S.op("pe", lambda e, h=h, mc=mc, dc=dc, ps=ps, b=b: e.matmul(
                                ps[:, (mc * 4 + h) * 8:(mc * 4 + h + 1) * 8], lhsT=KT[:, 2 * h + dc, mc * 128:(mc + 1) * 128],
                                rhs=QT[:, 2 * h + dc, t.col0 + b * 8:t.col0 + (b + 1) * 8], start=(dc == 0), stop=(dc == 1)),
                                reads=[rKT, rQ(t, h)], writes=[pr])
                S.op("act", lambda e, ps=ps, pts=pts: e.activation(out=pts, in_=ps[:, 0:64], func=AF.Exp), reads=[pr], writes=[rPT[pi]])
                for mc in range(2):
                    S.op("pe", lambda e, mc=mc, b=b, pts=pts: e.matmul(PS[hb[2]][:, b * 32:(b + 1) * 32], lhsT=ONB[:],
                                                                      rhs=pts[:, mc * 32:(mc + 1) * 32], start=(mc == 0), stop=(mc == 1)),
                         reads=[rPT[pi], rm("ONB")], writes=[rhold[2]])
                for h in range(4):
                    for dc in range(2):
                        col = ((h % 2) * 2 + dc) * 128 + b * 8
                        for mc in range(2):
                            S.op("pe", lambda e, h=h, dc=dc, mc=mc, col=col, vi=vi, pts=pts: e.matmul(
                                PS[hb[h // 2]][:, col:col + 8], lhsT=VV[vi][:, mc, (2 * h + dc) * 128:(2 * h + dc + 1) * 128],
                                rhs=pts[:, (mc * 4 + h) * 8:(mc * 4 + h + 1) * 8], start=(mc == 0), stop=(mc == 1)),
                                reads=[rVV[vi], rPT[pi]], writes=[rhold[h // 2]])
            tb, tr, _ = tmp()
            S.op("dve", lambda e, tb=tb: e.reciprocal(out=tb[:, :], in_=PS[hb[2]][:, :]), reads=[rhold[2]], writes=[tr])
            rsv = tb[:, :].rearrange("p (b h t) -> p b h t", b=16, h=4)
            for h in range(4):
                for dc in range(2):
                    col = ((h % 2) * 2 + dc) * 128
                    S.op("dve", lambda e, h=h, dc=dc, col=col, rsv=rsv: e.tensor_tensor(
                        out=QT[:, 2 * h + dc, t.cols].rearrange("p (b t) -> p b t", b=16),
                        in0=PS[hb[h // 2]][:, col:col + 128].rearrange("p (b t) -> p b t", b=16), in1=rsv[:, :, h, :], op=ALU.mult),
                        reads=[rhold[h // 2], tr], writes=[rQ(t, h)])
            for i in hb:
                psum_release(i)

        def epi2(blk, ji, t, pss, prs):
            c = blk * 4 + ji
            act_copy(M[:, c, t.mcols], pss[0][:, 0:t.n], [prs[0]], [rMt(t)])
        proj(tiles, lambda blk: [(x_w_o[l][:, blk * 512:(blk + 1) * 512], 0, 512)], 8, lambda k, t: QT[:, k, t.cols],
             lambda t: [rQ(t, h) for h in range(4)], lambda blk: [[0], [128], [256], [384]], epi2, 2)
        norm_boundary(tiles, row_norm(l, 3), nxt)

    GPW = 30 + 1024
    GP = BIG[:, 0:8 * GPW].rearrange("p (c w) -> p c w", c=8)
    GS = BIG[:, 8 * GPW:8 * GPW + 8 * 16 * 38].rearrange("p (c b w) -> p c b w", c=8, b=16)
    rGPh, rGSh = rm("GPhist"), rm("GShist")
    rGSF, rGPF = rm("GSF"), rm("GPF")
    rCARG = [Res("carg0"), Res("carg1")]

    def rG(t):
        return res(rMisc, ("G", t.ti), "G")

    def mixer0(l, gi, tiles, pre_done, nxt, dl=False):
        flush_deferred()
        j = l // 3
        if not pre_done:
            for t in tiles:
                pre_norm(t, row_norm(l, 0))
        join_BIG()
        join_M()
        has_s = any(t.kind == "S" for t in tiles)
        if gi == 0:
            S.op("pool", lambda e: e.memset(GP[:, :, 0:30], 0.0), writes=[rGPh])
        else:
            S.op("pool", lambda e: e.tensor_copy(out=GP[:, :, 0:30], in_=CARRY_G[j][:]), reads=[rCARG[j]], writes=[rGPh])
        if has_s:
            sem = S.dsem(f"stc{j}")
            MS = MF[:, 0:4 * D].rearrange("p (b d) -> p b d", b=4)
            src = st_conv[j].rearrange("b j d -> (b j) d")
            S.op("sp", lambda e: e.dma_start(out=MS[:, 0:3, :], in_=src[0:384, :].rearrange("(b p) d -> p b d", p=128)),
                 writes=[rm("Mstage")], dma=sem)
            S.op("sp", lambda e: e.dma_start(out=MS[0:96, 3, :], in_=src[384:480, :]), writes=[rm("Mstage")], dma=sem, par=True)
            for c in range(NCH):
                ps, pr = psum()
                for rb in range(4):
                    nr = 128 if rb < 3 else 96
                    S.op("pe", lambda e, c=c, rb=rb, nr=nr, ps=ps: e.transpose(out=ps[:, rb * 128:rb * 128 + nr],
                                                                             in_=MS[0:nr, rb, c * 128:(c + 1) * 128], identity=IDF[0:nr, 0:nr]),
                         reads=[rm("Mstage"), rCONST], writes=[pr])
                any_copy(GS[:, c, :, 0:30], ps[:, 0:480].rearrange("p (b w) -> p b w", b=16), [pr], [rGSh])
            osm = osem(f"ncas_cp{j}")
            S.op("sp", lambda e: e.dma_start(out=ncas[j][:, 0:22, :], in_=st_conv[j][:, 8:30, :]), dma=osm)
            join_M()

        def wparts(blk):
            return [(a_w_pw1[j][:, blk * 256:(blk + 1) * 256], 0, 256), (a_w_pw1[j][:, D + blk * 256:D + (blk + 1) * 256], 256, 256)]

        def epi(blk, ji, t, pss, prs):
            c = blk * 2 + ji
            n = t.n
            tb, tr, _ = tmp()
            S.op("act", lambda e: e.activation(out=tb[:, 0:n], in_=pss[1][:, 0:n], func=AF.Sigmoid, bias=cvs(ROW_BPW1 + 2 * j + 1, c)),
                 reads=[prs[1], rCV], writes=[tr])
            if t.kind == "P":
                S.op("dve", lambda e: e.scalar_tensor_tensor(out=GP[:, c, 30 + t.col0:30 + t.col0 + n], in0=pss[0][:, 0:n],
                                                           scalar=cvs(ROW_BPW1 + 2 * j, c), in1=tb[:, 0:n], op0=ALU.add, op1=ALU.mult),
                     reads=[prs[0], tr, rCV], writes=[rG(t)])
                if t.tok0 + n == 2048:
                    S.op("dve", lambda e: e.scalar_tensor_tensor(out=GPF[:, c, 0:30], in0=pss[0][:, n - 30:n], scalar=cvs(ROW_BPW1 + 2 * j, c),
                                                               in1=tb[:, n - 30:n], op0=ALU.add, op1=ALU.mult),
                         reads=[prs[0], tr, rCV], writes=[rGPF])
            else:
                S.op("dve", lambda e: e.scalar_tensor_tensor(out=GSF[:, c, 0:128], in0=pss[0][:, 0:n], scalar=cvs(ROW_BPW1 + 2 * j, c),
                                                           in1=tb[:, 0:n], op0=ALU.add, op1=ALU.mult),
                     reads=[prs[0], tr, rCV], writes=[rGSF])
                S.op("dve", lambda e: e.tensor_copy(out=GS[:, c, :, 30:38], in_=GSF[:, c, 0:128].rearrange("p (b t) -> p b t", b=16)),
                     reads=[rGSF], writes=[rG(t)])
        proj(tiles, wparts, 8, lambda k, t: H[:, k, t.cols], lambda t: [rHt(t)], lambda blk: [[0, 256], [128, 384]], epi, 4)

        ptiles = [t for t in tiles if t.kind == "P"]
        if gi == 0:
            last = ptiles[-1]
            S.op("pool", lambda e: e.tensor_copy(out=CARRY_G[j][:], in_=GP[:, :, 1024:1054]), reads=[rG(last)], writes=[rCARG[j]])
        if has_s:
            for hh in range(2):
                ps, pr = psum()
                for cc in range(4):
                    c = hh * 4 + cc
                    S.op("pe", lambda e, c=c, cc=cc, ps=ps: e.transpose(out=ps[:, cc * 128:(cc + 1) * 128], in_=GSF[:, c, 0:128], identity=IDF[:]),
                         reads=[rGSF, rCONST], writes=[pr])
                tb, tr, tsem = tmp()
                any_copy(tb[:, :], ps[:, :], [pr], [tr])
                for b in range(16):
                    S.op("sp", lambda e, tb=tb, hh=hh, b=b: e.dma_start(out=ncas[j][b, 22:30, hh * 512:(hh + 1) * 512], in_=tb[b * 8:(b + 1) * 8, :]),
                         reads=[tr], dma=tsem, par=(b > 0))
        if gi == 1:
            for hh in range(2):
                ps, pr = psum()
                for cc in range(4):
                    c = hh * 4 + cc
                    S.op("pe", lambda e, c=c, cc=cc, ps=ps: e.transpose(out=ps[0:30, cc * 128:(cc + 1) * 128], in_=GPF[:, c, 0:30], identity=IDF[:]),
                         reads=[rGPF, rCONST], writes=[pr])
                tb, tr, tsem = tmp()
                any_copy(tb[0:30, :], ps[0:30, :], [pr], [tr])
                S.op("sp", lambda e, tb=tb, hh=hh: e.dma_start(out=ncap[j][:, hh * 512:(hh + 1) * 512], in_=tb[0:30, :]), reads=[tr], dma=tsem)

        hist_res = {"P": rGPh, "S": rGSh}
        for c in range(NCH):
            slot, sres, ssem = wslot()
            DG = slot[:, 0:31 * 128].rearrange("p (k m) -> p k m", k=31)
            r0 = ROW_WDW + 31 * j
            S.op("dve", lambda e, c=c, DG=DG: e.tensor_tensor(out=DG, in0=IDB[:].unsqueeze(1).to_broadcast([128, 31, 128]),
                                                            in1=CV[:, c, r0:r0 + 31].unsqueeze(2).to_broadcast([128, 31, 128]), op=ALU.mult),
                 reads=[rm("IDB"), rCV], writes=[sres])
            for ti, t in enumerate(tiles):
                ps, pr = psum()
                rd = [sres, rG(t), hist_res[t.kind]] + ([rG(tiles[ti - 1])] if (t.kind == "P" and ti > 0) else [])
                for k in range(31):
                    if t.kind == "P":
                        rhs = (lambda k=k, c=c, t=t: GP[:, c, t.col0 + k:t.col0 + k + t.n])
                    else:
                        rhs = (lambda k=k, c=c: GS[:, c, :, k:k + 8])
                    S.op("pe", lambda e, k=k, ps=ps, rhs=rhs, DG=DG, t=t: e.matmul(ps[:, 0:t.n], lhsT=DG[:, k, :], rhs=rhs(),
                                                                               start=(k == 0), stop=(k == 30)), reads=rd, writes=[pr])
                S.op("act", lambda e, c=c, ps=ps, t=t: e.activation(out=M[:, c, t.mcols], in_=ps[:, 0:t.n], func=AF.Identity,
                                                                  bias=cvs(ROW_BDW + j, c)), reads=[pr, rCV], writes=[rMt(t)])
        lnb = {}
        for t in tiles:
            slot, sres, _ = wslot()
            SQv = slot[:, 0:NCH * 512].rearrange("p (c w) -> p c w", c=NCH)
            S.op("act", lambda e, t=t, SQv=SQv: e.activation(out=SQv[:, :, 0:t.n], in_=M[:, :, t.mcols], func=AF.Square), reads=[rMt(t)], writes=[sres])
            S.op("dve", lambda e, t=t: e.tensor_copy(out=H[:, :, t.cols], in_=M[:, :, t.mcols]), reads=[rMt(t)], writes=[rHt(t)])
            lnb[t.ti] = (SQv, sres)
        for t in tiles:
            SQv, sres = lnb[t.ti]
            psq, prq = psum()
            pmu, prm = psum()
            for c in range(NCH):
                S.op("pe", lambda e, c=c, t=t, psq=psq, SQv=SQv: e.matmul(psq[:, 0:t.n], lhsT=ONB[:], rhs=SQv[:, c, 0:t.n], start=(c == 0), stop=(c == NCH - 1)),
                     reads=[sres, rm("ONB")], writes=[prq])
            for c in range(NCH):
                S.op("pe", lambda e, c=c, t=t, pmu=pmu: e.matmul(pmu[:, 0:t.n], lhsT=ONB[:], rhs=H[:, c, t.cols], start=(c == 0), stop=(c == NCH - 1)),
                     reads=[rHt(t), rm("ONB")], writes=[prm])
            lnb[t.ti] = (psq, prq, pmu, prm)
        for t in tiles:
            n = t.n
            psq, prq, pmu, prm = lnb[t.ti]
            mu, rmu, _ = tmp()
            tv, rtv, _ = tmp()
            S.op("dve", lambda e, t=t, mu=mu, pmu=pmu: e.tensor_scalar(out=mu[:, 0:t.n], in0=pmu[:, 0:t.n], scalar1=1.0 / D, scalar2=None, op0=ALU.mult),
                 reads=[prm], writes=[rmu])
            S.op("dve", lambda e, t=t, mu=mu, tv=tv: e.tensor_tensor(out=tv[:, 0:t.n], in0=mu[:, 0:t.n], in1=mu[:, 0:t.n], op=ALU.mult),
                 reads=[rmu], writes=[rtv])
            S.op("dve", lambda e, t=t, tv=tv, psq=psq: e.scalar_tensor_tensor(out=tv[:, 0:t.n], in0=psq[:, 0:t.n], scalar=1.0 / D, in1=tv[:, 0:t.n],
                                                                          op0=ALU.mult, op1=ALU.subtract), reads=[prq, rtv], writes=[rtv])
            RSb, rRSb, _ = tmp()
            S.op("act", lambda e, t=t, tv=tv, RSb=RSb: e.activation(out=RSb[:, 0:t.n], in_=tv[:, 0:t.n], func=AF.Sqrt, bias=EPSB[LN_EPS][:, 0:1]),
                 reads=[rtv, rCONST2], writes=[rRSb])
            S.op("dve", lambda e, t=t, RSb=RSb: e.reciprocal(out=RSb[:, 0:t.n], in_=RSb[:, 0:t.n]), reads=[rRSb], writes=[rRSb])
            S.op("dve", lambda e, t=t, mu=mu: e.tensor_tensor(out=M[:, :, t.mcols], in0=M[:, :, t.mcols],
                                                            in1=mu[:, 0:t.n].unsqueeze(1).to_broadcast([128, NCH, t.n]), op=ALU.subtract),
                 reads=[rMt(t), rmu], writes=[rMt(t)])
            for c in range(NCH):
                S.op("dve", lambda e, c=c, t=t, RSb=RSb: e.scalar_tensor_tensor(out=M[:, c, t.mcols], in0=M[:, c, t.mcols], scalar=cvs(ROW_LNG + j, c),
                                                                     in1=RSb[:, 0:t.n], op0=ALU.mult, op1=ALU.mult),
                     reads=[rMt(t), rRSb, rCV], writes=[rMt(t)])
                S.op("act", lambda e, c=c, t=t: e.activation(out=H[:, c, t.cols], in_=M[:, c, t.mcols], func=AF.Silu, bias=cvs(ROW_LNB + j, c)),
                     reads=[rMt(t), rCV], writes=[rHt(t)])

        def epi2(blk, ji, t, pss, prs):
            c = blk * 4 + ji
            S.op("act", lambda e: e.activation(out=M[:, c, t.mcols], in_=pss[0][:, 0:t.n], func=AF.Identity, bias=cvs(ROW_BPW2 + j, c)),
                 reads=[prs[0], rCV], writes=[rMt(t)])
        proj(tiles, lambda blk: [(a_w_pw2[j][:, blk * 512:(blk + 1) * 512], 0, 512)], 8, lambda k, t: H[:, k, t.cols],
             lambda t: [rHt(t)], lambda blk: [[0], [128], [256], [384]], epi2, 2)
        norm_boundary(tiles, row_norm(l, 1), nxt)

    PSH = GSF[:, :, 0:160].rearrange("p c (b w) -> p c b w", b=16)
    rPSH = rm("PSH")
    rCARP = rm("CARP")

    def mixer2(l, gi, tiles, pre_done, nxt, dl=False):
        flush_deferred()
        if not pre_done:
            for t in tiles:
                pre_norm(t, row_norm(l, 0))
        join_BIG()
        join_M()
        has_s = any(t.kind == "S" for t in tiles)
        rMh = rm("Mhist")
        if has_s:
            sem = S.dsem("stsc")
            S.op("sp", lambda e: e.dma_start(out=MF[0:32, 0:D], in_=st_sc.rearrange("b j d -> (b j) d")), writes=[rm("Mstage")], dma=sem)
            ps, pr = psum()
            for c in range(NCH):
                S.op("pe", lambda e, c=c, ps=ps: e.transpose(out=ps[:, c * 32:(c + 1) * 32], in_=MF[0:32, c * 128:(c + 1) * 128], identity=IDF[0:32, 0:32]),
                     reads=[rm("Mstage"), rCONST], writes=[pr])
            for c in range(NCH):
                any_copy(PSH[:, c, :, 0:2], ps[:, c * 32:(c + 1) * 32].rearrange("p (b w) -> p b w", b=16), [pr], [rPSH])
            join_M()
        if gi == 0:
            S.op("pool", lambda e: e.memset(M[:, :, 0:2], 0.0), writes=[rMh])
        else:
            S.op("pool", lambda e: e.tensor_copy(out=M[:, :, 0:2], in_=CARRY_P[:]), reads=[rCARP], writes=[rMh])

        def wparts(blk):
            return [(c_w_in[:, D + blk * 256:D + (blk + 1) * 256], 0, 256), (c_w_in[:, 2 * D + blk * 256:2 * D + (blk + 1) * 256], 256, 256)]

        def epi(blk, ji, t, pss, prs):
            c = blk * 2 + ji
            n = t.n
            tb, tr, _ = tmp()
            act_copy(tb[:, 0:n], pss[0][:, 0:n], [prs[0]], [tr])
            if t.kind == "P":
                S.op("dve", lambda e: e.tensor_tensor(out=M[:, c, t.mcols], in0=tb[:, 0:n], in1=pss[1][:, 0:n], op=ALU.mult),
                     reads=[tr, prs[1]], writes=[rMt(t)])
            else:
                S.op("dve", lambda e: e.tensor_tensor(out=PSH[:, c, :, 2:10], in0=tb[:, 0:n].rearrange("p (b t) -> p b t", b=16),
                                                    in1=pss[1][:, 0:n].rearrange("p (b t) -> p b t", b=16), op=ALU.mult),
                     reads=[tr, prs[1]], writes=[rPSH])
        proj(tiles, wparts, 8, lambda k, t: H[:, k, t.cols], lambda t: [rHt(t)], lambda blk: [[0, 256], [128, 384]], epi, 4)
        ptiles = [t for t in tiles if t.kind == "P"]
        lastp = ptiles[-1]
        lc = 2 + lastp.col0 + lastp.n
        if gi == 0:
            S.op("pool", lambda e: e.tensor_copy(out=CARRY_P[:], in_=M[:, :, lc - 2:lc]), reads=[rMt(lastp)], writes=[rCARP])
        if has_s:
            tb, tr, _ = tmp()
            tbv = tb[:, 0:256].rearrange("p (c b w) -> p c b w", c=8, b=16)
            S.op("dve", lambda e: e.tensor_copy(out=tbv, in_=PSH[:, :, :, 8:10]), reads=[rPSH], writes=[tr])
            for hh in range(2):
                ps, pr = psum()
                for cc in range(4):
                    c = hh * 4 + cc
                    S.op("pe", lambda e, c=c, cc=cc, ps=ps, tb=tb: e.transpose(out=ps[0:32, cc * 128:(cc + 1) * 128], in_=tb[:, c * 32:(c + 1) * 32],
                                                                             identity=IDF[:]), reads=[tr, rCONST], writes=[pr])
                t2, tr2, tsem2 = tmp()
                any_copy(t2[0:32, :], ps[0:32, :], [pr], [tr2])
                S.op("sp", lambda e, t2=t2, hh=hh: e.dma_start(out=nscs.rearrange("b j d -> (b j) d")[:, hh * 512:(hh + 1) * 512], in_=t2[0:32, :]),
                     reads=[tr2], dma=tsem2)
        if gi == 1:
            for hh in range(2):
                ps, pr = psum()
                for cc in range(4):
                    c = hh * 4 + cc
                    S.op("pe", lambda e, c=c, cc=cc, ps=ps: e.transpose(out=ps[0:2, cc * 128:(cc + 1) * 128], in_=M[:, c, lc - 2:lc], identity=IDF[:]),
                         reads=[rMt(lastp), rCONST], writes=[pr])
                t2, tr2, tsem2 = tmp()
                any_copy(t2[0:2, :], ps[0:2, :], [pr], [tr2])
                S.op("sp", lambda e, t2=t2, hh=hh: e.dma_start(out=nscp[:, hh * 512:(hh + 1) * 512], in_=t2[0:2, :]), reads=[tr2], dma=tsem2)

        U = BIG[:, 0:8 * GW].rearrange("p (c w) -> p c w", c=8)

        def epi_b(blk, ji, t, pss, prs):
            c = blk * 4 + ji
            n = t.n
            tb, tr, _ = tmp()
            w0, w1, w2 = cvs(ROW_WCONV, c), cvs(ROW_WCONV + 1, c), cvs(ROW_WCONV + 2, c)
            if t.kind == "P":
                m0 = 2 + t.col0
                rd = [rMt(t), rMh, rCV] + [rMt(x) for x in tiles if x.kind == "P" and x.ti == t.ti - 1]
                x2, x1, x0 = M[:, c, m0:m0 + n], M[:, c, m0 - 1:m0 - 1 + n], M[:, c, m0 - 2:m0 - 2 + n]
                tv = tb[:, 0:n]
                pv = pss[0][:, 0:n]
                ov = U[:, c, t.cols]
            else:
                rd = [rPSH, rCV]
                x2, x1, x0 = PSH[:, c, :, 2:10], PSH[:, c, :, 1:9], PSH[:, c, :, 0:8]
                tv = tb[:, 0:n].rearrange("p (b t) -> p b t", b=16)
                pv = pss[0][:, 0:n].rearrange("p (b t) -> p b t", b=16)
                ov = U[:, c, t.cols].rearrange("p (b t) -> p b t", b=16)
            S.op("dve", lambda e: e.tensor_scalar(out=tv, in0=x2, scalar1=w2, scalar2=None, op0=ALU.mult), reads=rd, writes=[tr])
            S.op("dve", lambda e: e.scalar_tensor_tensor(out=tv, in0=x1, scalar=w1, in1=tv, op0=ALU.mult, op1=ALU.add), reads=rd + [tr], writes=[tr])
            S.op("dve", lambda e: e.scalar_tensor_tensor(out=tv, in0=x0, scalar=w0, in1=tv, op0=ALU.mult, op1=ALU.add), reads=rd + [tr], writes=[tr])
            S.op("dve", lambda e: e.tensor_tensor(out=ov, in0=pv, in1=tv, op=ALU.mult), reads=[prs[0], tr], writes=[rBt(t)])
        proj(tiles, lambda blk: [(c_w_in[:, blk * 512:(blk + 1) * 512], 0, 512)], 8, lambda k, t: H[:, k, t.cols], lambda t: [rHt(t)],
             lambda blk: [[0], [128], [256], [384]], epi_b, 2)

        def epi_c(blk, ji, t, pss, prs):
            c = blk * 4 + ji
            act_copy(M[:, c, t.mcols], pss[0][:, 0:t.n], [prs[0]], [rMt(t)] + ([rMh] if False else []))
        S.op("dve", lambda e: e.memset(DUMMY[:, 1:2], 0.0), writes=[rMt(t) for t in tiles] + [rMh])
        proj(tiles, lambda blk: [(c_w_out[:, blk * 512:(blk + 1) * 512], 0, 512)], 8, lambda k, t: U[:, k, t.cols], lambda t: [rBt(t)],
             lambda blk: [[0], [128], [256], [384]], epi_c, 2)
        norm_boundary(tiles, row_norm(l, 1), nxt)

    P_LR, P_LI, P_DT, P_MAG, P_TH, P_ABR, P_ABI, P_FR, P_FI, P_T0, P_T1, P_T2 = range(12)
    rS5P = rm("S5P")
    rLB, rLC = rm("LB"), rm("LC")
    rCARS = rm("CARS")
    rAS0 = rm("AS0")
    AS0 = BIG[:, 9216:11264].bitcast(F32).rearrange("p (r j b) -> p r j b", r=2, j=32)
    SCL = 0.999999

    def sp_(i):
        return S5P[:, i, :]

    PI_LO = 3.1415925

    def sin_of(out, in0, mul, shift, ang, ki, reads, writes):
        S.op("dve", lambda e: e.tensor_scalar(out=ang, in0=in0, scalar1=mul, scalar2=shift + 8 * TWO_PI, op0=ALU.mult, op1=ALU.add),
             reads=reads, writes=writes)
        S.op("dve", lambda e: e.tensor_scalar(out=ki, in0=ang, scalar1=1.0 / TWO_PI, scalar2=None, op0=ALU.mult), reads=writes, writes=writes)
        S.op("dve", lambda e: e.scalar_tensor_tensor(out=ang, in0=ki, scalar=-TWO_PI, in1=ang, op0=ALU.mult, op1=ALU.add),
             reads=writes, writes=writes)
        S.op("dve", lambda e: e.tensor_scalar(out=ang, in0=ang, scalar1=-PI_LO, scalar2=PI_LO, op0=ALU.max, op1=ALU.min), reads=writes, writes=writes)
        S.op("act", lambda e: e.activation(out=out, in_=ang, func=AF.Sin), reads=writes, writes=writes)

    def s5_setup():
        rS = rm("Mstage")
        sem = S.dsem("s5p")
        for gl in range(2):
            for (src, idx) in ((b_lam_re, P_LR), (b_lam_im, P_LI)):
                S.op("sp", lambda e, gl=gl, src=src, idx=idx: e.dma_start(
                    out=S5P[gl * 64:(gl + 1) * 64, idx, :], in_=src.rearrange("(j g) p -> g p j", g=2)[gl], allow_slow_non_contiguous=True),
                    writes=[rS5P], dma=sem, par=True)
            S.op("sp", lambda e, gl=gl: e.dma_start(
                out=S5P[gl * 64:(gl + 1) * 64, P_DT, :], in_=b_log_dt.rearrange("(j g) -> g j", g=2)[gl:gl + 1, :].partition_broadcast(64),
                allow_slow_non_contiguous=True), writes=[rS5P], dma=sem, par=True)
        S.op("act", lambda e: e.activation(out=sp_(P_DT), in_=sp_(P_DT), func=AF.Exp), reads=[rS5P], writes=[rS5P])
        S.op("dve", lambda e: e.tensor_tensor(out=sp_(P_T0), in0=sp_(P_LR), in1=sp_(P_DT), op=ALU.mult), reads=[rS5P], writes=[rS5P])
        S.op("act", lambda e: e.activation(out=sp_(P_MAG), in_=sp_(P_T0), func=AF.Exp), reads=[rS5P], writes=[rS5P])
        S.op("dve", lambda e: e.tensor_tensor(out=sp_(P_TH), in0=sp_(P_LI), in1=sp_(P_DT), op=ALU.mult), reads=[rS5P], writes=[rS5P])
        sin_of(sp_(P_ABI), sp_(P_TH), 1.0, 0.0, sp_(P_T0), sp_(P_T1).bitcast(I32), [rS5P], [rS5P])
        sin_of(sp_(P_ABR), sp_(P_TH), 1.0, math.pi / 2, sp_(P_T0), sp_(P_T1).bitcast(I32), [rS5P], [rS5P])
        S.op("dve", lambda e: e.tensor_tensor(out=sp_(P_ABI), in0=sp_(P_ABI), in1=sp_(P_MAG), op=ALU.mult), reads=[rS5P], writes=[rS5P])
        S.op("dve", lambda e: e.tensor_tensor(out=sp_(P_ABR), in0=sp_(P_ABR), in1=sp_(P_MAG), op=ALU.mult), reads=[rS5P], writes=[rS5P])
        S.op("dve", lambda e: e.tensor_tensor(out=sp_(P_T0), in0=sp_(P_LR), in1=sp_(P_LR), op=ALU.mult), reads=[rS5P], writes=[rS5P])
        S.op("dve", lambda e: e.tensor_tensor(out=sp_(P_T1), in0=sp_(P_LI), in1=sp_(P_LI), op=ALU.mult), reads=[rS5P], writes=[rS5P])
        S.op("dve", lambda e: e.tensor_tensor(out=sp_(P_T0), in0=sp_(P_T0), in1=sp_(P_T1), op=ALU.add), reads=[rS5P], writes=[rS5P])
        S.op("dve", lambda e: e.reciprocal(out=sp_(P_T0), in_=sp_(P_T0)), reads=[rS5P], writes=[rS5P])
        S.op("dve", lambda e: e.tensor_scalar(out=sp_(P_T1), in0=sp_(P_ABR), scalar1=-1.0, scalar2=None, op0=ALU.add), reads=[rS5P], writes=[rS5P])
        S.op("dve", lambda e: e.tensor_tensor(out=sp_(P_FR), in0=sp_(P_T1), in1=sp_(P_LR), op=ALU.mult), reads=[rS5P], writes=[rS5P])
        S.op("dve", lambda e: e.tensor_tensor(out=sp_(P_T2), in0=sp_(P_ABI), in1=sp_(P_LI), op=ALU.mult), reads=[rS5P], writes=[rS5P])
        S.op("dve", lambda e: e.tensor_tensor(out=sp_(P_FR), in0=sp_(P_FR), in1=sp_(P_T2), op=ALU.add), reads=[rS5P], writes=[rS5P])
        S.op("dve", lambda e: e.tensor_tensor(out=sp_(P_FR), in0=sp_(P_FR), in1=sp_(P_T0), op=ALU.mult), reads=[rS5P], writes=[rS5P])
        S.op("dve", lambda e: e.tensor_tensor(out=sp_(P_FI), in0=sp_(P_ABI), in1=sp_(P_LR), op=ALU.mult), reads=[rS5P], writes=[rS5P])
        S.op("dve", lambda e: e.tensor_tensor(out=sp_(P_T2), in0=sp_(P_T1), in1=sp_(P_LI), op=ALU.mult), reads=[rS5P], writes=[rS5P])
        S.op("dve", lambda e: e.tensor_tensor(out=sp_(P_FI), in0=sp_(P_FI), in1=sp_(P_T2), op=ALU.subtract), reads=[rS5P], writes=[rS5P])
        S.op("dve", lambda e: e.tensor_tensor(out=sp_(P_FI), in0=sp_(P_FI), in1=sp_(P_T0), op=ALU.mult), reads=[rS5P], writes=[rS5P])
        BR0 = MF[:, 0:1024].rearrange("p (j w) -> p j w", j=32)
        BI0 = MF[:, 1024:2048].rearrange("p (j w) -> p j w", j=32)
        BBR = MF[:, 2048:3072].rearrange("p (j w) -> p j w", j=32)
        BBI = MF[:, 3072:4096].rearrange("p (j w) -> p j w", j=32)
        TMPB = MF[:, 4096:5120].rearrange("p (j w) -> p j w", j=32)
        S.op("pool", lambda e: e.memset(MF[:, 0:2048], 0.0), writes=[rS])
        semb = S.dsem("s5b")
        for gl in range(2):
            for (src, dstb) in ((b_B_re, BR0), (b_B_im, BI0)):
                S.op("sp", lambda e, gl=gl, src=src, dstb=dstb: e.dma_start(
                    out=dstb[gl * 64:(gl + 1) * 64, :, gl * 16:(gl + 1) * 16], in_=src.rearrange("(j g) p h -> g p j h", g=2)[gl]),
                    reads=[rS], writes=[rS], dma=semb, par=True)
        fr_b = sp_(P_FR).unsqueeze(2).to_broadcast([128, 32, 32])
        fi_b = sp_(P_FI).unsqueeze(2).to_broadcast([128, 32, 32])
        S.op("dve", lambda e: e.tensor_tensor(out=BBR, in0=BR0, in1=fr_b, op=ALU.mult), reads=[rS, rS5P], writes=[rS])
        S.op("dve", lambda e: e.tensor_tensor(out=TMPB, in0=BI0, in1=fi_b, op=ALU.mult), reads=[rS, rS5P], writes=[rS])
        S.op("dve", lambda e: e.tensor_tensor(out=BBR, in0=BBR, in1=TMPB, op=ALU.subtract), reads=[rS], writes=[rS])
        S.op("dve", lambda e: e.tensor_tensor(out=BBI, in0=BI0, in1=fr_b, op=ALU.mult), reads=[rS, rS5P], writes=[rS])
        S.op("dve", lambda e: e.tensor_tensor(out=TMPB, in0=BR0, in1=fi_b, op=ALU.mult), reads=[rS, rS5P], writes=[rS])
        S.op("dve", lambda e: e.tensor_tensor(out=BBI, in0=BBI, in1=TMPB, op=ALU.add), reads=[rS], writes=[rS])
        for ri, BB in enumerate((BBR, BBI)):
            for q in range(8):
                ps, pr = psum()
                S.op("pe", lambda e, q=q, BB=BB, ps=ps: e.transpose(out=ps[:, 0:128], in_=BB[:, 4 * q:4 * q + 4, :], identity=IDF[:]),
                     reads=[rS, rCONST], writes=[pr])
                any_copy(LB[:, q, ri, :], ps[:, 0:128], [pr], [rLB])
        CINR = MF[:, 5120:6144].rearrange("p (c w) -> p c w", c=8)
        CINI = MF[:, 6144:7168].rearrange("p (c w) -> p c w", c=8)
        S.op("pool", lambda e: e.memset(MF[:, 5120:7168], 0.0), writes=[rS])
        semc = S.dsem("s5c")
        for (src, dstc) in ((b_C_re, CINR), (b_C_im, CINI)):
            sv = src.rearrange("(c jj g) h p -> jj g h c p", c=8, jj=4, g=2)
            for jj in range(4):
                for gp in range(2):
                    S.op("sp", lambda e, jj=jj, gp=gp, sv=sv, dstc=dstc: e.dma_start(
                        out=dstc[32 * jj + 16 * gp:32 * jj + 16 * gp + 16, :, 64 * gp:64 * gp + 64], in_=sv[jj, gp]),
                        reads=[rS], writes=[rS], dma=semc, par=True)
        for ri, CIN in enumerate((CINR, CINI)):
            for cc in range(8):
                ps, pr = psum()
                S.op("pe", lambda e, cc=cc, CIN=CIN, ps=ps: e.transpose(out=ps[:, 0:128], in_=CIN[:, cc, :], identity=IDF[:]),
                     reads=[rS, rCONST], writes=[pr])
                act_copy(LC[:, 4 * cc:4 * cc + 4, ri, :], ps[:, 0:128].rearrange("p (jj w) -> p jj w", jj=4), [pr], [rLC],
                         scale=(1.0 if ri == 0 else -1.0))
        S.op("pool", lambda e: e.memset(MASK[:], 1.0), writes=[rm("MASK")])
        S.op("pool", lambda e: e.memset(MASK[:].rearrange("p (b t) -> p b t", b=16)[:, :, 0:1], 0.0), writes=[rm("MASK")])

    def mixer1(l, gi, tiles, pre_done, nxt, dl=False):
        flush_deferred()
        strict_prev = S.strict
        if not pre_done:
            for t in tiles:
                pre_norm(t, row_norm(l, 0))
        has_s = any(t.kind == "S" for t in tiles)
        rS = rm("Mstage")
        join_M()
        join_BIG()
        if has_s:
            sem = S.dsem("s5s0")
            S0 = MF[:, 4096:5120].rearrange("p (r j b) -> p r j b", r=2, j=32)
            for ri, src in enumerate((st_re, st_im)):
                S.op("sp", lambda e, src=src: e.dma_start(out=MF[0:16, 0:4096], in_=src), reads=[rS], writes=[rS], dma=sem)
                ps, pr = psum()
                for jx in range(32):
                    S.op("pe", lambda e, jx=jx, ps=ps: e.transpose(out=ps[:, jx * 16:(jx + 1) * 16], in_=MF[0:16, jx * 128:(jx + 1) * 128],
                                                                 identity=IDF[0:16, 0:16]), reads=[rS, rCONST], writes=[pr])
                any_copy(S0[:, ri], ps[:, 0:512].rearrange("p (j b) -> p j b", j=32), [pr, rS], [rS])
            abr_b = sp_(P_ABR).unsqueeze(2).to_broadcast([128, 32, 16])
            abi_b = sp_(P_ABI).unsqueeze(2).to_broadcast([128, 32, 16])
            TM = MF[:, 5120:5632].rearrange("p (j b) -> p j b", j=32)
            S.op("dve", lambda e: e.tensor_tensor(out=AS0[:, 0], in0=S0[:, 0], in1=abr_b, op=ALU.mult), reads=[rS, rS5P], writes=[rAS0])
            S.op("dve", lambda e: e.tensor_tensor(out=TM, in0=S0[:, 1], in1=abi_b, op=ALU.mult), reads=[rS, rS5P], writes=[rS])
            S.op("dve", lambda e: e.tensor_tensor(out=AS0[:, 0], in0=AS0[:, 0], in1=TM, op=ALU.subtract), reads=[rS, rAS0], writes=[rAS0])
            S.op("dve", lambda e: e.tensor_tensor(out=AS0[:, 1], in0=S0[:, 1], in1=abr_b, op=ALU.mult), reads=[rS, rS5P], writes=[rAS0])
            S.op("dve", lambda e: e.tensor_tensor(out=TM, in0=S0[:, 0], in1=abi_b, op=ALU.mult), reads=[rS, rS5P, rAS0], writes=[rS])
            S.op("dve", lambda e: e.tensor_tensor(out=AS0[:, 1], in0=AS0[:, 1], in1=TM, op=ALU.add), reads=[rS, rAS0], writes=[rAS0])
            join_M()
        def scr(i):
            return MF[:, i * 512:(i + 1) * 512], s5_res[i]
        (IOT, rIOT), (COS, rCOS), (SIN, rSIN), (A, rA), (B, rB), (VR, rVR), (VI, rVI), (RR, rRR), (RI, rRI), (CS8, rCS8) = [scr(i) for i in range(10)]
        MPb = [MF[:, 5120 + k * 256:5376 + k * 256].bitcast(BF16) for k in range(4)]
        NSN, rNSN = MF[:, 6144:6656], rm("NSN")
        MSKM = MF[:, 6656:6784]
        INI = MF[:, 6784:6792]
        SOUT = GSF[:].rearrange("p c w -> p (c w)")[:, 0:1024].rearrange("p (r j b) -> p r j b", r=2, j=32)
        rSRB, rSIB, rMSK, rINI, rSOUT = rm("SRB"), rm("SIB"), rm("MSKM"), rm("INI"), rm("SOUT")
        S.op("pool", lambda e: e.iota(IOT, [[1, 512]], base=0, channel_multiplier=0, allow_small_or_imprecise_dtypes=True), writes=[rIOT])
        YT = BIG[:, 0:8 * GW].rearrange("p (c w) -> p c w", c=8)
        hold = {}
        bankd = {}

        def emit_B(j, ti):
            t = tiles[ti]
            q, jj = j // 4, j % 4
            pxr, prr = psum()
            pxi, pri = psum()
            for (px, prx, ri) in ((pxr, prr, 0), (pxi, pri, 1)):
                S.op("pe", lambda e, px=px, ri=ri: e.matmul(
                    px[:, 0:t.n], lhsT=LB[32 * jj:32 * jj + 32, q, ri, :], rhs=H[32 * jj:32 * jj + 32, q, t.cols], start=True, stop=True,
                    tile_position=(32 * jj, 0)), reads=[rLB, rHt(t)], writes=[prx])
            bankd[(j, ti)] = (pxr, prr, pxi, pri)

        TBL = [(COS, SIN, NSN, rCOS, rSIN, rNSN),
               (MF[:, 6912:7424], MF[:, 7424:7936], MF[:, 7936:8448], rm("COS1"), rm("SIN1"), rm("NSN1"))]

        def gen_tables(jn):
            COSn, SINn, NSNn, rCn, rSn, rNn = TBL[jn % 2]
            thj = S5P[:, P_TH, jn:jn + 1]
            kI = VR.bitcast(I32)
            S.op("dve", lambda e: e.tensor_scalar(out=B, in0=IOT, scalar1=thj, scalar2=8 * TWO_PI, op0=ALU.mult, op1=ALU.add),
                 reads=[rIOT, rS5P], writes=[rB])
            S.op("dve", lambda e: e.tensor_scalar(out=kI, in0=B, scalar1=1.0 / TWO_PI, scalar2=None, op0=ALU.mult), reads=[rB], writes=[rVR])
            S.op("dve", lambda e: e.scalar_tensor_tensor(out=B, in0=kI, scalar=-TWO_PI, in1=B, op0=ALU.mult, op1=ALU.add),
                 reads=[rVR, rB], writes=[rB])
            S.op("dve", lambda e: e.tensor_scalar(out=B, in0=B, scalar1=-PI_LO, scalar2=PI_LO, op0=ALU.max, op1=ALU.min), reads=[rB], writes=[rB])
            S.op("act", lambda e: e.activation(out=SINn, in_=B, func=AF.Sin), reads=[rB], writes=[rSn])
            S.op("act", lambda e: e.activation(out=A, in_=B, func=AF.Abs), reads=[rB], writes=[rA])
            S.op("act", lambda e: e.activation(out=COSn, in_=A, func=AF.Sin, scale=-1.0, bias=HPI[:, 0:1]), reads=[rA, rCONST2], writes=[rCn])
            S.op("act", lambda e: e.mul(out=NSNn, in_=SINn, mul=-1.0), reads=[rSn], writes=[rNn])

        pend1, pend2 = [], []
        rT2, rT3, rT4, rT5, rIr, rIi = rm("INI2"), rm("INI3"), rm("INI4"), rm("INI5"), rm("INIr"), rm("INIi")
        gen_tables(0)
        for j in range(32):
            q, jj = j // 4, j % 4
            COS, SIN, NSN, rCOS, rSIN, rNSN = TBL[j % 2]
            if has_s:
                S.op("dve", lambda e: e.tensor_copy(out=CS8[:, 0:128].rearrange("p (b t) -> p b t", b=16),
                                                    in_=COS[:, 0:8].unsqueeze(1).to_broadcast([128, 16, 8])), reads=[rCOS], writes=[rCS8])
                S.op("dve", lambda e: e.tensor_copy(out=CS8[:, 128:256].rearrange("p (b t) -> p b t", b=16),
                                                    in_=SIN[:, 0:8].unsqueeze(1).to_broadcast([128, 16, 8])), reads=[rSIN], writes=[rCS8])
                S.op("dve", lambda e: e.tensor_copy(out=CS8[:, 256:384].rearrange("p (b t) -> p b t", b=16),
                                                    in_=NSN[:, 0:8].unsqueeze(1).to_broadcast([128, 16, 8])), reads=[rNSN], writes=[rCS8])
                S.op("dve", lambda e, j=j: e.tensor_scalar(out=MSKM, in0=MASK[:], scalar1=S5P[:, P_MAG, j:j + 1], scalar2=None, op0=ALU.mult),
                     reads=[rm("MASK"), rS5P], writes=[rMSK])
            for ti, t in enumerate(tiles):
                n = t.n
                if jj == 0:
                    hold[ti] = psum_hold()
                if t.kind == "P":
                    cs, sn, nsn, rcs, rsn, rns = COS[:, 0:n], SIN[:, 0:n], NSN[:, 0:n], rCOS, rSIN, rNSN
                else:
                    cs, sn, nsn, rcs, rsn, rns = CS8[:, 0:128], CS8[:, 128:256], CS8[:, 256:384], rCS8, rCS8, rCS8
                if not bankd:
                    emit_B(j, ti)
                pxr, prr, pxi, pri = bankd.pop((j, ti))
                S.op("dve", lambda e, pxr=pxr, cs=cs, n=n: e.tensor_tensor(out=VR[:, 0:n], in0=pxr[:, 0:n], in1=cs, op=ALU.mult), reads=[prr, rcs], writes=[rVR])
                S.op("dve", lambda e, pxi=pxi, sn=sn, n=n: e.tensor_tensor(out=A[:, 0:n], in0=pxi[:, 0:n], in1=sn, op=ALU.mult), reads=[pri, rsn], writes=[rA])
                S.op("dve", lambda e, pxi=pxi, cs=cs, n=n: e.tensor_tensor(out=VI[:, 0:n], in0=pxi[:, 0:n], in1=cs, op=ALU.mult), reads=[pri, rcs], writes=[rVI])
                S.op("dve", lambda e, pxr=pxr, sn=sn, n=n: e.tensor_tensor(out=B[:, 0:n], in0=pxr[:, 0:n], in1=sn, op=ALU.mult), reads=[prr, rsn], writes=[rB])
                S.op("dve", lambda e, n=n: e.tensor_tensor(out=VR[:, 0:n], in0=VR[:, 0:n], in1=A[:, 0:n], op=ALU.add), reads=[rVR, rA], writes=[rVR])
                S.op("dve", lambda e, n=n: e.tensor_tensor(out=VI[:, 0:n], in0=VI[:, 0:n], in1=B[:, 0:n], op=ALU.subtract), reads=[rVI, rB], writes=[rVI])
                nxt_it = (j, ti + 1) if ti + 1 < len(tiles) else ((j + 1, 0) if j + 1 < 32 else None)
                if nxt_it is not None:
                    emit_B(*nxt_it)
                while pend1:
                    pend1.pop(0)()
                mag = S5P[:, P_MAG, j:j + 1]
                if t.kind == "P":
                    first = (t.tok0 == 0)
                    if not first:
                        c1, s1 = COS[:, 1:2], SIN[:, 1:2]
                        sr, si = CARRY_S[:, j, 0:1], CARRY_S[:, j, 1:2]
                        S.op("dve", lambda e, si=si, s1=s1: e.tensor_scalar(out=INI[:, 2:3], in0=si, scalar1=s1, scalar2=None, op0=ALU.mult),
                             reads=[rCARS, rSIN], writes=[rT2])
                        S.op("dve", lambda e, si=si, c1=c1: e.tensor_scalar(out=INI[:, 3:4], in0=si, scalar1=c1, scalar2=None, op0=ALU.mult),
                             reads=[rCARS, rCOS], writes=[rT3])
                        S.op("dve", lambda e, sr=sr, c1=c1: e.scalar_tensor_tensor(out=INI[:, 0:1], in0=sr, scalar=c1, in1=INI[:, 2:3], op0=ALU.mult,
                                                                                  op1=ALU.subtract), reads=[rCARS, rCOS, rT2], writes=[rIr])
                        S.op("dve", lambda e, sr=sr, s1=s1: e.scalar_tensor_tensor(out=INI[:, 1:2], in0=sr, scalar=s1, in1=INI[:, 3:4], op0=ALU.mult,
                                                                                  op1=ALU.add), reads=[rCARS, rSIN, rT3], writes=[rIi])
                        ir, ii = INI[:, 0:1], INI[:, 1:2]
                    else:
                        ir, ii = 0.0, 0.0
                    d0 = mag.to_broadcast([128, n])
                    S.op("dve", lambda e, d0=d0, ir=ir, n=n: e.tensor_tensor_scan(out=RR[:, 0:n], data0=d0, data1=VR[:, 0:n], initial=ir, op0=ALU.mult, op1=ALU.add),
                         reads=[rVR, rS5P, rIr], writes=[rRR])
                    S.op("dve", lambda e, d0=d0, ii=ii, n=n: e.tensor_tensor_scan(out=RI[:, 0:n], data0=d0, data1=VI[:, 0:n], initial=ii, op0=ALU.mult, op1=ALU.add),
                         reads=[rVI, rS5P, rIi], writes=[rRI])
                else:
                    vr3 = VR[:, 0:128].rearrange("p (b t) -> p b t", b=16)
                    vi3 = VI[:, 0:128].rearrange("p (b t) -> p b t", b=16)
                    S.op("dve", lambda e, j=j, vr3=vr3: e.tensor_tensor(out=vr3[:, :, 0], in0=vr3[:, :, 0], in1=AS0[:, 0, j, :], op=ALU.add),
                         reads=[rVR, rAS0], writes=[rVR])
                    S.op("dve", lambda e, j=j, vi3=vi3: e.tensor_tensor(out=vi3[:, :, 0], in0=vi3[:, :, 0], in1=AS0[:, 1, j, :], op=ALU.add),
                         reads=[rVI, rAS0], writes=[rVI])
                    S.op("dve", lambda e: e.tensor_tensor_scan(out=RR[:, 0:128], data0=MSKM, data1=VR[:, 0:128], initial=0.0, op0=ALU.mult, op1=ALU.add),
                         reads=[rVR, rMSK], writes=[rRR])
                    S.op("dve", lambda e: e.tensor_tensor_scan(out=RI[:, 0:128], data0=MSKM, data1=VI[:, 0:128], initial=0.0, op0=ALU.mult, op1=ALU.add),
                         reads=[rVI, rMSK], writes=[rRI])
                if ti == len(tiles) - 1 and j + 1 < 32:
                    gen_tables(j + 1)
                S.op("dve", lambda e, cs=cs, n=n: e.tensor_tensor(out=MPb[0][:, 0:n], in0=RR[:, 0:n], in1=cs, op=ALU.mult), reads=[rRR, rcs], writes=[rSRB])
                S.op("dve", lambda e, sn=sn, n=n: e.tensor_tensor(out=MPb[2][:, 0:n], in0=RR[:, 0:n], in1=sn, op=ALU.mult), reads=[rRR, rsn], writes=[rSIB])
                S.op("dve", lambda e, nsn=nsn, n=n: e.tensor_tensor(out=MPb[1][:, 0:n], in0=RI[:, 0:n], in1=nsn, op=ALU.mult), reads=[rRI, rns], writes=[rSRB])
                S.op("dve", lambda e, cs=cs, n=n: e.tensor_tensor(out=MPb[3][:, 0:n], in0=RI[:, 0:n], in1=cs, op=ALU.mult), reads=[rRI, rcs], writes=[rSIB])
                while pend2:
                    pend2.pop(0)()
                if t.kind == "P":
                    cl, sl = COS[:, n - 1:n], SIN[:, n - 1:n]
                    rl, il = RR[:, n - 1:n], RI[:, n - 1:n]
                    S.op("dve", lambda e, il=il, sl=sl: e.tensor_scalar(out=INI[:, 4:5], in0=il, scalar1=sl, scalar2=None, op0=ALU.mult),
                         reads=[rRI, rSIN], writes=[rT4])
                    S.op("dve", lambda e, il=il, cl=cl: e.tensor_scalar(out=INI[:, 5:6], in0=il, scalar1=cl, scalar2=None, op0=ALU.mult),
                         reads=[rRI, rCOS], writes=[rT5])
                    S.op("dve", lambda e, j=j, rl=rl, cl=cl: e.scalar_tensor_tensor(out=CARRY_S[:, j, 0:1], in0=rl, scalar=cl, in1=INI[:, 4:5],
                                                                                  op0=ALU.mult, op1=ALU.subtract), reads=[rRR, rCOS, rT4], writes=[rCARS])
                    S.op("dve", lambda e, j=j, rl=rl, sl=sl: e.scalar_tensor_tensor(out=CARRY_S[:, j, 1:2], in0=rl, scalar=sl, in1=INI[:, 5:6],
                                                                                  op0=ALU.mult, op1=ALU.add), reads=[rRR, rSIN, rT5], writes=[rCARS])
                else:
                    c7, s7 = COS[:, 7:8], SIN[:, 7:8]
                    rr7 = RR[:, 0:128].rearrange("p (b t) -> p b t", b=16)[:, :, 7]
                    ri7 = RI[:, 0:128].rearrange("p (b t) -> p b t", b=16)[:, :, 7]
                    t16 = MF[:, 6792:6808]
                    S.op("dve", lambda e, ri7=ri7, s7=s7: e.tensor_scalar(out=t16, in0=ri7, scalar1=s7, scalar2=None, op0=ALU.mult),
                         reads=[rRI, rSIN], writes=[rINI])
                    S.op("dve", lambda e, j=j, rr7=rr7, c7=c7: e.scalar_tensor_tensor(out=SOUT[:, 0, j, :], in0=rr7, scalar=c7, in1=t16, op0=ALU.mult,
                                                                                    op1=ALU.subtract), reads=[rRR, rCOS, rINI], writes=[rSOUT])
                    S.op("dve", lambda e, ri7=ri7, c7=c7: e.tensor_scalar(out=t16, in0=ri7, scalar1=c7, scalar2=None, op0=ALU.mult),
                         reads=[rRI, rCOS, rINI, rSOUT], writes=[rINI])
                    S.op("dve", lambda e, j=j, rr7=rr7, s7=s7: e.scalar_tensor_tensor(out=SOUT[:, 1, j, :], in0=rr7, scalar=s7, in1=t16, op0=ALU.mult,
                                                                                    op1=ALU.add), reads=[rRR, rSIN, rINI], writes=[rSOUT])
                hb = hold[ti]
                for k4, (ri_, mk, rk) in enumerate(((0, MPb[0], rSRB), (1, MPb[2], rSIB), (0, MPb[1], rSRB), (1, MPb[3], rSIB))):
                    S.op("pe", lambda e, hb=hb, j=j, jj=jj, n=n, ri_=ri_, mk=mk, k4=k4: e.matmul(
                        PS[hb][32 * jj:32 * jj + 32, 0:n], lhsT=LC[:, j, ri_, :], rhs=mk[:, 0:n], start=(k4 == 0), stop=(k4 == 3),
                        tile_position=(0, 32 * jj)), reads=[rLC, rk], writes=[PSR[hb]])
                if jj == 3:
                    def g1(hb=hb, q=q, t=t):
                        tb, tr, _ = tmp()
                        t2, tr2, _ = tmp()
                        S.op("dve", lambda e: e.scalar_tensor_tensor(out=tb[:, 0:t.n], in0=H[:, q, t.cols], scalar=cvs(ROW_BD, q),
                                                                    in1=PS[hb][:, 0:t.n], op0=ALU.mult, op1=ALU.add),
                             reads=[rHt(t), rCV, PSR[hb]], writes=[tr])
                        S.op("dve", lambda e: e.tensor_tensor(out=t2[:, 0:t.n], in0=tb[:, 0:t.n], in1=tb[:, 0:t.n], op=ALU.mult), reads=[tr], writes=[tr2])
                        S.op("dve", lambda e: e.tensor_scalar(out=t2[:, 0:t.n], in0=t2[:, 0:t.n], scalar1=0.044715, scalar2=1.0, op0=ALU.mult, op1=ALU.add),
                             reads=[tr2], writes=[tr2])
                        S.op("dve", lambda e: e.tensor_tensor(out=t2[:, 0:t.n], in0=t2[:, 0:t.n], in1=tb[:, 0:t.n], op=ALU.mult), reads=[tr, tr2], writes=[tr2])
                        S.op("act", lambda e: e.activation(out=t2[:, 0:t.n], in_=t2[:, 0:t.n], func=AF.Sigmoid, scale=2.0 * math.sqrt(2.0 / math.pi)),
                             reads=[tr2], writes=[tr2])
                        psum_release(hb)

                        def g2():
                            S.op("dve", lambda e: e.tensor_tensor(out=YT[:, q, t.cols], in0=tb[:, 0:t.n], in1=t2[:, 0:t.n], op=ALU.mult),
                                 reads=[tr, tr2], writes=[rBt(t)])
                        pend2.append(g2)
                    pend1.append(g1)
        while pend1:
            pend1.pop(0)()
        while pend2:
            pend2.pop(0)()
        if has_s:
            for ri, dst in enumerate((nres, nims)):
                for g4 in range(8):
                    ps, pr = psum()
                    for jx in range(4):
                        j = g4 * 4 + jx
                        S.op("pe", lambda e, j=j, jx=jx, ri=ri, ps=ps: e.transpose(out=ps[0:16, jx * 128:(jx + 1) * 128], in_=SOUT[:, ri, j, :], identity=IDF[:]),
                             reads=[rSOUT, rCONST], writes=[pr])
                    tb, tr, tsem = tmp()
                    any_copy(tb[0:16, :], ps[0:16, :], [pr], [tr])
                    S.op("sp", lambda e, dst=dst, g4=g4, tb=tb: e.dma_start(out=dst[:, g4 * 512:(g4 + 1) * 512], in_=tb[0:16, :]), reads=[tr], dma=tsem)
        if gi == 1:
            for ri, dst in enumerate((nrep, nimp)):
                ps, pr = psum()
                S.op("pe", lambda e, ri=ri, ps=ps: e.transpose(out=ps[0:32, 0:128], in_=CARRY_S[:, :, ri], identity=IDF[:]), reads=[rCARS, rCONST], writes=[pr])
                tb, tr, tsem = tmp()
                any_copy(tb[0:32, 0:128], ps[0:32, 0:128], [pr], [tr])
                S.op("sp", lambda e, dst=dst, tb=tb: e.dma_start(out=dst.rearrange("o (j p) -> (o j) p", p=128), in_=tb[0:32, 0:128]), reads=[tr], dma=tsem)
        join_M()

        def wparts(blk):
            return [(b_w_glu[:, blk * 256:(blk + 1) * 256], 0, 256), (b_w_glu[:, D + blk * 256:D + (blk + 1) * 256], 256, 256)]

        def epi(blk, ji, t, pss, prs):
            c = blk * 2 + ji
            n = t.n
            tb, tr, _ = tmp()
            S.op("act", lambda e: e.activation(out=tb[:, 0:n], in_=pss[1][:, 0:n], func=AF.Sigmoid, bias=cvs(ROW_BGLU + 1, c)),
                 reads=[prs[1], rCV], writes=[tr])
            S.op("dve", lambda e: e.scalar_tensor_tensor(out=M[:, c, t.mcols], in0=pss[0][:, 0:n], scalar=cvs(ROW_BGLU, c), in1=tb[:, 0:n],
                                                       op0=ALU.add, op1=ALU.mult), reads=[prs[0], tr, rCV], writes=[rMt(t)])
        S.strict = strict_prev
        proj(tiles, wparts, 8, lambda k, t: YT[:, k, t.cols], lambda t: [rBt(t)], lambda blk: [[0, 256], [128, 384]], epi, 4)
        norm_boundary(tiles, row_norm(l, 1), nxt)

    MARKS = cfg.setdefault("_marks", [])
    setup_consts()
    setup_eps()
    subs = cfg.get("subs", ("mix", "attn", "ffn"))
    if any(l % 3 == 1 for l in layers) and "mix" in subs:
        s5_setup()
        join_M()
    if "attn" in subs:
        setup_memT()
        join_M()
    for gi in groups_sel:
        tiles = GROUPS[gi]
        if cfg.get("no_sample"):
            tiles = [t for t in tiles if t.kind == "P"]
        load_x(tiles)
        join_M()
        seq = []
        for l in layers:
            if "mix" in subs:
                seq.append(([mixer0, mixer1, mixer2][l % 3], l, row_norm(l, 0), "mix"))
            if "attn" in subs:
                seq.append((attention, l, row_norm(l, 2), "attn"))
            if "ffn" in subs:
                seq.append((ffn, l, row_norm(l, 4), "ffn"))
        for si, (fn, l, grow, kind) in enumerate(seq):
            nxt = seq[si + 1][2] if si + 1 < len(seq) else None
            dl = (si + 1 < len(seq)) and seq[si + 1][3] in ("attn", "ffn") and not cfg.get("no_defer")
            MARKS.append((f"g{gi} L{l} {kind}", len(S.ops["pe"]), len(S.ops["dve"]), len(S.ops["act"]), len(S.ops["pool"])))
            fn(l, gi, tiles, si > 0, nxt, dl)
        flush_deferred()
        MARKS.append((f"g{gi} store", len(S.ops["pe"]), len(S.ops["dve"]), len(S.ops["act"]), len(S.ops["pool"])))
        join_M()
        store_y(tiles)
        join_M()

    blk_ctx = es.enter_context(nc.Block())
    S.emit(blk_ctx)
    es.close()
    return nc


def make_vecs(inp):
    rows = []
    rows.append(inp["norm_g"].reshape(24, D))
    rows.append(inp["a_b_pw1"].reshape(4, D))
    rows.append(inp["a_w_dw"].reshape(62, D))
    rows.append(inp["a_b_dw"].reshape(2, D))
    rows.append(inp["a_ln_g"].reshape(2, D))
    rows.append(inp["a_ln_b"].reshape(2, D))
    rows.append(inp["a_b_pw2"].reshape(2, D))
    rows.append(inp["b_D"].reshape(1, D))
    rows.append(inp["b_b_glu"].reshape(2, D))
    rows.append(inp["c_w_conv"].reshape(3, D))
    v = np.concatenate(rows, axis=0).astype(np.float32)
    out = np.zeros((128, D), np.float32)
    out[:v.shape[0]] = v
    return out


def make_in_maps(inp, cores):
    vecs = make_vecs(inp)
    shared = dict(
        vecs=vecs,
        a_w_pw1=inp["a_w_pw1"], a_w_pw2=inp["a_w_pw2"],
        b_lam_re=inp["b_lam_re"][0], b_lam_im=inp["b_lam_im"][0], b_log_dt=inp["b_log_dt"][0],
        b_B_re=inp["b_B_re"][0], b_B_im=inp["b_B_im"][0], b_C_re=inp["b_C_re"][0], b_C_im=inp["b_C_im"][0],
        b_w_glu=inp["b_w_glu"][0], c_w_in=inp["c_w_in"][0], c_w_out=inp["c_w_out"][0],
        x_w_q=inp["x_w_q"], x_w_k=inp["x_w_k"], x_w_v=inp["x_w_v"], x_w_o=inp["x_w_o"],
        f_w_gu=inp["f_w_gu"], f_w_down=inp["f_w_down"],
    )
    shared = {k: np.ascontiguousarray(v, dtype=np.float32) for k, v in shared.items()}
    maps = []
    for c in cores:
        b0, b1 = 16 * c, 16 * c + 16
        m = dict(shared)
        m["xp"] = np.ascontiguousarray(inp["x_prompt"][c])
        m["xs"] = np.ascontiguousarray(inp["x_sample"][b0:b1].reshape(128, D))
        m["st_conv"] = np.ascontiguousarray(inp["state_conv_a"][:, b0:b1])
        m["st_re"] = np.ascontiguousarray(inp["state_ssm_re"][0, b0:b1].reshape(16, 4096))
        m["st_im"] = np.ascontiguousarray(inp["state_ssm_im"][0, b0:b1].reshape(16, 4096))
        m["st_sc"] = np.ascontiguousarray(inp["state_sconv"][0, b0:b1])
        m["ck"] = np.ascontiguousarray(inp["cache_mem_k"][:, b0:b1].reshape(4, 16, NMEM, D))
        m["cv"] = np.ascontiguousarray(inp["cache_mem_v"][:, b0:b1].reshape(4, 16, NMEM, D))
        m["memp"] = np.ascontiguousarray(inp["mem_prompt"][c])
        maps.append(m)
    return maps


def kernel(**inp):
    inp = {k: np.asarray(v) for k, v in inp.items()}
    nc = build_program({})
    maps = make_in_maps(inp, list(range(8)))
    res = run_bass_kernel_spmd(nc, maps, core_ids=list(range(8)))
    r = res.results
    f32 = np.float32
    y_prompt = np.stack([r[c]["yp"] for c in range(8)]).astype(f32)
    y_sample = np.concatenate([r[c]["ys"].reshape(16, 8, D) for c in range(8)]).astype(f32)
    nk = np.stack([r[c]["nk"] for c in range(8)], axis=1).reshape(4, 8, NMEM, 4, 256).astype(f32)
    nv = np.stack([r[c]["nv"] for c in range(8)], axis=1).reshape(4, 8, NMEM, 4, 256).astype(f32)
    ncap = np.stack([r[c]["ncap"] for c in range(8)], axis=1).astype(f32)
    ncas = np.concatenate([r[c]["ncas"] for c in range(8)], axis=1).astype(f32)
    nrep = np.stack([r[c]["nrep"].reshape(1, 64, 64) for c in range(8)], axis=1).astype(f32)
    nimp = np.stack([r[c]["nimp"].reshape(1, 64, 64) for c in range(8)], axis=1).astype(f32)
    nres = np.concatenate([r[c]["nres"].reshape(1, 16, 64, 64) for c in range(8)], axis=1).astype(f32)
    nims = np.concatenate([r[c]["nims"].reshape(1, 16, 64, 64) for c in range(8)], axis=1).astype(f32)
    nscp = np.stack([r[c]["nscp"].reshape(1, 2, D) for c in range(8)], axis=1).astype(f32)
    nscs = np.concatenate([r[c]["nscs"].reshape(1, 16, 2, D) for c in range(8)], axis=1).astype(f32)
    return (y_prompt, y_sample, nk, nv, ncap, ncas, nrep, nimp, nres, nims, nscp, nscs)
```
